# Optimizing a Trainium2 kernel written in Bass

```python
import math
import jax, jax.numpy as jnp
from jax import lax
import numpy as np

D_MODEL = 1024
BATCH = 32
SEQ = 256
DEPTH = 4
DEC_BATCH = 2
DEC_SEQ = 4096
PAST_LEN = 256

GRID_W = 64
MIX_W = D_MODEL
S5_W = MIX_W // 2
S5_P = 16
S5_G = S5_W // S5_P
S5_N = 64
GDN_W = MIX_W - S5_W
GDN_DK = 128
GDN_DV = 128
GDN_H = GDN_W // GDN_DK
CONV_K = 3
CHUNK = 64
N_DIR = 2
D_FF = 4 * D_MODEL
N_MOD = 6
IN_W = 2 * S5_W + 4 * GDN_W + 2 * N_DIR * GDN_H
SPLITS = (S5_W, 2 * S5_W, 2 * S5_W + 3 * GDN_W, 2 * S5_W + 4 * GDN_W,
          2 * S5_W + 4 * GDN_W + N_DIR * GDN_H)
POS_BASE = 10000.0
EPS = 1e-6
F32 = jnp.float32

kernel_name = 'hymba_s5_gdn_prefix_dit_step'


def rmsnorm(x, g):
    xf = x.astype(F32)
    y = xf * lax.rsqrt(jnp.mean(xf * xf, axis=-1, keepdims=True) + EPS)
    return (y * g.astype(F32)).astype(x.dtype)


def l2norm(x):
    return x * lax.rsqrt(jnp.sum(x * x, axis=-1, keepdims=True) + EPS)


def rev(t):
    return jnp.flip(t, axis=1)


def grid_pos_embed(n_tok):
    rows = n_tok // GRID_W
    r, col = jnp.meshgrid(jnp.arange(rows, dtype=F32), jnp.arange(GRID_W, dtype=F32), indexing='ij')
    r = r.reshape(-1, 1)
    col = col.reshape(-1, 1)
    quarter = D_MODEL // 4
    omega = 1.0 / (POS_BASE ** (jnp.arange(quarter, dtype=F32) / quarter))
    ang_r = r * omega
    ang_c = col * omega
    return jnp.concatenate([jnp.sin(ang_r), jnp.cos(ang_r), jnp.sin(ang_c), jnp.cos(ang_c)], axis=-1)


def modulation(cond, w_ada, b_ada):
    m = jnp.dot(jax.nn.silu(cond), w_ada) + b_ada
    return jnp.split(m[..., None, :], N_MOD, axis=-1)


def short_conv(x, w):
    pad = (CONV_K - 1) // 2
    return lax.conv_general_dilated(x, w[:, None, :], window_strides=(1,),
                                    padding=[(pad, CONV_K - 1 - pad)],
                                    dimension_numbers=('NWC', 'WIO', 'NWC'),
                                    feature_group_count=x.shape[-1])


def s5_discretise(lam_re, lam_im, log_dt, b_re, b_im):
    lam_re = jnp.minimum(lam_re, -1e-4)
    dt = jnp.exp(log_dt)[:, None]
    mag = jnp.exp(lam_re * dt)
    abar_re = mag * jnp.cos(lam_im * dt)
    abar_im = mag * jnp.sin(lam_im * dt)
    num_re = abar_re - 1.0
    num_im = abar_im
    den = lam_re * lam_re + lam_im * lam_im
    f_re = (num_re * lam_re + num_im * lam_im) / den
    f_im = (num_im * lam_re - num_re * lam_im) / den
    bbar_re = f_re[..., None] * b_re - f_im[..., None] * b_im
    bbar_im = f_re[..., None] * b_im + f_im[..., None] * b_re
    return abar_re, abar_im, bbar_re, bbar_im


def complex_linear_scan(abar_re, abar_im, bu_re, bu_im, h0_re, h0_im):
    bu_re = bu_re.at[:, 0].add(abar_re * h0_re - abar_im * h0_im)
    bu_im = bu_im.at[:, 0].add(abar_re * h0_im + abar_im * h0_re)
    a_re = jnp.broadcast_to(abar_re, bu_re.shape)
    a_im = jnp.broadcast_to(abar_im, bu_im.shape)

    def combine(e1, e2):
        a1r, a1i, b1r, b1i = e1
        a2r, a2i, b2r, b2i = e2
        return (a2r * a1r - a2i * a1i, a2r * a1i + a2i * a1r,
                a2r * b1r - a2i * b1i + b2r, a2r * b1i + a2i * b1r + b2i)

    _, _, h_re, h_im = lax.associative_scan(combine, (a_re, a_im, bu_re, bu_im), axis=1)
    return h_re, h_im


def s5_mixer(u, z, lam_re, lam_im, log_dt, b_re, b_im, c_re, c_im, d_skip, h0_re, h0_im):
    bsz, length, _ = u.shape
    u = u.astype(F32).reshape(bsz, length, S5_G, S5_P)
    y = u * d_skip.astype(F32).reshape(S5_G, S5_P)
    fin_re, fin_im = [], []
    for d in range(N_DIR):
        abr, abi, bbr, bbi = s5_discretise(lam_re[d].astype(F32), lam_im[d].astype(F32),
                                           log_dt[d].astype(F32), b_re[d].astype(F32), b_im[d].astype(F32))
        ud = u if d == 0 else rev(u)
        bu_re = jnp.einsum('gnp,blgp->blgn', bbr, ud)
        bu_im = jnp.einsum('gnp,blgp->blgn', bbi, ud)
        hr, hi = complex_linear_scan(abr, abi, bu_re, bu_im,
                                     h0_re[:, d].astype(F32), h0_im[:, d].astype(F32))
        yd = (jnp.einsum('gpn,blgn->blgp', c_re[d].astype(F32), hr)
              - jnp.einsum('gpn,blgn->blgp', c_im[d].astype(F32), hi))
        y = y + (yd if d == 0 else rev(yd))
        fin_re.append(hr[:, -1])
        fin_im.append(hi[:, -1])
    out = jax.nn.gelu(y.reshape(bsz, length, S5_W)) * jax.nn.sigmoid(z.astype(F32))
    return out, jnp.stack(fin_re, axis=1), jnp.stack(fin_im, axis=1)


def gated_delta_chunked(q, k, v, g, beta, s0):
    bsz, length, nh, _ = q.shape
    nc = length // CHUNK

    def to_chunks(t):
        return t.reshape(bsz, nc, CHUNK, nh, -1).transpose(1, 0, 3, 2, 4)

    qc, kc, vc = to_chunks(q), to_chunks(k), to_chunks(v)
    gc = g.reshape(bsz, nc, CHUNK, nh).transpose(1, 0, 3, 2)
    bc = beta.reshape(bsz, nc, CHUNK, nh).transpose(1, 0, 3, 2)
    gcum = jnp.cumsum(gc, axis=-1)
    tril_incl = jnp.tril(jnp.ones((CHUNK, CHUNK), dtype=bool))
    tril_strict = jnp.tril(jnp.ones((CHUNK, CHUNK), dtype=bool), k=-1)
    diff = gcum[..., :, None] - gcum[..., None, :]
    decay = jnp.where(tril_incl, jnp.exp(jnp.where(tril_incl, diff, 0.0)), 0.0)
    kb = kc * bc[..., None]
    vb = vc * bc[..., None]
    m = jnp.where(tril_strict, jnp.einsum('nbhid,nbhjd->nbhij', kb, kc) * decay, 0.0)
    eye = jnp.eye(CHUNK, dtype=F32)
    t_inv = lax.linalg.triangular_solve(eye + m, jnp.broadcast_to(eye, m.shape),
                                        left_side=True, lower=True, unit_diagonal=True)
    u_c = jnp.matmul(t_inv, vb)
    w_c = jnp.matmul(t_inv, kb * jnp.exp(gcum)[..., None])
    a_intra = jnp.where(tril_incl, jnp.einsum('nbhid,nbhjd->nbhij', qc, kc) * decay, 0.0)

    def step(s, xs):
        q_i, k_i, u_i, w_i, a_i, g_i = xs
        v_new = u_i - jnp.matmul(w_i, s)
        o = jnp.matmul(q_i * jnp.exp(g_i)[..., None], s) + jnp.matmul(a_i, v_new)
        g_last = g_i[..., -1]
        s = (s * jnp.exp(g_last)[..., None, None]
             + jnp.einsum('bhcd,bhce->bhde', k_i * jnp.exp(g_last[..., None] - g_i)[..., None], v_new))
        return s, o

    s_fin, o = lax.scan(step, s0, (qc, kc, u_c, w_c, a_intra, gcum))
    o = o.transpose(1, 0, 3, 2, 4).reshape(bsz, length, nh, -1)
    return o, s_fin


def gdn_mixer(qkv, z, beta_raw, a_raw, conv_w, a_log, dt_bias, norm_w, s0):
    bsz, length, _ = qkv.shape
    qkv = jax.nn.silu(short_conv(qkv, conv_w)).astype(F32)
    q, k, v = jnp.split(qkv, 3, axis=-1)
    q = l2norm(q.reshape(bsz, length, GDN_H, GDN_DK)) * (GDN_DK ** -0.5)
    k = l2norm(k.reshape(bsz, length, GDN_H, GDN_DK))
    v = v.reshape(bsz, length, GDN_H, GDN_DV)
    beta = jax.nn.sigmoid(beta_raw.astype(F32).reshape(bsz, length, N_DIR, GDN_H))
    g = -jnp.exp(a_log.astype(F32)) * jax.nn.softplus(
        a_raw.astype(F32).reshape(bsz, length, N_DIR, GDN_H) + dt_bias.astype(F32))
    outs, fins = [], []
    for d in range(N_DIR):
        if d == 0:
            od, sd = gated_delta_chunked(q, k, v, g[:, :, d], beta[:, :, d], s0[:, d].astype(F32))
        else:
            od, sd = gated_delta_chunked(rev(q), rev(k), rev(v), rev(g[:, :, d]), rev(beta[:, :, d]),
                                         s0[:, d].astype(F32))
            od = rev(od)
        outs.append(od)
        fins.append(sd)
    o = outs[0] + outs[1]
    o = rmsnorm(o, norm_w) * jax.nn.silu(z.astype(F32).reshape(bsz, length, GDN_H, GDN_DV))
    return o.reshape(bsz, length, GDN_W), jnp.stack(fins, axis=1)


def mixer(h, lp, s5_h0_re, s5_h0_im, gdn_s0):
    proj = jnp.dot(h, lp['w_in'])
    u, z_s5, qkv, z_gdn, beta_raw, a_raw = jnp.split(proj, SPLITS, axis=-1)
    s5_out, s5_re, s5_im = s5_mixer(u, z_s5, lp['s5_lambda_re'], lp['s5_lambda_im'], lp['s5_log_dt'],
                                    lp['s5_b_re'], lp['s5_b_im'], lp['s5_c_re'], lp['s5_c_im'],
                                    lp['s5_d'], s5_h0_re, s5_h0_im)
    gdn_out, gdn_state = gdn_mixer(qkv, z_gdn, beta_raw, a_raw, lp['conv_qkv'], lp['gdn_a_log'],
                                   lp['gdn_dt_bias'], lp['gdn_norm'], gdn_s0)
    mixed = jnp.concatenate([s5_out, gdn_out], axis=-1).astype(h.dtype)
    return jnp.dot(mixed, lp['w_out']), s5_re, s5_im, gdn_state


def trunk_layer(x, mods, lp, s5_h0_re, s5_h0_im, gdn_s0):
    sh1, sc1, gt1, sh2, sc2, gt2 = mods
    h = rmsnorm(x, lp['norm_mix']) * (1 + sc1) + sh1
    mix, s5_re, s5_im, gdn_state = mixer(h, lp, s5_h0_re, s5_h0_im, gdn_s0)
    x = x + gt1 * mix
    h = rmsnorm(x, lp['norm_mlp']) * (1 + sc2) + sh2
    ff = jnp.square(jax.nn.relu(jnp.dot(h, lp['w_mlp_in'])))
    x = x + gt2 * jnp.dot(ff, lp['w_mlp_out'])
    return x, s5_re, s5_im, gdn_state


def setup_inputs(seed: int = 0) -> dict:
    key = jax.random.key(seed)
    ks = jax.random.split(key, 32)

    def nrm(k, shape, s):
        return jax.random.normal(k, shape, F32) * s

    s5_shape = (DEPTH, N_DIR, S5_G, S5_N)
    dt_gdn = jnp.exp(jax.random.uniform(ks[22], (DEPTH, N_DIR, GDN_H), F32, math.log(1e-3), math.log(1e-1)))
    n_idx = jnp.arange(S5_N, dtype=F32)
    return {
        'x_prompt': nrm(ks[0], (BATCH, SEQ, D_MODEL), 1.0),
        'x_sample': nrm(ks[1], (DEC_BATCH, DEC_SEQ, D_MODEL), 1.0),
        'state_s5_re': nrm(ks[2], (DEC_BATCH, DEPTH, N_DIR, S5_G, S5_N), 0.1),
        'state_s5_im': nrm(ks[3], (DEC_BATCH, DEPTH, N_DIR, S5_G, S5_N), 0.1),
        'state_gdn': nrm(ks[4], (DEC_BATCH, DEPTH, N_DIR, GDN_H, GDN_DK, GDN_DV), 0.05),
        'c': nrm(ks[5], (DEC_BATCH, D_MODEL), 1.0),
        'c_ctx': nrm(ks[6], (D_MODEL,), 1.0),
        'norm_mix': 1.0 + nrm(ks[7], (DEPTH, D_MODEL), 0.02),
        'norm_mlp': 1.0 + nrm(ks[8], (DEPTH, D_MODEL), 0.02),
        'w_ada': nrm(ks[9], (DEPTH, D_MODEL, N_MOD * D_MODEL), 0.5 * D_MODEL ** -0.5),
        'b_ada': nrm(ks[10], (DEPTH, N_MOD * D_MODEL), 0.02),
        'w_in': nrm(ks[11], (DEPTH, D_MODEL, IN_W), D_MODEL ** -0.5),
        'conv_qkv': nrm(ks[12], (DEPTH, CONV_K, 3 * GDN_W), CONV_K ** -0.5),
        's5_lambda_re': -0.5 + nrm(ks[13], s5_shape, 0.01),
        's5_lambda_im': jnp.pi * n_idx + nrm(ks[14], s5_shape, 0.01),
        's5_log_dt': jax.random.uniform(ks[15], (DEPTH, N_DIR, S5_G), F32, math.log(1e-3), math.log(1e-1)),
        's5_b_re': nrm(ks[16], (DEPTH, N_DIR, S5_G, S5_N, S5_P), (2 * S5_P) ** -0.5),
        's5_b_im': nrm(ks[17], (DEPTH, N_DIR, S5_G, S5_N, S5_P), (2 * S5_P) ** -0.5),
        's5_c_re': nrm(ks[18], (DEPTH, N_DIR, S5_G, S5_P, S5_N), (2 * S5_N) ** -0.5),
        's5_c_im': nrm(ks[19], (DEPTH, N_DIR, S5_G, S5_P, S5_N), (2 * S5_N) ** -0.5),
        's5_d': nrm(ks[20], (DEPTH, S5_W), 1.0),
        'gdn_a_log': jnp.log(jax.random.uniform(ks[21], (DEPTH, N_DIR, GDN_H), F32, 1.0, 16.0)),
        'gdn_dt_bias': dt_gdn + jnp.log(-jnp.expm1(-dt_gdn)),
        'gdn_norm': 1.0 + nrm(ks[23], (DEPTH, GDN_DV), 0.02),
        'w_out': nrm(ks[24], (DEPTH, MIX_W, D_MODEL), MIX_W ** -0.5),
        'w_mlp_in': nrm(ks[25], (DEPTH, D_MODEL, D_FF), D_MODEL ** -0.5),
        'w_mlp_out': nrm(ks[26], (DEPTH, D_FF, D_MODEL), D_FF ** -0.5),
        'norm_final': 1.0 + nrm(ks[27], (D_MODEL,), 0.02),
    }


def reference(x_prompt, x_sample, state_s5_re, state_s5_im, state_gdn, c, c_ctx,
              norm_mix, norm_mlp, w_ada, b_ada, w_in, conv_qkv,
              s5_lambda_re, s5_lambda_im, s5_log_dt, s5_b_re, s5_b_im, s5_c_re, s5_c_im, s5_d,
              gdn_a_log, gdn_dt_bias, gdn_norm, w_out, w_mlp_in, w_mlp_out, norm_final):
    n_req = x_prompt.shape[0]
    x_ctx = x_prompt
    x_lat = x_sample + grid_pos_embed(x_sample.shape[1]).astype(x_sample.dtype)
    zero_s5 = jnp.zeros((n_req, N_DIR, S5_G, S5_N), F32)
    zero_gdn = jnp.zeros((n_req, N_DIR, GDN_H, GDN_DK, GDN_DV), F32)
    ctx_re, ctx_im, ctx_gdn = [], [], []
    for l in range(DEPTH):
        lp = {
            'norm_mix': norm_mix[l], 'norm_mlp': norm_mlp[l], 'w_in': w_in[l], 'conv_qkv': conv_qkv[l],
            's5_lambda_re': s5_lambda_re[l], 's5_lambda_im': s5_lambda_im[l], 's5_log_dt': s5_log_dt[l],
            's5_b_re': s5_b_re[l], 's5_b_im': s5_b_im[l], 's5_c_re': s5_c_re[l], 's5_c_im': s5_c_im[l],
            's5_d': s5_d[l], 'gdn_a_log': gdn_a_log[l], 'gdn_dt_bias': gdn_dt_bias[l],
            'gdn_norm': gdn_norm[l], 'w_out': w_out[l], 'w_mlp_in': w_mlp_in[l], 'w_mlp_out': w_mlp_out[l],
        }
        x_ctx, s_re, s_im, s_gdn = trunk_layer(x_ctx, modulation(c_ctx, w_ada[l], b_ada[l]), lp,
                                               zero_s5, zero_s5, zero_gdn)
        ctx_re.append(s_re)
        ctx_im.append(s_im)
        ctx_gdn.append(s_gdn)
        x_lat, _, _, _ = trunk_layer(x_lat, modulation(c, w_ada[l], b_ada[l]), lp,
                                     state_s5_re[:, l], state_s5_im[:, l], state_gdn[:, l])
    y_prompt = rmsnorm(x_ctx, norm_final)
    y_sample = rmsnorm(x_lat, norm_final)
    new_state_s5_re = jnp.stack(ctx_re, axis=1)
    new_state_s5_im = jnp.stack(ctx_im, axis=1)
    new_state_gdn = jnp.stack(ctx_gdn, axis=1)
    return (y_prompt, y_sample, new_state_s5_re, new_state_s5_im, new_state_gdn)
```

```python
import contextlib
import math
import numpy as np
import concourse.bass as bass
import concourse.mybir as mybir
from concourse.bass_utils import run_bass_kernel_spmd

F32 = mybir.dt.float32
BF16 = mybir.dt.bfloat16
I32 = mybir.dt.int32
ALU = mybir.AluOpType
ACT = mybir.ActivationFunctionType
AX = mybir.AxisListType

D = 1024
DEPTH = 4
NPS = 4
LP = 256
LS = 4096
NTOK = NPS * LP + LS
IN_W = 3088
DFF = 4096
EPS = 1e-6
TWO_PI = 2.0 * math.pi

SEQS = [(i * LP, LP) for i in range(NPS)] + [(NPS * LP, LS)]


class Buf:
    def __init__(self, t, name):
        self.t = t
        self.name = name
        self.lastw = None
        self.readers = {}

    def __getitem__(self, k):
        return self.t[k]


class K:
    def __init__(self, nc, es):
        self.nc = nc
        self.es = es
        self.eng = {"pe": nc.tensor, "dve": nc.vector, "act": nc.scalar, "pool": nc.gpsimd, "sp": nc.sync}
        self.sem = {}
        self.cnt = {}
        for e in self.eng:
            self.sem[e] = es.enter_context(nc.semaphore("sem_" + e))
            self.cnt[e] = 0
        self.seen = {e: {} for e in self.eng}
        self.dsem = {"sp": [], "pool": [], "act": [], "bg": []}
        self.duse = {"sp": [], "pool": [], "act": [], "bg": []}
        self.dnext = {"sp": 0, "pool": 0, "act": 0, "bg": 0}
        self.eng["bg"] = nc.gpsimd
        self.seen["bg"] = self.seen["pool"]
        for q, n in (("sp", 16), ("pool", 12), ("act", 8), ("bg", 8)):
            for i in range(n):
                self.dsem[q].append(es.enter_context(nc.semaphore("dsem_%s%d" % (q, i))))
                self.duse[q].append(0)
        self.ninst = 0

    def sb(self, name, shape, dt):
        self.uid = getattr(self, "uid", 0) + 1
        name = "%s_u%d" % (name, self.uid)
        return Buf(self.es.enter_context(self.nc.sbuf_tensor(name, list(shape), dt)), name)

    def ps(self, name, shape, dt):
        return Buf(self.es.enter_context(self.nc.psum_tensor(name, list(shape), dt)), name)

    def dram(self, name, shape, dt, kind="Internal"):
        b = Buf(self.nc.dram_tensor(name, list(shape), dt, kind=kind).ap(), name)
        b.is_dram = True
        return b

    def _wait(self, e, dep):
        if dep is None:
            return
        sem, val = dep
        key = sem.name
        if self.seen[e].get(key, 0) >= val:
            return
        self.eng[e].wait_ge(sem, val)
        self.seen[e][key] = val

    def _deps(self, e, reads, writes):
        for b in reads:
            self._wait(e, b.lastw)
        for b in writes:
            self._wait(e, b.lastw)
            for r in b.readers.values():
                self._wait(e, r)

    def _commit(self, tok, reads, writes):
        for b in reads:
            b.readers[tok[0].name] = tok
        for b in writes:
            b.lastw = tok
            b.readers = {}

    def op(self, e, inst_fn, reads=(), writes=(), inc=True):
        self._deps(e, reads, writes)
        inst = inst_fn(self.eng[e])
        if inc or e != "pe":
            self.cnt[e] += 1
            inst.then_inc(self.sem[e], 1)
            tok = (self.sem[e], self.cnt[e])
        else:
            tok = (self.sem[e], self.cnt[e] + 1)
        if e == "pe":
            self.seen["pe"][self.sem["pe"].name] = tok[1]
        self._commit(tok, reads, writes)
        self.ninst += 1
        self._yield()
        return inst

    def dma(self, q, out, in_, reads=(), writes=(), slow=False):
        eq = "pool" if q == "bg" else q
        i = self.dnext[q]
        self.dnext[q] = (i + 1) % len(self.dsem[q])
        sem = self.dsem[q][i]
        if self.duse[q][i] > 0:
            self._wait(eq, (sem, 16 * self.duse[q][i]))
        self._deps(eq, reads, writes)
        self.duse[q][i] += 1
        if slow:
            inst = self.eng[eq].dma_start(out=out, in_=in_, allow_slow_non_contiguous=True)
        else:
            inst = self.eng[eq].dma_start(out=out, in_=in_)
        inst.then_inc(sem, 16)
        tok = (sem, 16 * self.duse[q][i])
        self._commit(tok, reads, writes)
        self.ninst += 1
        self._yield()
        return inst

    def _yield(self, force=False):
        il = getattr(self, "_il", None)
        if il is None:
            return
        import threading
        me = il["ids"].get(threading.get_ident())
        if me is None:
            return
        if not force:
            il["cnt"][me] += 1
            if il["cnt"][me] % il["w"][me] != 0:
                return
        with il["cv"]:
            alive = [i for i in range(il["n"]) if il["alive"][i]]
            nxt = [i for i in alive if i > me] + [i for i in alive if i < me]
            if nxt:
                il["turn"] = nxt[0]
                il["cv"].notify_all()
                while il["turn"] != me:
                    il["cv"].wait()

    def interleave(self, fns, weights=None):
        import threading
        n = len(fns)
        il = {"turn": 0, "alive": [True] * n, "cv": threading.Condition(), "n": n, "ids": {}, "err": [],
              "w": list(weights or [1] * n), "cnt": [0] * n}
        self._il = il

        def runner(i):
            il["ids"][threading.get_ident()] = i
            try:
                with il["cv"]:
                    while il["turn"] != i:
                        il["cv"].wait()
                fns[i]()
            except BaseException as ex:
                il["err"].append(ex)
            finally:
                with il["cv"]:
                    il["alive"][i] = False
                    alive = [j for j in range(n) if il["alive"][j]]
                    if alive:
                        nxt = [j for j in alive if j > i] + [j for j in alive if j < i]
                        il["turn"] = nxt[0]
                    il["cv"].notify_all()

        ths = [threading.Thread(target=runner, args=(i,)) for i in range(n)]
        for t in ths:
            t.start()
        for t in ths:
            t.join()
        self._il = None
        if il["err"]:
            raise il["err"][0]

    def barrier(self, bufs=()):
        toks = [(self.sem[e], self.cnt[e]) for e in self.sem if self.cnt[e] > 0]
        for q in self.dsem:
            if q == "bg" and not getattr(self, "_final", False):
                continue
            for i, s in enumerate(self.dsem[q]):
                if self.duse[q][i] > 0:
                    toks.append((s, 16 * self.duse[q][i]))
        for e in self.sem:
            for t in toks:
                self._wait(e, t)

    def finish(self, outs):
        self._final = True
        for b in outs:
            self._wait("sp", b.lastw)
        self.barrier()


def build(flags=None):
    flags = flags or {}
    TEST_DENSE = flags.get("test_dense", False)
    nlayers = flags.get("nlayers", DEPTH)

    nc = bass.Bass("TRN2", target_bir_lowering=False)
    es = contextlib.ExitStack()
    k = K(nc, es)

    def din(name, shape, dt=F32):
        return k.dram(name, shape, dt, kind="ExternalInput")

    def dout(name, shape, dt=F32):
        return k.dram(name, shape, dt, kind="ExternalOutput")

    xin = din("xin", [NTOK, D])
    cvec = din("cvec", [2, D])
    h0re = din("h0re", [DEPTH, 2, 32, 64])
    h0im = din("h0im", [DEPTH, 2, 32, 64])
    s0 = din("s0", [DEPTH, 2, 4, 128, 128])
    norm_mix = din("norm_mix", [DEPTH, D])
    norm_mlp = din("norm_mlp", [DEPTH, D])
    w_ada = din("w_ada", [DEPTH, D, 6 * D])
    b_ada = din("b_ada", [DEPTH, 6 * D])
    w_in = din("w_in", [DEPTH, D, IN_W])
    conv_qkv = din("conv_qkv", [DEPTH, 3, 1536])
    lam_re = din("s5_lambda_re", [DEPTH, 2, 32, 64])
    lam_im = din("s5_lambda_im", [DEPTH, 2, 32, 64])
    log_dt = din("s5_log_dt", [DEPTH, 2, 32])
    b_re = din("s5_b_re", [DEPTH, 2, 32, 64, 16])
    b_im = din("s5_b_im", [DEPTH, 2, 32, 64, 16])
    c_re = din("s5_c_re", [DEPTH, 2, 32, 16, 64])
    c_im = din("s5_c_im", [DEPTH, 2, 32, 16, 64])
    s5_d = din("s5_d", [DEPTH, 512])
    a_log = din("gdn_a_log", [DEPTH, 2, 4])
    dt_bias = din("gdn_dt_bias", [DEPTH, 2, 4])
    gdn_norm = din("gdn_norm", [DEPTH, 128])
    w_out = din("w_out", [DEPTH, D, D])
    w_mlp_in = din("w_mlp_in", [DEPTH, D, DFF])
    w_mlp_out = din("w_mlp_out", [DEPTH, DFF, D])
    norm_final = din("norm_final", [D])
    ident_in = din("ident", [128, 128])
    cmask_in = din("cmask", [2, 128, 128])
    selrows_in = din("selrows", [8, 8 * 128])
    sel8_in = din("sel8", [8, 2])
    swap_in = din("swapm", [128, 128])
    lvl_in = din("lvlmask", [4, 128, 128])

    y_out = dout("y", [NTOK, D])
    nre_out = dout("nre", [NPS, DEPTH, 2, 32, 64])
    nim_out = dout("nim", [NPS, DEPTH, 2, 32, 64])
    ngdn_out = dout("ngdn", [NPS, DEPTH, 2, 4, 128, 128])

    DBGK = "ExternalOutput" if flags.get("dbg") else "Internal"
    xres = k.dram("xres", [NTOK, D], F32)
    projT = k.dram("projT", [IN_W + 16, NTOK], F32, kind=DBGK)
    mixT = k.dram("mixT", [D, NTOK], BF16, kind=DBGK)
    mod_d = k.dram("mod_d", [2, 6 * D], F32)
    w1bs = [k.dram("w1b%d" % i, [D, DFF], BF16) for i in range(2)]
    w2bs = [k.dram("w2b%d" % i, [DFF, D], BF16) for i in range(2)]

    winbs = [k.dram("winb%d" % i, [D, IN_W], BF16) for i in range(2)]
    woutbs = [k.dram("woutb%d" % i, [D, D], BF16) for i in range(2)]

    def cast_mlp_weights(layer):
        for c in range(8):
            k.dma("bg", winbs[layer % 2].t[c * 128:(c + 1) * 128, :], w_in.t[layer, c * 128:(c + 1) * 128, :],
                  reads=[w_in], writes=[winbs[layer % 2]])
        for c in range(8):
            k.dma("bg", woutbs[layer % 2].t[c * 128:(c + 1) * 128, :], w_out.t[layer, c * 128:(c + 1) * 128, :],
                  reads=[w_out], writes=[woutbs[layer % 2]])
        for c in range(8):
            k.dma("bg", w1bs[layer % 2].t[c * 128:(c + 1) * 128, :], w_mlp_in.t[layer, c * 128:(c + 1) * 128, :],
                  reads=[w_mlp_in], writes=[w1bs[layer % 2]])
        for c in range(32):
            k.dma("bg", w2bs[layer % 2].t[c * 128:(c + 1) * 128, :], w_mlp_out.t[layer, c * 128:(c + 1) * 128, :],
                  reads=[w_mlp_out], writes=[w2bs[layer % 2]])
    pe_d = k.dram("pe_d", [64, 512], F32)
    gqkvT = k.dram("gqkvT", [1536, NTOK], BF16, kind=DBGK)
    gkv_tm = k.dram("gkv_tm", [NTOK, 8, 128], BF16, kind=DBGK)
    grow = k.dram("grow", [2, 8, NTOK], F32, kind=DBGK)
    gtm = k.dram("gtm", [NTOK, 16], F32, kind=DBGK)
    ktab = k.dram("ktab", [2, 4, 128, 32 * 128], BF16, kind=DBGK)
    wctab = k.dram("wctab", [64, 128, 32 * 32], BF16, kind=DBGK)
    ofT = k.dram("ofT", [512, NTOK], F32, kind=DBGK)
    obT = k.dram("obT", [512, NTOK], F32, kind=DBGK)
    odT = [ofT, obT]

    ident_f = k.sb("ident_f", [128, 128], F32)
    ident_b = k.sb("ident_b", [128, 128], BF16)
    k.dma("sp", ident_f[:], ident_in[:, :], writes=[ident_f])
    k.op("dve", lambda e: e.tensor_copy(out=ident_b[:], in_=ident_f[:]), reads=[ident_f], writes=[ident_b])
    sc_t = k.sb("sc_t", [128, 8, 2], F32)
    sc_b = k.sb("sc_b", [128, 8, 2], BF16)
    epsc = k.sb("epsc", [128, 1], F32)
    k.op("dve", lambda e: e.memset(epsc[:], EPS), writes=[epsc])

    psf = [k.ps("psf%d" % i, [128, 512], F32) for i in range(6)]
    psb = k.ps("psb", [128, 1024], BF16)
    psb2 = k.ps("psb2", [128, 1024], BF16)
    psf6 = Buf(psb2.t[:].bitcast(F32), "psb2_as_f32")
    psi = [0]

    def nps():
        p = psf[psi[0] % 6]
        psi[0] += 1
        return p

    cast_mlp_weights(0)

    with contextlib.ExitStack() as es0:
        k.es = es0
        craw = k.sb("craw", [128, 2, 8], F32)
        for v in range(2):
            k.dma("sp", craw[:, v, :], cvec.t[v, :].rearrange("(c p) -> p c", p=128), reads=[cvec], writes=[craw], slow=True)
        k.op("act", lambda e: e.activation(out=sc_t[:].rearrange("p c v -> p v c"), in_=craw[:], func=ACT.Silu),
             reads=[craw], writes=[sc_t])
        k.op("dve", lambda e: e.tensor_copy(out=sc_b[:], in_=sc_t[:]), reads=[sc_t], writes=[sc_b])

        jrow = k.sb("jrow", [64, 256], F32)
        rcol = k.sb("rcol", [64, 1], F32)
        k.op("pool", lambda e: e.iota(jrow[:], pattern=[[1, 256]], base=0, channel_multiplier=0,
                                      allow_small_or_imprecise_dtypes=True), writes=[jrow])
        k.op("pool", lambda e: e.iota(rcol[:], pattern=[[0, 1]], base=0, channel_multiplier=1,
                                      allow_small_or_imprecise_dtypes=True), writes=[rcol])
        om = k.sb("om", [64, 256], F32)
        k.op("act", lambda e: e.activation(out=om[:], in_=jrow[:], func=ACT.Exp, scale=-math.log(10000.0) / 256.0),
             reads=[jrow], writes=[om])
        ang = k.sb("ang", [64, 2, 256], F32)
        k.op("dve", lambda e: e.tensor_scalar(out=ang[:, 0, :], in0=om[:], scalar1=rcol[:, 0:1], scalar2=None,
                                              op0=ALU.mult), reads=[om, rcol], writes=[ang])
        k.op("dve", lambda e: e.tensor_scalar(out=ang[:, 1, :], in0=ang[:, 0, :], scalar1=math.pi / 2, scalar2=None,
                                              op0=ALU.add), reads=[ang], writes=[ang])
        pet = k.sb("pet", [64, 2, 256], F32)

        def sin_reduced(dst, src, shape, tagn):
            kf = k.sb("kf" + tagn, shape, F32)
            ki = k.sb("ki" + tagn, shape, I32)
            k.op("dve", lambda e: e.tensor_scalar(out=kf[:], in0=src, scalar1=1.0 / TWO_PI, scalar2=None,
                                                  op0=ALU.mult), reads=[], writes=[kf])
            k.op("dve", lambda e: e.tensor_copy(out=ki[:], in_=kf[:]), reads=[kf], writes=[ki])
            k.op("dve", lambda e: e.tensor_copy(out=kf[:], in_=ki[:]), reads=[ki], writes=[kf])
            k.op("dve", lambda e: e.scalar_tensor_tensor(out=kf[:], in0=kf[:], scalar=-TWO_PI, in1=src,
                                                         op0=ALU.mult, op1=ALU.add), reads=[kf], writes=[kf])
            k.op("dve", lambda e: e.tensor_scalar(out=kf[:], in0=kf[:], scalar1=math.pi, scalar2=-math.pi,
                                                  op0=ALU.min, op1=ALU.max), reads=[kf], writes=[kf])
            k.op("act", lambda e: e.activation(out=dst, in_=kf[:], func=ACT.Sin), reads=[kf], writes=[])

        k.barrier()
        sin_reduced(pet[:], ang[:], [64, 2, 256], "pe")
        k.barrier()
        k.dma("sp", pe_d.t[:, :], pet[:].rearrange("p a b -> p (a b)"), reads=[pet], writes=[pe_d])

        pcol = k.sb("pcol", [128, 512], F32)
        k.dma("sp", pcol[0:64, :], pe_d.t[:, :], reads=[pe_d], writes=[pcol])
        k.dma("sp", pcol[64:128, :], pe_d.t[:, :], reads=[pe_d], writes=[pcol])
        xt0 = [k.sb("xt0_%d" % i, [128, D], F32) for i in range(2)]
        prow = [k.sb("prow%d" % i, [128, 512], F32) for i in range(2)]
        for t in range(NTOK // 128):
            xb = xt0[t % 2]
            k.dma("sp", xb[:], xin.t[t * 128:(t + 1) * 128, :], reads=[xin], writes=[xb])
            if t * 128 >= NPS * LP:
                ti = t - NPS * LP // 128
                pr = prow[t % 2]
                k.dma("pool", pr[0:64, :], pe_d.t[2 * ti:2 * ti + 1, :].to_broadcast([64, 512]), reads=[pe_d], writes=[pr])
                k.dma("pool", pr[64:128, :], pe_d.t[2 * ti + 1:2 * ti + 2, :].to_broadcast([64, 512]), reads=[pe_d], writes=[pr])
                k.op("dve", lambda e: e.tensor_tensor(out=xb[:, 0:512], in0=xb[:, 0:512], in1=pr[:], op=ALU.add),
                     reads=[xb, pr], writes=[xb])
                k.op("dve", lambda e: e.tensor_tensor(out=xb[:, 512:1024], in0=xb[:, 512:1024], in1=pcol[:], op=ALU.add),
                     reads=[xb, pcol], writes=[xb])
            k.dma("sp", xres.t[t * 128:(t + 1) * 128, :], xb[:], reads=[xb], writes=[xres])
        k.barrier()
    k.es = es

    def rms_tile(xb, hb, A, SH, sq, ss):
        k.op("act", lambda e: e.activation(out=sq[:], in_=xb[:], func=ACT.Square, accum_out=ss[:, 0:1]),
             reads=[xb], writes=[sq, ss])
        k.op("act", lambda e: e.activation(out=ss[:, 1:2], in_=ss[:, 0:1], func=ACT.Ln, scale=1.0 / D, bias=epsc[:, 0:1]),
             reads=[ss, epsc], writes=[ss])
        k.op("act", lambda e: e.activation(out=ss[:, 2:3], in_=ss[:, 1:2], func=ACT.Exp, scale=-0.5),
             reads=[ss], writes=[ss])
        if A is None:
            return
        k.op("dve", lambda e: e.scalar_tensor_tensor(out=sq[:], in0=xb[:], scalar=ss[:, 2:3], in1=A[:],
                                                     op0=ALU.mult, op1=ALU.mult), reads=[xb, ss, A], writes=[sq])
        k.op("dve", lambda e: e.tensor_tensor(out=hb[:], in0=sq[:], in1=SH[:], op=ALU.add),
             reads=[sq, SH], writes=[hb])

    def transpose_to(hb, hT, j):
        for c in range(8):
            k.op("pe", lambda e, c=c: e.transpose(out=psb[:, c * 128:(c + 1) * 128], in_=hb[:, c * 128:(c + 1) * 128],
                                                  identity=ident_b[:]), reads=[hb, ident_b], writes=[psb])
        k.op("dve", lambda e: e.tensor_copy(out=hT[:, :, j * 128:(j + 1) * 128],
                                            in_=psb[:].rearrange("p (c t) -> p c t", c=8)), reads=[psb], writes=[hT])

    def load_mod(dst, v, idx, layer, tmpb, kind, gsrc=None, gt=None):
        k.dma("sp", dst[:], mod_d.t[v:v + 1, idx * D:(idx + 1) * D].to_broadcast([128, D]), reads=[mod_d], writes=[dst])
        if kind == "scale":
            k.dma("sp", gt[:], gsrc.t[layer:layer + 1, :].to_broadcast([128, D]), reads=[gsrc], writes=[gt])
            k.op("dve", lambda e: e.scalar_tensor_tensor(out=dst[:], in0=dst[:], scalar=1.0, in1=gt[:],
                                                         op0=ALU.add, op1=ALU.mult), reads=[dst, gt], writes=[dst])

    PIECES = []
    for (s0_, l_) in SEQS:
        for off in range(0, l_, 512):
            PIECES.append((s0_ + off, min(512, l_ - off), s0_, s0_ + l_))

    def V(fn, r, w):
        return k.op("dve", fn, reads=r, writes=w)

    def A(fn, r, w):
        return k.op("act", fn, reads=r, writes=w)

    def PE(fn, r, w):
        return k.op("pe", fn, reads=r, writes=w)

    def gdn_phase(layer):
        with contextlib.ExitStack() as esg:
            k.es = esg
            ones_b = k.sb("ones_b", [128, 128], BF16)
            V(lambda e: e.memset(ones_b[:], 1.0), [], [ones_b])
            cm = [k.sb("cm%d" % d, [128, 4, 128], F32) for d in range(2)]
            for d in range(2):
                for h in range(4):
                    k.dma("sp", cm[d][:, h, :], cmask_in.t[d, :, :], reads=[cmask_in], writes=[cm[d]])
            offd = k.sb("offd", [128, 4, 128], F32)
            idb4 = k.sb("idb4", [128, 4, 128], BF16)
            for h in range(4):
                V(lambda e, h=h: e.tensor_scalar(out=offd[:, h, :], in0=ident_f[:], scalar1=-1.0, scalar2=1.0,
                                                 op0=ALU.mult, op1=ALU.add), [ident_f], [offd])
                V(lambda e, h=h: e.tensor_copy(out=idb4[:, h, :], in_=ident_f[:]), [ident_f], [idb4])
            lvm = [k.sb("lvm%d" % i, [128, 4, 128], F32) for i in range(4)]
            for i in range(4):
                for h in range(4):
                    k.dma("sp", lvm[i][:, h, :], lvl_in.t[i, :, :], reads=[lvl_in], writes=[lvm[i]])
            idf4 = k.sb("idf4", [128, 4, 128], F32)
            for h in range(4):
                V(lambda e, h=h: e.tensor_copy(out=idf4[:, h, :], in_=ident_f[:]), [ident_f], [idf4])
            selr = k.sb("selr", [8, 8, 128], F32)
            k.dma("sp", selr[:].rearrange("p a b -> p (a b)"), selrows_in.t[:, :], reads=[selrows_in], writes=[selr])
            sel8 = k.sb("sel8", [8, 2], F32)
            k.dma("sp", sel8[:], sel8_in.t[:, :], reads=[sel8_in], writes=[sel8])
            convw = k.sb("convw", [128, 3, 12], F32)
            for tap in range(3):
                k.dma("sp", convw[:, tap, :], conv_qkv.t[layer, tap, :].rearrange("(b p) -> p b", p=128),
                      reads=[conv_qkv], writes=[convw], slow=True)
            dg = k.sb("dg", [128, 36, 128], BF16)
            for tap in range(3):
                for blk in range(12):
                    V(lambda e, tap=tap, blk=blk: e.tensor_scalar(
                        out=dg[:, tap * 12 + blk, :], in0=ident_f[:], scalar1=convw[:, tap, blk:blk + 1], scalar2=None,
                        op0=ALU.mult), [ident_f, convw], [dg])
            dtb = k.sb("dtb", [8, 1], F32)
            nal = k.sb("nal", [8, 1], F32)
            k.dma("sp", dtb[:], dt_bias.t[layer].rearrange("d (h o) -> (d h) o", o=1), reads=[dt_bias], writes=[dtb])
            k.dma("sp", nal[:], a_log.t[layer].rearrange("d (h o) -> (d h) o", o=1), reads=[a_log], writes=[nal])
            A(lambda e: e.activation(out=nal[:], in_=nal[:], func=ACT.Exp), [nal], [nal])
            V(lambda e: e.tensor_scalar(out=nal[:], in0=nal[:], scalar1=-1.0, scalar2=None, op0=ALU.mult), [nal], [nal])
            gnw = k.sb("gnw", [128, 1], F32)
            k.dma("sp", gnw[:], gdn_norm.t[layer, :].rearrange("(p o) -> p o", o=1), reads=[gdn_norm], writes=[gnw])
            mask01 = k.sb("mask01", [8, 512], F32)
            V(lambda e: e.memset(mask01[:], 1.0), [], [mask01])
            V(lambda e: e.memset(mask01[:].rearrange("p (c t) -> p c t", t=128)[:, :, 0:1], 0.0), [], [mask01])

            esp = contextlib.ExitStack()
            k.es = esp
            psbs = [psb, psb2]
            class TSP:
                pass

            def mk_prep(tag):
                P_ = TSP()
                P_.raw = [k.sb(tag + "raw%d" % i, [128, 12, 514], BF16) for i in range(2)]
                P_.cv = [k.sb(tag + "cv%d" % i, [128, 12, 512], BF16) for i in range(2)]
                P_.sqb = k.sb(tag + "sqb", [128, 512], BF16)
                P_.lnt = k.sb(tag + "lnt", [128, 512], F32)
                P_.rnt = k.sb(tag + "rnt", [128, 512], F32)
                P_.ktm = [k.sb(tag + "ktm%d" % i, [128, 8, 128], BF16) for i in range(2)]
                P_.br = k.sb(tag + "br", [8, 512], F32)
                P_.ar = k.sb(tag + "ar", [8, 512], F32)
                P_.bt = k.sb(tag + "bt", [8, 512], F32)
                P_.gg = k.sb(tag + "gg", [8, 512], F32)
                P_.Gf = k.sb(tag + "Gf", [8, 512], F32)
                P_.Gb = k.sb(tag + "Gb", [8, 512], F32)
                P_.Gm = k.sb(tag + "Gm", [8, 512], F32)
                P_.gtt = [k.sb(tag + "gtt%d" % i, [128, 16], F32) for i in range(2)]
                return P_

            PP_ = [mk_prep("pa_"), mk_prep("pb_")]

            def prep_chain(par):
                P_ = PP_[par]
                raw = P_.raw
                cv = P_.cv
                sqb = P_.sqb
                lnt = P_.lnt
                rnt = P_.rnt
                ktm = P_.ktm
                br = P_.br
                ar = P_.ar
                bt = P_.bt
                gg = P_.gg
                Gf = P_.Gf
                Gb = P_.Gb
                Gm = P_.Gm
                gtt = P_.gtt
                psb_ = psbs[par]
                bkp = psf[3 * par:3 * par + 3]
                bip = [0]

                def nps():
                    p = bkp[bip[0] % 3]
                    bip[0] += 1
                    return p

                for pi, (st, ln, sq0, sq1) in enumerate(PIECES):
                    if pi % 2 != par:
                        continue
                    rw = raw[pi % 2]
                    cvp = cv[pi % 2]
                    src = projT.t[1024:2560, :].rearrange("(b p) t -> p b t", p=128)
                    k.dma("pool", rw[:, :, 1:ln + 1], src[:, :, st:st + ln], reads=[projT], writes=[rw])
                    if st == sq0:
                        V(lambda e, rw=rw: e.memset(rw[:, :, 0:1], 0.0), [], [rw])
                    else:
                        k.dma("pool", rw[:, :, 0:1], src[:, :, st - 1:st], reads=[projT], writes=[rw], slow=True)
                    if st + ln == sq1:
                        V(lambda e, rw=rw, ln=ln: e.memset(rw[:, :, ln + 1:ln + 2], 0.0), [], [rw])
                    else:
                        k.dma("pool", rw[:, :, ln + 1:ln + 2], src[:, :, st + ln:st + ln + 1], reads=[projT], writes=[rw], slow=True)
                    for blk in range(12):
                        pp = nps()
                        for tap in range(3):
                            PE(lambda e, pp=pp, tap=tap, blk=blk, rw=rw, ln=ln: e.matmul(
                                out=pp[:, 0:ln], lhsT=dg[:, tap * 12 + blk, :], rhs=rw[:, blk, tap:tap + ln],
                                start=(tap == 0), stop=(tap == 2)), [dg, rw], [pp])
                        A(lambda e, pp=pp, blk=blk, cvp=cvp, ln=ln: e.activation(out=cvp[:, blk, 0:ln], in_=pp[:, 0:ln], func=ACT.Silu),
                          [pp], [cvp])
                    for blk in range(8):
                        V(lambda e, blk=blk, cvp=cvp, ln=ln: e.tensor_tensor(out=sqb[:, 0:ln], in0=cvp[:, blk, 0:ln],
                                                                           in1=cvp[:, blk, 0:ln], op=ALU.mult), [cvp], [sqb])
                        pp = nps()
                        PE(lambda e, pp=pp, ln=ln: e.matmul(out=pp[:, 0:ln], lhsT=ones_b[:], rhs=sqb[:, 0:ln], start=True, stop=True),
                           [ones_b, sqb], [pp])
                        A(lambda e, pp=pp, ln=ln: e.activation(out=lnt[:, 0:ln], in_=pp[:, 0:ln], func=ACT.Ln, bias=epsc[:, 0:1]),
                          [pp, epsc], [lnt])
                        A(lambda e, ln=ln: e.activation(out=rnt[:, 0:ln], in_=lnt[:, 0:ln], func=ACT.Exp, scale=-0.5), [lnt], [rnt])
                        scl = (128.0 ** -0.5) if blk < 4 else 1.0
                        V(lambda e, blk=blk, cvp=cvp, ln=ln, scl=scl: e.scalar_tensor_tensor(
                            out=cvp[:, blk, 0:ln], in0=cvp[:, blk, 0:ln], scalar=scl, in1=rnt[:, 0:ln],
                            op0=ALU.mult, op1=ALU.mult), [cvp, rnt], [cvp])
                    k.dma("sp", gqkvT.t[:, st:st + ln].rearrange("(b p) t -> p b t", p=128), cvp[:, :, 0:ln],
                          reads=[cvp], writes=[gqkvT])
                    for tt in range(ln // 128):
                        kt = ktm[tt % 2]
                        for b8 in range(8):
                            PE(lambda e, b8=b8, tt=tt, cvp=cvp: e.transpose(
                                out=psb_[:, b8 * 128:(b8 + 1) * 128], in_=cvp[:, 4 + b8, tt * 128:(tt + 1) * 128],
                                identity=ident_b[:]), [cvp, ident_b], [psb_])
                        V(lambda e, kt=kt: e.tensor_copy(out=kt[:], in_=psb_[:].rearrange("p (b t) -> p b t", b=8)), [psb_], [kt])
                        k.dma("sp", gkv_tm.t[st + tt * 128:st + (tt + 1) * 128, :, :], kt[:], reads=[kt], writes=[gkv_tm])
                    k.dma("sp", br[:, 0:ln], projT.t[3072:3080, st:st + ln], reads=[projT], writes=[br])
                    k.dma("sp", ar[:, 0:ln], projT.t[3080:3088, st:st + ln], reads=[projT], writes=[ar])
                    A(lambda e, ln=ln: e.activation(out=bt[:, 0:ln], in_=br[:, 0:ln], func=ACT.Sigmoid), [br], [bt])
                    A(lambda e, ln=ln: e.activation(out=ar[:, 0:ln], in_=ar[:, 0:ln], func=ACT.Exp, bias=dtb[:, 0:1]), [ar, dtb], [ar])
                    A(lambda e, ln=ln: e.activation(out=ar[:, 0:ln], in_=ar[:, 0:ln], func=ACT.Ln, bias=1.0), [ar], [ar])
                    V(lambda e, ln=ln: e.tensor_scalar(out=gg[:, 0:ln], in0=ar[:, 0:ln], scalar1=nal[:, 0:1], scalar2=None,
                                                       op0=ALU.mult), [ar, nal], [gg])
                    V(lambda e, ln=ln: e.tensor_tensor_scan(out=Gf[:, 0:ln], data0=mask01[:, 0:ln], data1=gg[:, 0:ln],
                                                            initial=0.0, op0=ALU.mult, op1=ALU.add), [mask01, gg], [Gf])
                    nch = ln // 128
                    V(lambda e, ln=ln: e.tensor_tensor(out=Gb[:, 0:ln], in0=gg[:, 0:ln], in1=Gf[:, 0:ln], op=ALU.subtract),
                      [gg, Gf], [Gb])
                    V(lambda e, ln=ln, nch=nch: e.tensor_tensor(
                        out=Gb[:, 0:ln].rearrange("p (c t) -> p c t", t=128), in0=Gb[:, 0:ln].rearrange("p (c t) -> p c t", t=128),
                        in1=Gf[:, 0:ln].rearrange("p (c t) -> p c t", t=128)[:, :, 127:128].to_broadcast([8, nch, 128]),
                        op=ALU.add), [Gb, Gf], [Gb])
                    V(lambda e, ln=ln: e.tensor_scalar(out=Gm[:, 0:ln], in0=Gf[:, 0:ln], scalar1=sel8[:, 0:1], scalar2=None,
                                                       op0=ALU.mult), [Gf, sel8], [Gm])
                    V(lambda e, ln=ln: e.scalar_tensor_tensor(out=Gm[:, 0:ln], in0=Gb[:, 0:ln], scalar=sel8[:, 1:2], in1=Gm[:, 0:ln],
                                                              op0=ALU.mult, op1=ALU.add), [Gb, sel8, Gm], [Gm])
                    k.dma("sp", grow.t[0, :, st:st + ln], bt[:, 0:ln], reads=[bt], writes=[grow])
                    k.dma("sp", grow.t[1, :, st:st + ln], Gm[:, 0:ln], reads=[Gm], writes=[grow])
                    for tt in range(nch):
                        pp = nps()
                        PE(lambda e, pp=pp, tt=tt: e.transpose(out=pp[:, 0:8], in_=bt[:, tt * 128:(tt + 1) * 128],
                                                              identity=ident_f[0:8, 0:8]), [bt, ident_f], [pp])
                        PE(lambda e, pp=pp, tt=tt: e.transpose(out=pp[:, 8:16], in_=Gm[:, tt * 128:(tt + 1) * 128],
                                                              identity=ident_f[0:8, 0:8]), [Gm, ident_f], [pp])
                        g2 = gtt[tt % 2]
                        V(lambda e, pp=pp, g2=g2: e.tensor_copy(out=g2[:], in_=pp[:, 0:16]), [pp], [g2])
                        k.dma("sp", gtm.t[st + tt * 128:st + (tt + 1) * 128, :], g2[:], reads=[g2], writes=[gtm])

            k.interleave([lambda: prep_chain(0), lambda: prep_chain(1)])
            k.barrier()
            esp.close()
            k.es = esg

            def v4(t):
                return t[:].rearrange("p a b -> p (a b)")

            def pv4(pp):
                return pp[:].rearrange("p (a b) -> p a b", a=4)

            class TS:
                pass

            def mk_tiles(tag):
                T = TS()
                for nm, shp, dt_ in (("qk", [128, 8, 128], BF16),
                                     ("kv", [128, 8, 128], BF16), ("rows", [8, 2, 128], F32), ("gt", [128, 16], F32),
                                     ("cols", [128, 24], F32), ("tmpm", [128, 4, 128], F32), ("DTi", [128, 4, 128], F32),
                                     ("EG", [128, 4, 128], F32), ("W1", [128, 4, 128], F32), ("Nf", [128, 4, 128], F32),
                                     ("NT", [128, 4, 128], F32), ("Pm0", [128, 4, 128], F32), ("Pm1", [128, 4, 128], F32),
                                     ("PT0", [128, 4, 128], F32), ("PT1", [128, 4, 128], F32), ("Xf", [128, 4, 128], F32),

                                     ("Xb", [128, 4, 128], BF16), ("NlTb", [128, 4, 128], BF16), ("XTb", [128, 4, 128], BF16),
                                     ("Zb", [128, 4, 128], BF16), ("ktil", [128, 4, 128], BF16), ("khat", [128, 4, 128], BF16),
                                     ("bv", [128, 4, 128], BF16), ("nwT", [128, 4, 128], BF16), ("vn", [128, 4, 128], BF16),
                                     ("AT", [128, 4, 128], BF16), ("qs", [128, 4, 128], BF16), ("oT", [128, 4, 128], F32)):
                    setattr(T, nm, k.sb(nm + tag, shp, dt_))
                return T

            esm2 = contextlib.ExitStack()
            k.es = esm2
            TT = [[mk_tiles("_c%d%d" % (d_, p_)) for p_ in range(2)] for d_ in range(2)]
            SS = [(k.sb("S_d%d" % d_, [128, 4, 128], F32), k.sb("Sb_d%d" % d_, [128, 4, 128], BF16)) for d_ in range(2)]
            psf7 = Buf(psb.t[:].bitcast(F32), "psb_as_f32")
            allbanks = psf[0:6] + [psf6, psf7]
            progB = [0, 0]
            POS = {}
            for d_ in range(2):
                lst = []
                for si, (sq0, sl) in enumerate(SEQS):
                    nchunk = sl // 128
                    order = list(range(nchunk)) if d_ == 0 else list(range(nchunk - 1, -1, -1))
                    for oi, c in enumerate(order):
                        lst.append((si, sq0 + c * 128, oi == 0, oi == nchunk - 1))
                POS[d_] = lst

            def chain(d, par):
                T = TT[d][par]
                S, Sb = SS[d]
                qk, kv, rows, gt, cols, tmpm, DTi, EG, W1, Nf, NT = (T.qk, T.kv, T.rows, T.gt, T.cols, T.tmpm,
                                                                    T.DTi, T.EG, T.W1, T.Nf, T.NT)
                Pm = [T.Pm0, T.Pm1]
                PT = [T.PT0, T.PT1]
                Xf, Xb, ktil, khat, bv, nwT, vn, AT, qs, oT = (T.Xf, T.Xb, T.ktil, T.khat, T.bv, T.nwT, T.vn, T.AT, T.qs, T.oT)
                banks = allbanks[2 * (2 * d + par):2 * (2 * d + par) + 2]
                bi = [0]

                def nps():
                    p = banks[bi[0] % 2]
                    bi[0] += 1
                    return p

                last = 127 if d == 0 else 0
                for pos, (si, t0, first_, last_) in enumerate(POS[d]):
                    if pos % 2 != par:
                        continue
                    if True:
                        k.dma("sp", qk[:], gqkvT.t[0:1024, t0:t0 + 128].rearrange("(b p) t -> p b t", p=128),
                              reads=[gqkvT], writes=[qk])
                        k.dma("sp", kv[:], gkv_tm.t[t0:t0 + 128, :, :], reads=[gkv_tm], writes=[kv])
                        k.dma("sp", rows[:], grow.t[:, :, t0:t0 + 128].rearrange("a r t -> r a t"), reads=[grow], writes=[rows])
                        k.dma("sp", gt[:], gtm.t[t0:t0 + 128, :], reads=[gtm], writes=[gt])
                        pGB = nps()
                        pBB = nps()
                        for h in range(4):
                            PE(lambda e, h=h: e.matmul(out=pGB[:, h * 128:(h + 1) * 128], lhsT=selr[:, 4 * d + h, :],
                                                       rhs=rows[:, 1, :], start=True, stop=True), [selr, rows], [pGB])
                        for h in range(4):
                            PE(lambda e, h=h: e.matmul(out=pBB[:, h * 128:(h + 1) * 128], lhsT=selr[:, 4 * d + h, :],
                                                       rhs=rows[:, 0, :], start=True, stop=True), [selr, rows], [pBB])
                        V(lambda e: e.tensor_scalar(out=cols[:, 0:4], in0=gt[:, 8 + 4 * d:12 + 4 * d], scalar1=-1.0,
                                                    scalar2=None, op0=ALU.mult), [gt], [cols])
                        V(lambda e: e.tensor_copy(out=cols[:, 4:8], in_=pv4(pGB)[:, :, last]), [pGB], [cols])
                        A(lambda e: e.activation(out=cols[:, 8:12], in_=cols[:, 4:8], func=ACT.Exp), [cols], [cols])
                        A(lambda e: e.activation(out=cols[:, 12:16], in_=gt[:, 8 + 4 * d:12 + 4 * d], func=ACT.Exp),
                          [gt], [cols])
                        V(lambda e: e.tensor_tensor(out=cols[:, 12:16], in0=cols[:, 12:16], in1=gt[:, 4 * d:4 * d + 4],
                                                    op=ALU.mult), [cols, gt], [cols])
                        V(lambda e: e.tensor_tensor(out=cols[:, 16:20], in0=cols[:, 4:8], in1=cols[:, 0:4], op=ALU.add),
                          [cols], [cols])
                        A(lambda e: e.activation(out=cols[:, 16:20], in_=cols[:, 16:20], func=ACT.Exp), [cols], [cols])
                        V(lambda e: e.tensor_tensor(out=v4(tmpm), in0=pGB[:, :], in1=v4(cm[d]), op=ALU.add), [pGB, cm[d]], [tmpm])
                        for h in range(4):
                            A(lambda e, h=h: e.activation(out=DTi[:, h, :], in_=tmpm[:, h, :], func=ACT.Exp,
                                                          bias=cols[:, h:h + 1]), [tmpm, cols], [DTi])
                        A(lambda e: e.activation(out=v4(EG), in_=pGB[:, :], func=ACT.Exp), [pGB], [EG])
                        V(lambda e: e.tensor_tensor(out=v4(W1), in0=v4(DTi), in1=v4(offd), op=ALU.mult), [DTi, offd], [W1])
                        V(lambda e: e.tensor_tensor(out=v4(W1), in0=pBB[:, :], in1=v4(W1), op=ALU.mult), [pBB, W1], [W1])
                        pKK = nps()
                        for h in range(4):
                            PE(lambda e, h=h: e.matmul(out=pKK[:, h * 128:(h + 1) * 128], lhsT=qk[:, 4 + h, :],
                                                       rhs=qk[:, 4 + h, :], start=True, stop=True), [qk], [pKK])
                        V(lambda e: e.scalar_tensor_tensor(out=v4(Nf), in0=pKK[:, :], scalar=-1.0, in1=v4(W1),
                                                           op0=ALU.mult, op1=ALU.mult), [pKK, W1], [Nf])
                        pp = nps()
                        for h in range(4):
                            PE(lambda e, h=h, pp=pp: e.transpose(out=pp[:, h * 128:(h + 1) * 128], in_=Nf[:, h, :],
                                                                 identity=ident_f[:]), [Nf, ident_f], [pp])
                        A(lambda e, pp=pp: e.copy(out=v4(NT), in_=pp[:, :]), [pp], [NT])
                        Pc, Pn = Pm
                        Tc, Tn = PT
                        V(lambda e, Pc=Pc: e.tensor_tensor(out=v4(Pc), in0=v4(Nf), in1=v4(lvm[0]), op=ALU.mult), [Nf, lvm[0]], [Pc])
                        V(lambda e, Tc=Tc: e.tensor_tensor(out=v4(Tc), in0=v4(NT), in1=v4(lvm[0]), op=ALU.mult), [NT, lvm[0]], [Tc])
                        V(lambda e, Pc=Pc: e.tensor_tensor(out=v4(Xf), in0=v4(Pc), in1=v4(idf4), op=ALU.add), [Pc, idf4], [Xf])
                        for s_ in range(1, 4):
                            if s_ < 3:
                                pP = nps()
                                for h in range(4):
                                    PE(lambda e, h=h, pP=pP, Tc=Tc, Pc=Pc: e.matmul(
                                        out=pP[:, h * 128:(h + 1) * 128], lhsT=Tc[:, h, :], rhs=Pc[:, h, :],
                                        start=True, stop=True), [Tc, Pc], [pP])
                            pT = nps()
                            for h in range(4):
                                PE(lambda e, h=h, pT=pT, Tc=Tc, Pc=Pc: e.matmul(
                                    out=pT[:, h * 128:(h + 1) * 128], lhsT=Pc[:, h, :], rhs=Tc[:, h, :],
                                    start=True, stop=True), [Tc, Pc], [pT])
                            if s_ < 3:
                                A(lambda e, pP=pP, Pn=Pn: e.copy(out=v4(Pn), in_=pP[:, :]), [pP], [Pn])
                            V(lambda e, pT=pT, Tn=Tn: e.tensor_copy(out=v4(Tn), in_=pT[:, :]), [pT], [Tn])
                            pX = nps()
                            for h in range(4):
                                PE(lambda e, h=h, pX=pX, Tn=Tn: e.matmul(
                                    out=pX[:, h * 128:(h + 1) * 128], lhsT=Tn[:, h, :], rhs=Xf[:, h, :],
                                    start=True, stop=True), [Tn, Xf], [pX])
                            V(lambda e, pX=pX: e.tensor_tensor(out=v4(Xf), in0=pX[:, :], in1=v4(Xf), op=ALU.add), [pX, Xf], [Xf])
                            Pc, Pn, Tc, Tn = Pn, Pc, Tn, Tc
                        for lv in range(1, 4):
                            V(lambda e, lv=lv: e.tensor_tensor(out=v4(T.NlTb), in0=v4(NT), in1=v4(lvm[lv]), op=ALU.mult),
                              [NT, lvm[lv]], [T.NlTb])
                            A(lambda e: e.copy(out=Xb[:], in_=Xf[:]), [Xf], [Xb])
                            pp = nps()
                            ppb = pp.t[:].bitcast(BF16)
                            for h in range(4):
                                PE(lambda e, h=h, ppb=ppb: e.transpose(out=ppb[:, h * 128:(h + 1) * 128], in_=Xb[:, h, :],
                                                                       identity=ident_b[:]), [Xb, ident_b], [pp])
                            A(lambda e, ppb=ppb: e.copy(out=v4(T.XTb), in_=ppb[:, 0:512]), [pp], [T.XTb])
                            pZ = nps()
                            for h in range(4):
                                PE(lambda e, h=h, pZ=pZ: e.matmul(out=pZ[:, h * 128:(h + 1) * 128], lhsT=T.NlTb[:, h, :],
                                                                  rhs=Xb[:, h, :], start=True, stop=True), [T.NlTb, Xb], [pZ])
                            A(lambda e, pZ=pZ: e.copy(out=v4(T.Zb), in_=pZ[:, :]), [pZ], [T.Zb])
                            pY = nps()
                            for h in range(4):
                                PE(lambda e, h=h, pY=pY: e.matmul(out=pY[:, h * 128:(h + 1) * 128], lhsT=T.XTb[:, h, :],
                                                                  rhs=T.Zb[:, h, :], start=True, stop=True), [T.XTb, T.Zb], [pY])
                            V(lambda e, pY=pY: e.tensor_tensor(out=v4(Xf), in0=v4(Xf), in1=pY[:, :], op=ALU.add),
                              [pY, Xf], [Xf])
                        A(lambda e: e.copy(out=Xb[:], in_=Xf[:]), [Xf], [Xb])
                        X = Xb
                        V(lambda e: e.tensor_tensor(out=ktil[:], in0=kv[:, 0:4, :],
                                                    in1=cols[:, 12:16].unsqueeze(2).to_broadcast([128, 4, 128]), op=ALU.mult),
                          [kv, cols], [ktil])
                        V(lambda e: e.tensor_tensor(out=khat[:], in0=kv[:, 0:4, :],
                                                    in1=cols[:, 16:20].unsqueeze(2).to_broadcast([128, 4, 128]), op=ALU.mult),
                          [kv, cols], [khat])
                        V(lambda e: e.tensor_tensor(out=bv[:], in0=kv[:, 4:8, :],
                                                    in1=gt[:, 4 * d:4 * d + 4].unsqueeze(2).to_broadcast([128, 4, 128]), op=ALU.mult),
                          [kv, gt], [bv])
                        pW = nps()
                        for h in range(4):
                            PE(lambda e, h=h, X=X: e.matmul(out=pW[:, h * 128:(h + 1) * 128], lhsT=ktil[:, h, :], rhs=X[:, h, :],
                                                            start=True, stop=True), [ktil, X], [pW])
                        V(lambda e: e.tensor_scalar(out=v4(nwT), in0=pW[:, :], scalar1=-1.0, scalar2=None, op0=ALU.mult), [pW], [nwT])
                        pQK = nps()
                        for h in range(4):
                            PE(lambda e, h=h: e.matmul(out=pQK[:, h * 128:(h + 1) * 128], lhsT=qk[:, 4 + h, :], rhs=qk[:, h, :],
                                                       start=True, stop=True), [qk], [pQK])
                        V(lambda e: e.tensor_tensor(out=v4(AT), in0=pQK[:, :], in1=v4(DTi), op=ALU.mult), [pQK, DTi], [AT])
                        V(lambda e: e.tensor_tensor(out=qs[:], in0=qk[:, 0:4, :], in1=EG[:], op=ALU.mult), [qk, EG], [qs])
                        while progB[d] != pos:
                            k._yield(force=True)
                        if first_:
                            if si < NPS:
                                V(lambda e: e.memset(S[:], 0.0), [], [S])
                            else:
                                k.dma("sp", S[:], s0.t[layer, d].rearrange("h p e -> p h e"), reads=[s0], writes=[S])
                            A(lambda e: e.copy(out=Sb[:], in_=S[:]), [S], [Sb])
                        pV = nps()
                        for h in range(4):
                            PE(lambda e, h=h, X=X: e.matmul(out=pV[:, h * 128:(h + 1) * 128], lhsT=X[:, h, :], rhs=bv[:, h, :],
                                                            start=True, stop=False), [X, bv], [pV])
                            PE(lambda e, h=h: e.matmul(out=pV[:, h * 128:(h + 1) * 128], lhsT=nwT[:, h, :], rhs=Sb[:, h, :],
                                                       start=False, stop=True), [nwT, Sb], [pV])
                        A(lambda e: e.copy(out=v4(vn), in_=pV[:, :]), [pV], [vn])
                        pO = nps()
                        for h in range(4):
                            PE(lambda e, h=h: e.matmul(out=pO[:, h * 128:(h + 1) * 128], lhsT=Sb[:, h, :], rhs=qs[:, h, :],
                                                       start=True, stop=False), [Sb, qs], [pO])
                            PE(lambda e, h=h: e.matmul(out=pO[:, h * 128:(h + 1) * 128], lhsT=vn[:, h, :], rhs=AT[:, h, :],
                                                       start=False, stop=True), [vn, AT], [pO])
                        A(lambda e: e.copy(out=v4(oT), in_=pO[:, :]), [pO], [oT])
                        pS = nps()
                        for h in range(4):
                            PE(lambda e, h=h: e.matmul(out=pS[:, h * 128:(h + 1) * 128], lhsT=khat[:, h, :], rhs=vn[:, h, :],
                                                       start=True, stop=True), [khat, vn], [pS])
                        V(lambda e: e.tensor_tensor(out=S[:], in0=S[:], in1=cols[:, 8:12].unsqueeze(2).to_broadcast([128, 4, 128]),
                                                    op=ALU.mult), [S, cols], [S])
                        V(lambda e: e.tensor_tensor(out=v4(S), in0=pS[:, :], in1=v4(S), op=ALU.add), [pS, S], [S])
                        A(lambda e: e.copy(out=Sb[:], in_=S[:]), [S], [Sb])
                        k.dma("sp", odT[d].t[:, t0:t0 + 128].rearrange("(h p) t -> p h t", p=128), oT[:], reads=[oT], writes=[odT[d]])
                        if last_ and si < NPS:
                            k.dma("sp", ngdn_out.t[si, layer, d].rearrange("h p e -> p h e"), S[:], reads=[S], writes=[ngdn_out])
                        progB[d] = pos + 1

            k.interleave([lambda: chain(0, 0), lambda: chain(1, 0), lambda: chain(0, 1), lambda: chain(1, 1)])
            k.barrier()
            esm2.close()
            k.es = esg

            of = [k.sb("of%d" % i, [128, 4, 128], F32) for i in range(2)]
            o2 = [k.sb("o2_%d" % i, [128, 4, 128], F32) for i in range(2)]
            zt = [k.sb("zt%d" % i, [128, 4, 128], F32) for i in range(2)]
            osq = k.sb("osq", [128, 4, 128], BF16)
            ob = [k.sb("ob%d" % i, [128, 4, 128], BF16) for i in range(2)]
            for ci in range(NTOK // 128):
                t0 = ci * 128
                a_, b_, z_, r_ = of[ci % 2], o2[ci % 2], zt[ci % 2], ob[ci % 2]
                k.dma("sp", a_[:], odT[0].t[:, t0:t0 + 128].rearrange("(h p) t -> p h t", p=128), reads=[odT[0]], writes=[a_])
                k.dma("sp", b_[:], odT[1].t[:, t0:t0 + 128].rearrange("(h p) t -> p h t", p=128), reads=[odT[1]], writes=[b_])
                k.dma("sp", z_[:], projT.t[2560:3072, t0:t0 + 128].rearrange("(h p) t -> p h t", p=128), reads=[projT], writes=[z_])
                V(lambda e, a_=a_, b_=b_: e.tensor_tensor(out=b_[:], in0=b_[:], in1=a_[:], op=ALU.add), [a_, b_], [b_])
                V(lambda e, b_=b_: e.tensor_tensor(out=osq[:], in0=b_[:], in1=b_[:], op=ALU.mult), [b_], [osq])
                pN = nps()
                PE(lambda e, pN=pN: e.matmul(out=pN[:, :], lhsT=ones_b[:], rhs=v4(osq), start=True, stop=True), [ones_b, osq], [pN])
                A(lambda e, pN=pN, a_=a_: e.activation(out=v4(a_), in_=pN[:, :], func=ACT.Ln, scale=1.0 / 128.0, bias=epsc[:, 0:1]),
                  [pN, epsc], [a_])
                A(lambda e, a_=a_: e.activation(out=v4(a_), in_=v4(a_), func=ACT.Exp, scale=-0.5), [a_], [a_])
                A(lambda e, z_=z_: e.activation(out=z_[:], in_=z_[:], func=ACT.Silu), [z_], [z_])
                V(lambda e, a_=a_, b_=b_: e.tensor_tensor(out=b_[:], in0=b_[:], in1=a_[:], op=ALU.mult), [a_, b_], [b_])
                V(lambda e, b_=b_, z_=z_, r_=r_: e.scalar_tensor_tensor(out=r_[:], in0=b_[:], scalar=gnw[:, 0:1], in1=z_[:],
                                                                     op0=ALU.mult, op1=ALU.mult), [b_, gnw, z_], [r_])
                k.dma("sp", mixT.t[512:1024, t0:t0 + 128].rearrange("(h p) t -> p h t", p=128), r_[:], reads=[r_], writes=[mixT])
            k.barrier()
        k.es = es

    TB = 32
    NBLK = NTOK // TB

    def s5_phase(layer):
        with contextlib.ExitStack() as ess:
            k.es = ess
            uT = k.sb("uT", [128, 4, NTOK], BF16)
            for c in range(4):
                for hh in range(2):
                    k.dma("pool", uT[:, c, hh * 2560:(hh + 1) * 2560], projT.t[c * 128:(c + 1) * 128, hh * 2560:(hh + 1) * 2560],
                          reads=[projT], writes=[uT])
            Hsb = [k.sb("Hsb%d" % d, [128, 32, NBLK], BF16) for d in range(2)]
            swapm = k.sb("swapm", [128, 128], F32)
            k.dma("sp", swapm[:], swap_in.t[:, :], reads=[swap_in], writes=[swapm])

            with contextlib.ExitStack() as est:
                k.es = est
                def load_T(src, name):
                    raw_ = k.sb(name + "_raw", [64, 128], F32)
                    v = src.t[layer].rearrange("d g n -> (d g) n")
                    k.dma("sp", raw_[:, 0:64], v, reads=[src], writes=[raw_])
                    k.dma("sp", raw_[:, 64:128], v, reads=[src], writes=[raw_])
                    pp = nps()
                    PE(lambda e: e.transpose(out=pp[:, 0:64], in_=raw_[:], identity=ident_f[0:64, 0:64]), [raw_, ident_f], [pp])
                    t_ = k.sb(name, [128, 64], F32)
                    V(lambda e: e.tensor_copy(out=t_[:], in_=pp[:, 0:64]), [pp], [t_])
                    return t_
                LR = load_T(lam_re, "LR")
                LI = load_T(lam_im, "LI")
                DT = k.sb("DT", [128, 64], F32)
                k.dma("sp", DT[:].rearrange("p (d g) -> p d g", d=2),
                      log_dt.t[layer:layer + 1, :, :].to_broadcast([128, 2, 32]), reads=[log_dt], writes=[DT])
                A(lambda e: e.activation(out=DT[:].rearrange("p (d g) -> p d g", d=2), in_=DT[:].rearrange("p (d g) -> p d g", d=2), func=ACT.Exp), [DT], [DT])
                V(lambda e: e.tensor_scalar(out=LR[:], in0=LR[:], scalar1=-1e-4, scalar2=None, op0=ALU.min), [LR], [LR])
                mag = k.sb("mag", [128, 64], F32)
                th = k.sb("th", [128, 2, 64], F32)
                cs = k.sb("cs", [128, 2, 64], F32)
                V(lambda e: e.tensor_tensor(out=mag[:], in0=LR[:], in1=DT[:], op=ALU.mult), [LR, DT], [mag])
                A(lambda e: e.activation(out=mag[:], in_=mag[:], func=ACT.Exp), [mag], [mag])
                V(lambda e: e.tensor_tensor(out=th[:, 0, :], in0=LI[:], in1=DT[:], op=ALU.mult), [LI, DT], [th])
                V(lambda e: e.tensor_scalar(out=th[:, 1, :], in0=th[:, 0, :], scalar1=math.pi / 2, scalar2=None, op0=ALU.add), [th], [th])
                k.barrier()
                sin_reduced(cs[:], th[:], [128, 2, 64], "s5")
                k.barrier()
                QR = k.sb("QR", [128, 64, 33], F32)
                QI = k.sb("QI", [128, 64, 33], F32)
                V(lambda e: e.memset(QR[:, :, 0:1], 1.0), [], [QR])
                V(lambda e: e.memset(QI[:, :, 0:1], 0.0), [], [QI])
                V(lambda e: e.tensor_tensor(out=QR[:, :, 1], in0=mag[:], in1=cs[:, 1, :], op=ALU.mult), [mag, cs], [QR])
                V(lambda e: e.tensor_tensor(out=QI[:, :, 1], in0=mag[:], in1=cs[:, 0, :], op=ALU.mult), [mag, cs], [QI])
                SR = k.sb("SR", [128, 64], F32)
                SI = k.sb("SI", [128, 64], F32)
                S2 = k.sb("S2", [128, 64], F32)
                T1 = k.sb("T1", [128, 64, 16], F32)
                T2 = k.sb("T2", [128, 64, 16], F32)
                V(lambda e: e.tensor_copy(out=SR[:], in_=QR[:, :, 1]), [QR], [SR])
                V(lambda e: e.tensor_copy(out=SI[:], in_=QI[:, :, 1]), [QI], [SI])

                def cmul_bc(oR, oI, aR, aI, bR, bI, w):
                    bRb = bR.unsqueeze(2).to_broadcast([128, 64, w])
                    bIb = bI.unsqueeze(2).to_broadcast([128, 64, w])
                    V(lambda e: e.tensor_tensor(out=T1[:, :, 0:w], in0=aR, in1=bRb, op=ALU.mult), [QR, QI, SR, SI, GRt, GIt, FRt, FIt], [T1])
                    V(lambda e: e.tensor_tensor(out=T2[:, :, 0:w], in0=aI, in1=bIb, op=ALU.mult), [QR, QI, SR, SI, GRt, GIt, FRt, FIt], [T2])
                    V(lambda e: e.tensor_tensor(out=oR, in0=T1[:, :, 0:w], in1=T2[:, :, 0:w], op=ALU.subtract), [T1, T2], [QR, GRt])
                    V(lambda e: e.tensor_tensor(out=T1[:, :, 0:w], in0=aR, in1=bIb, op=ALU.mult), [QR, QI, SR, SI, GRt, GIt, FRt, FIt], [T1])
                    V(lambda e: e.tensor_tensor(out=T2[:, :, 0:w], in0=aI, in1=bRb, op=ALU.mult), [QR, QI, SR, SI, GRt, GIt, FRt, FIt], [T2])
                    V(lambda e: e.tensor_tensor(out=oI, in0=T1[:, :, 0:w], in1=T2[:, :, 0:w], op=ALU.add), [T1, T2], [QI, GIt])

                GRt = k.sb("GR", [128, 64, 32], F32)
                GIt = k.sb("GI", [128, 64, 32], F32)
                FRt = k.sb("FR", [128, 64], F32)
                FIt = k.sb("FI", [128, 64], F32)
                w = 2
                while w <= 32:
                    V(lambda e: e.tensor_tensor(out=S2[:], in0=SR[:], in1=SI[:], op=ALU.mult), [SR, SI], [S2])
                    V(lambda e: e.tensor_tensor(out=SR[:], in0=SR[:], in1=SR[:], op=ALU.mult), [SR], [SR])
                    V(lambda e: e.tensor_tensor(out=SI[:], in0=SI[:], in1=SI[:], op=ALU.mult), [SI], [SI])
                    V(lambda e: e.tensor_tensor(out=SR[:], in0=SR[:], in1=SI[:], op=ALU.subtract), [SR, SI], [SR])
                    V(lambda e: e.tensor_scalar(out=SI[:], in0=S2[:], scalar1=2.0, scalar2=None, op0=ALU.mult), [S2], [SI])
                    if w < 32:
                        cmul_bc(QR[:, :, w:2 * w], QI[:, :, w:2 * w], QR[:, :, 0:w], QI[:, :, 0:w], SR[:], SI[:], w)
                    else:
                        V(lambda e: e.tensor_copy(out=QR[:, :, 32], in_=SR[:]), [SR], [QR])
                        V(lambda e: e.tensor_copy(out=QI[:, :, 32], in_=SI[:]), [SI], [QI])
                    w *= 2
                nr = k.sb("nr", [128, 64], F32)
                den = k.sb("den", [128, 64], F32)
                V(lambda e: e.tensor_scalar(out=nr[:], in0=QR[:, :, 1], scalar1=-1.0, scalar2=None, op0=ALU.add), [QR], [nr])
                V(lambda e: e.tensor_tensor(out=den[:], in0=LR[:], in1=LR[:], op=ALU.mult), [LR], [den])
                V(lambda e: e.tensor_tensor(out=S2[:], in0=LI[:], in1=LI[:], op=ALU.mult), [LI], [S2])
                V(lambda e: e.tensor_tensor(out=den[:], in0=den[:], in1=S2[:], op=ALU.add), [den, S2], [den])
                V(lambda e: e.reciprocal(out=den[:], in_=den[:]), [den], [den])
                V(lambda e: e.tensor_tensor(out=FRt[:], in0=nr[:], in1=LR[:], op=ALU.mult), [nr, LR], [FRt])
                V(lambda e: e.tensor_tensor(out=S2[:], in0=QI[:, :, 1], in1=LI[:], op=ALU.mult), [QI, LI], [S2])
                V(lambda e: e.tensor_tensor(out=FRt[:], in0=FRt[:], in1=S2[:], op=ALU.add), [FRt, S2], [FRt])
                V(lambda e: e.tensor_tensor(out=FRt[:], in0=FRt[:], in1=den[:], op=ALU.mult), [FRt, den], [FRt])
                V(lambda e: e.tensor_tensor(out=FIt[:], in0=QI[:, :, 1], in1=LR[:], op=ALU.mult), [QI, LR], [FIt])
                V(lambda e: e.tensor_tensor(out=S2[:], in0=nr[:], in1=LI[:], op=ALU.mult), [nr, LI], [S2])
                V(lambda e: e.tensor_tensor(out=FIt[:], in0=FIt[:], in1=S2[:], op=ALU.subtract), [FIt, S2], [FIt])
                V(lambda e: e.tensor_tensor(out=FIt[:], in0=FIt[:], in1=den[:], op=ALU.mult), [FIt, den], [FIt])
                for h2 in range(2):
                    ks = slice(h2 * 16, (h2 + 1) * 16)
                    cmul_bc(GRt[:, :, ks], GIt[:, :, ks], QR[:, :, ks], QI[:, :, ks], FRt[:], FIt[:], 16)
                AR = k.sb("ARt", [128, 64], F32)
                AIs = k.sb("AIs", [128, 64], F32)
                V(lambda e: e.tensor_copy(out=AR[:], in_=QR[:, :, 32]), [QR], [AR])
                V(lambda e: e.tensor_scalar(out=AIs[0:64, :], in0=QI[0:64, :, 32], scalar1=-1.0, scalar2=None, op0=ALU.mult), [QI], [AIs])
                V(lambda e: e.tensor_copy(out=AIs[64:128, :], in_=QI[64:128, :, 32]), [QI], [AIs])
                Ba = k.sb("Ba", [128, 64, 16], F32)
                Bb = k.sb("Bb", [128, 64, 16], F32)
                bre_v = b_re.t[layer].rearrange("d g n q -> n (d g) q")
                bim_v = b_im.t[layer].rearrange("d g n q -> n (d g) q")
                k.dma("sp", Ba[0:64, :, :], bre_v, reads=[b_re], writes=[Ba])
                k.dma("sp", Ba[64:128, :, :], bim_v, reads=[b_im], writes=[Ba])
                k.dma("sp", Bb[0:64, :, :], bim_v, reads=[b_im], writes=[Bb])
                k.dma("sp", Bb[64:128, :, :], bre_v, reads=[b_re], writes=[Bb])
                V(lambda e: e.tensor_scalar(out=Bb[0:64, :, :], in0=Bb[0:64, :, :], scalar1=-1.0, scalar2=None, op0=ALU.mult), [Bb], [Bb])
                Ca = k.sb("Ca", [128, 64, 16], BF16)
                Cb = k.sb("Cb", [128, 64, 16], BF16)
                CaB = Ca
                esc_ = contextlib.ExitStack()
                k.es = esc_
                CaT = k.sb("CaT", [16, 16, 128], F32)
                CbT = k.sb("CbT", [16, 16, 128], F32)
                cre_v = c_re.t[layer].rearrange("d g q n -> q (d g) n")
                cim_v = c_im.t[layer].rearrange("d g q n -> q (d g) n")
                for q4 in range(4):
                    qs_ = slice(q4 * 16, (q4 + 1) * 16)
                    k.dma("sp", CaT[:, :, 0:64], cre_v[:, qs_, :], reads=[c_re], writes=[CaT])
                    k.dma("sp", CaT[:, :, 64:128], cim_v[:, qs_, :], reads=[c_im], writes=[CaT])
                    k.dma("sp", CbT[:, :, 0:64], cim_v[:, qs_, :], reads=[c_im], writes=[CbT])
                    k.dma("sp", CbT[:, :, 64:128], cre_v[:, qs_, :], reads=[c_re], writes=[CbT])
                    V(lambda e: e.tensor_scalar(out=CaT[:, :, 64:128], in0=CaT[:, :, 64:128], scalar1=-1.0, scalar2=None,
                                                op0=ALU.mult), [CaT], [CaT])
                    V(lambda e: e.tensor_scalar(out=CbT[:], in0=CbT[:], scalar1=-1.0, scalar2=None, op0=ALU.mult), [CbT], [CbT])
                    for (srcT, dstC) in ((CaT, Ca), (CbT, Cb)):
                        pp = nps()
                        for i in range(16):
                            PE(lambda e, pp=pp, i=i, srcT=srcT: e.transpose(
                                out=pp[:, i * 16:(i + 1) * 16], in_=srcT[:, i, :], identity=ident_f[0:16, 0:16]),
                                [srcT, ident_f], [pp])
                        V(lambda e, pp=pp, qs_=qs_, dstC=dstC: e.tensor_copy(
                            out=dstC[:, qs_, :].rearrange("p a b -> p (a b)"), in_=pp[:, 0:256]), [pp], [dstC])
                k.barrier()
                esc_.close()
                k.es = est

                class TS5:
                    pass

                def mk_s5(d_):
                    C_ = TS5()
                    tag = "_s%d" % d_
                    C_.Xp = k.sb("Xp" + tag, [128, 32, 128], BF16)
                    V(lambda e: e.memset(C_.Xp[:], 0.0), [], [C_.Xp])
                    C_.XT1 = Buf(T1.t[:, 32 * d_:32 * d_ + 32, :], "XT1" + tag)
                    C_.XT2 = Buf(T2.t[:, 32 * d_:32 * d_ + 32, :], "XT2" + tag)
                    C_.WB = [k.sb("WB" + tag, [128, 32, 128], BF16)] * 2
                    C_.WCg = [k.sb("WCg%d" % i + tag, [128, 32, 32], BF16) for i in range(2)]
                    for i in range(2):
                        V(lambda e, i=i: e.memset(C_.WCg[i][:], 0.0), [], [C_.WCg[i]])
                    C_.Kt = k.sb("Kt" + tag, [128, 32, 128], BF16)
                    C_.Et = k.sb("Et" + tag, [128, 32, NBLK], BF16)
                    C_.cur = k.sb("cur" + tag, [128, 32, 4], F32)
                    C_.curs = k.sb("curs" + tag, [128, 32, 1], F32)
                    C_.tA = k.sb("tA" + tag, [128, 32, 4], F32)
                    C_.tB = k.sb("tB" + tag, [128, 32, 4], F32)
                    C_.hin = k.sb("hin" + tag, [32, 128], F32)
                    C_.fin = k.sb("fin" + tag, [32, 128], F32)
                    return C_

                CS5 = [mk_s5(0), mk_s5(1)]
                psbs5 = [psb, psb2]
                def s5_chain(d):
                    C_ = CS5[d]
                    Xp, XT1, XT2, WB, WCg, Kt, Et, cur, curs, tA, tB, hin, fin = (C_.Xp, C_.XT1, C_.XT2, C_.WB, C_.WCg, C_.Kt, C_.Et,
                                                                                 C_.cur, C_.curs, C_.tA, C_.tB, C_.hin, C_.fin)
                    psb_ = psbs5[d]
                    bk5 = psf[3 * d:3 * d + 3]
                    bi5 = [0]

                    def nps():
                        p = bk5[bi5[0] % 3]
                        bi5[0] += 1
                        return p

                    prev_slot = [None]
                    for g in range(32):
                        dg = d * 32 + g
                        c = g // 8
                        slot = g % 8
                        pr = slot // 2
                        if g % 8 == 0:
                            V(lambda e: e.memset(Kt[:], 0.0), [], [Kt])
                        if prev_slot[0] is not None:
                            ps_ = prev_slot[0]
                            V(lambda e, ps_=ps_: e.memset(Xp[:, :, ps_ * 16:(ps_ + 1) * 16], 0.0), [], [Xp])
                        prev_slot[0] = slot
                        V(lambda e, dg=dg: e.tensor_tensor(
                            out=XT1[:], in0=GRt[:, dg, :].unsqueeze(2).to_broadcast([128, 32, 16]),
                            in1=Ba[:, dg, :].unsqueeze(1).to_broadcast([128, 32, 16]), op=ALU.mult), [GRt, Ba], [XT1])
                        V(lambda e, dg=dg: e.tensor_tensor(
                            out=XT2[:], in0=GIt[:, dg, :].unsqueeze(2).to_broadcast([128, 32, 16]),
                            in1=Bb[:, dg, :].unsqueeze(1).to_broadcast([128, 32, 16]), op=ALU.mult), [GIt, Bb], [XT2])
                        V(lambda e, slot=slot: e.tensor_tensor(out=Xp[:, :, slot * 16:(slot + 1) * 16], in0=XT1[:], in1=XT2[:],
                                                               op=ALU.add), [XT1, XT2], [Xp])
                        pk = nps()
                        for tau in range(32):
                            PE(lambda e, tau=tau, pk=pk, dg=dg: e.matmul(out=pk[:, tau * 16:(tau + 1) * 16], lhsT=Xp[:, tau, :],
                                                                         rhs=CaB[:, dg, :], start=True, stop=True), [Xp, CaB], [pk])
                        A(lambda e, pk=pk, pr=pr, slot=slot: e.copy(
                            out=Kt[32 * pr:32 * pr + 32, :, slot * 16:(slot + 1) * 16],
                            in_=pk[32 * pr:32 * pr + 32, :].rearrange("p (t q) -> p t q", q=16)), [pk], [Kt])
                        if g % 8 == 7:
                            k.dma("sp", ktab.t[d, c, :, :], Kt[:].rearrange("p a b -> p (a b)"), reads=[Kt], writes=[ktab])
                        wc = WCg[g % 2]
                        sl2 = g % 2
                        V(lambda e, dg=dg: e.tensor_tensor(
                            out=XT1[:], in0=QR[:, dg, 1:33].unsqueeze(2).to_broadcast([128, 32, 16]),
                            in1=Ca[:, dg, :].unsqueeze(1).to_broadcast([128, 32, 16]), op=ALU.mult), [QR, Ca], [XT1])
                        V(lambda e, dg=dg: e.tensor_tensor(
                            out=XT2[:], in0=QI[:, dg, 1:33].unsqueeze(2).to_broadcast([128, 32, 16]),
                            in1=Cb[:, dg, :].unsqueeze(1).to_broadcast([128, 32, 16]), op=ALU.mult), [QI, Cb], [XT2])
                        V(lambda e, wc=wc, sl2=sl2: e.tensor_tensor(out=wc[:, :, sl2 * 16:(sl2 + 1) * 16], in0=XT1[:], in1=XT2[:],
                                                                   op=ALU.add), [XT1, XT2], [wc])
                        k.dma("sp", wctab.t[dg, :, :], wc[:].rearrange("p a b -> p (a b)"), reads=[wc], writes=[wctab])
                        wb = WB[g % 2]
                        for q8 in range(4):
                            for kk in range(8):
                                kx = q8 * 8 + kk
                                kwt = dict(tile_position=(0, 96)) if pr == 3 else {}
                                PE(lambda e, kk=kk, kx=kx, pr=pr, kwt=kwt: e.transpose(
                                    out=psb_[32 * pr:32 * pr + 32, kk * 128:(kk + 1) * 128], in_=Xp[:, kx, 32 * pr:32 * pr + 32],
                                    identity=ident_b[:], **kwt), [Xp, ident_b], [psb_])
                            V(lambda e, q8=q8, pr=pr, wb=wb: e.tensor_copy(
                                out=wb[32 * pr:32 * pr + 32, q8 * 8:(q8 + 1) * 8, :].rearrange("p a b -> p (a b)"),
                                in_=psb_[32 * pr:32 * pr + 32, :]), [psb_], [wb])
                        pe_ = nps()
                        for rho in range(32):
                            kx = (31 - rho) if d == 0 else rho
                            kw = dict(tile_position=(96, 0)) if pr == 3 else {}
                            PE(lambda e, rho=rho, kx=kx, pr=pr, c=c, wb=wb, pe_=pe_, kw=kw: e.matmul(
                                out=pe_[:, 0:NBLK], lhsT=wb[32 * pr:32 * pr + 32, kx, :],
                                rhs=uT[32 * pr:32 * pr + 32, c, :].rearrange("p (b r) -> p b r", r=TB)[:, :, rho],
                                start=(rho == 0), stop=(rho == 31), **kw), [wb, uT], [pe_])
                        A(lambda e, g=g, pe_=pe_: e.copy(out=Et[:, g, :], in_=pe_[:, 0:NBLK]), [pe_], [Et])
                    ARd = AR[:, d * 32:(d + 1) * 32]
                    AId = AIs[:, d * 32:(d + 1) * 32]

                    def rec_step(cu, w, bsel_E, bsel_H):
                        A(lambda e: e.copy(out=bsel_H, in_=cu), [cur, curs], [Hsb[d]])
                        psw = nps()
                        PE(lambda e: e.matmul(out=psw[:, 0:32 * w], lhsT=swapm[:], rhs=cu.rearrange("p a b -> p (a b)"),
                                              start=True, stop=True), [swapm, cur, curs], [psw])
                        V(lambda e: e.tensor_tensor(out=tA[:, :, 0:w], in0=cu, in1=ARd.unsqueeze(2).to_broadcast([128, 32, w]),
                                                    op=ALU.mult), [cur, curs, AR], [tA])
                        V(lambda e: e.tensor_tensor(out=tB[:, :, 0:w], in0=psw[:, 0:32 * w].rearrange("p (a b) -> p a b", b=w),
                                                    in1=AId.unsqueeze(2).to_broadcast([128, 32, w]), op=ALU.mult), [psw, AIs], [tB])
                        V(lambda e: e.tensor_tensor(out=tA[:, :, 0:w], in0=tA[:, :, 0:w], in1=tB[:, :, 0:w], op=ALU.add), [tA, tB], [tA])
                        V(lambda e: e.tensor_tensor(out=cu, in0=tA[:, :, 0:w], in1=bsel_E, op=ALU.add), [tA, Et], [cur, curs])

                    V(lambda e: e.memset(cur[:], 0.0), [], [cur])
                    nbp = LP // TB
                    for s_ in range(nbp):
                        x = s_ if d == 0 else nbp - 1 - s_
                        selE = Et[:, :, 0:NPS * nbp].rearrange("p g (s x) -> p g s x", x=nbp)[:, :, :, x]
                        selH = Hsb[d][:, :, 0:NPS * nbp].rearrange("p g (s x) -> p g s x", x=nbp)[:, :, :, x]
                        rec_step(cur[:], NPS, selE, selH)
                    for si in range(NPS):
                        pf = nps()
                        PE(lambda e, si=si, pf=pf: e.transpose(out=pf[0:32, 0:128], in_=cur[:, :, si], identity=ident_f[:]),
                           [cur, ident_f], [pf])
                        V(lambda e, pf=pf: e.tensor_copy(out=fin[:], in_=pf[0:32, 0:128]), [pf], [fin])
                        k.dma("sp", nre_out.t[si, layer, d, :, :], fin[:, 0:64], reads=[fin], writes=[nre_out])
                        k.dma("sp", nim_out.t[si, layer, d, :, :], fin[:, 64:128], reads=[fin], writes=[nim_out])
                    k.dma("sp", hin[:, 0:64], h0re.t[layer, d, :, :], reads=[h0re], writes=[hin])
                    k.dma("sp", hin[:, 64:128], h0im.t[layer, d, :, :], reads=[h0im], writes=[hin])
                    ph = nps()
                    PE(lambda e: e.transpose(out=ph[:, 0:32], in_=hin[:], identity=ident_f[0:32, 0:32]), [hin, ident_f], [ph])
                    V(lambda e: e.tensor_copy(out=curs[:].rearrange("p a b -> p (a b)"), in_=ph[:, 0:32]), [ph], [curs])
                    b0 = NPS * nbp
                    nbs = LS // TB
                    for s_ in range(nbs):
                        b = b0 + (s_ if d == 0 else nbs - 1 - s_)
                        rec_step(curs[:], 1, Et[:, :, b:b + 1], Hsb[d][:, :, b:b + 1])
                k.interleave([lambda: s5_chain(0), lambda: s5_chain(1)])
                k.barrier()
            k.es = ess

            WCc = k.sb("WCc", [128, 16, 1024], BF16)
            Kc = k.sb("Kc", [128, 2, 32, 128], BF16)
            yfar = k.sb("yfar", [128, NBLK, TB], F32)
            dcol = k.sb("dcol", [128, 1], F32)
            dgm = k.sb("dgm", [128, 128], BF16)
            zt5 = k.sb("zt5", [128, 512], F32)
            ys = k.sb("ys", [128, 512], F32)
            yo = k.sb("yo", [128, 512], BF16)
            for c in range(4):
                for d in range(2):
                    k.dma("sp", WCc[:, d * 8:(d + 1) * 8, :], wctab.t[d * 32 + c * 8:d * 32 + c * 8 + 8, :, :].rearrange("g p x -> p g x"),
                          reads=[wctab], writes=[WCc])
                    k.dma("sp", Kc[:, d, :, :].rearrange("p a b -> p (a b)"), ktab.t[d, c, :, :], reads=[ktab], writes=[Kc])
                k.dma("sp", dcol[:], s5_d.t[layer, c * 128:(c + 1) * 128].rearrange("(p o) -> p o", o=1), reads=[s5_d], writes=[dcol])
                V(lambda e: e.tensor_scalar(out=dgm[:], in0=ident_f[:], scalar1=dcol[:, 0:1], scalar2=None, op0=ALU.mult),
                  [ident_f, dcol], [dgm])
                for rho in range(32):
                    pp = nps()
                    started = [False] * 4
                    for d in range(2):
                        kk = rho if d == 0 else 31 - rho
                        for gi in range(8):
                            j = gi // 2
                            kw = dict(tile_position=(0, 96)) if j == 3 else {}
                            last_ = (d == 1 and gi % 2 == 1)
                            PE(lambda e, d=d, gi=gi, j=j, kk=kk, pp=pp, st=not started[j], last_=last_, kw=kw: e.matmul(
                                out=pp[32 * j:32 * j + 32, 0:NBLK], lhsT=WCc[:, d * 8 + gi, kk * 32:(kk + 1) * 32],
                                rhs=Hsb[d][:, c * 8 + gi, :], start=st, stop=last_, **kw), [WCc, Hsb[d]], [pp])
                            started[j] = True
                    A(lambda e, pp=pp, rho=rho: e.copy(out=yfar[:, :, rho], in_=pp[:, 0:NBLK]), [pp], [yfar])
                for tg in range(NTOK // 512):
                    pp = nps()
                    usl = uT[:, c, tg * 512:(tg + 1) * 512]
                    PE(lambda e, pp=pp, usl=usl: e.matmul(out=pp[:, :], lhsT=dgm[:], rhs=usl, start=True, stop=False), [dgm, uT], [pp])
                    u3 = usl.rearrange("p (b r) -> p b r", r=TB)
                    p3 = pp[:, :].rearrange("p (b r) -> p b r", r=TB)
                    for d in range(2):
                        for tau in range(32):
                            if d == 0:
                                o_ap, r_ap = p3[:, :, tau:32], u3[:, :, 0:32 - tau]
                            else:
                                o_ap, r_ap = p3[:, :, 0:32 - tau], u3[:, :, tau:32]
                            PE(lambda e, d=d, tau=tau, o_ap=o_ap, r_ap=r_ap, pp=pp: e.matmul(
                                out=o_ap, lhsT=Kc[:, d, tau, :], rhs=r_ap, start=False, stop=(d == 1 and tau == 31)), [Kc, uT], [pp])
                    k.dma("sp", zt5[:], projT.t[512 + c * 128:512 + (c + 1) * 128, tg * 512:(tg + 1) * 512], reads=[projT], writes=[zt5])
                    V(lambda e, pp=pp, tg=tg: e.tensor_tensor(
                        out=ys[:], in0=pp[:, :], in1=yfar[:, tg * 16:(tg + 1) * 16, :].rearrange("p a b -> p (a b)"), op=ALU.add),
                      [pp, yfar], [ys])
                    A(lambda e: e.activation(out=ys[:], in_=ys[:], func=ACT.Gelu), [ys], [ys])
                    A(lambda e: e.activation(out=zt5[:], in_=zt5[:], func=ACT.Sigmoid), [zt5], [zt5])
                    V(lambda e: e.tensor_tensor(out=yo[:], in0=ys[:], in1=zt5[:], op=ALU.mult), [ys, zt5], [yo])
                    k.dma("sp", mixT.t[c * 128:(c + 1) * 128, tg * 512:(tg + 1) * 512], yo[:], reads=[yo], writes=[mixT])
            k.barrier()
        k.es = es

    for layer in range(nlayers):
        with contextlib.ExitStack() as esm:
            k.es = esm
            wa = [k.sb("wa%d" % i, [128, 8, 512], F32) for i in range(2)]
            mrow = [k.sb("mrow%d" % i, [2, 512], F32) for i in range(2)]
            brow = [k.sb("brow%d" % i, [2, 512], F32) for i in range(2)]
            for nt in range(12):
                wt = wa[nt % 2]
                k.dma("sp", wt[:], w_ada.t[layer, :, nt * 512:(nt + 1) * 512].rearrange("(c p) n -> p c n", p=128),
                      reads=[w_ada], writes=[wt])
                br = brow[nt % 2]
                k.dma("sp", br[:], b_ada.t[layer:layer + 1, nt * 512:(nt + 1) * 512].to_broadcast([2, 512]),
                      reads=[b_ada], writes=[br])
                pp = nps()
                for c in range(8):
                    k.op("pe", lambda e, c=c: e.matmul(out=pp[0:2, :], lhsT=sc_t[:, c, :], rhs=wt[:, c, :],
                                                       start=(c == 0), stop=(c == 7)), reads=[sc_t, wt], writes=[pp])
                mr = mrow[nt % 2]
                k.op("dve", lambda e: e.tensor_tensor(out=mr[:], in0=pp[0:2, :], in1=br[:], op=ALU.add),
                     reads=[pp, br], writes=[mr])
                k.dma("sp", mod_d.t[:, nt * 512:(nt + 1) * 512], mr[:], reads=[mr], writes=[mod_d])
            k.barrier()
        k.es = es

        with contextlib.ExitStack() as es1:
            k.es = es1
            win = k.sb("win", [128, 8, IN_W], BF16)
            for c in range(8):
                k.dma("sp", win[:, c, :], winbs[layer % 2].t[c * 128:(c + 1) * 128, :], reads=[winbs[layer % 2]], writes=[win])
            A1 = [k.sb("A1_%d" % v, [128, D], F32) for v in range(2)]
            S1 = [k.sb("S1_%d" % v, [128, D], F32) for v in range(2)]
            gtmp = k.sb("gtmp", [128, D], F32)
            for v in range(2):
                load_mod(S1[v], v, 0, layer, None, "shift")
                load_mod(A1[v], v, 1, layer, None, "scale", gsrc=norm_mix, gt=gtmp)
            xt = [k.sb("xt%d" % i, [128, D], F32) for i in range(2)]
            sq = [k.sb("sq%d" % i, [128, D], F32) for i in range(2)]
            ss = [k.sb("ss%d" % i, [128, 4], F32) for i in range(2)]
            hb = [k.sb("hb%d" % i, [128, D], BF16) for i in range(2)]
            hT = [k.sb("hT%d" % i, [128, 8, 512], BF16) for i in range(2)]
            ev = [k.sb("ev%d" % i, [128, 512], F32) for i in range(4)]
            NG1 = NTOK // 512
            prog1 = {"A": 0, "B": 0}

            def p1A():
                for g in range(NG1):
                    while prog1["B"] < g - 1:
                        k._yield(force=True)
                    hTg = hT[g % 2]
                    for j in range(4):
                        t = g * 4 + j
                        v = 0 if t * 128 < NPS * LP else 1
                        xb = xt[t % 2]
                        k.dma("sp", xb[:], xres.t[t * 128:(t + 1) * 128, :], reads=[xres], writes=[xb])
                        rms_tile(xb, hb[t % 2], A1[v], S1[v], sq[t % 2], ss[t % 2])
                        transpose_to(hb[t % 2], hTg, j)
                    prog1["A"] = g + 1

            def p1B():
                nchunks = (IN_W + 127) // 128
                for g in range(NG1):
                    while prog1["A"] < g + 1:
                        k._yield(force=True)
                    hTg = hT[g % 2]
                    for n in range(nchunks):
                        m = min(128, IN_W - n * 128)
                        pp = nps()
                        for c in range(8):
                            k.op("pe", lambda e, c=c, n=n, m=m, pp=pp: e.matmul(
                                out=pp[0:m, :], lhsT=win[:, c, n * 128:n * 128 + m], rhs=hTg[:, c, :],
                                start=(c == 0), stop=(c == 7)), reads=[win, hTg], writes=[pp], inc=(c == 7))
                        evb = ev[n % 4]
                        if n % 2 == 0:
                            k.op("act", lambda e, m=m, pp=pp, evb=evb: e.copy(out=evb[0:m, :], in_=pp[0:m, :]),
                                 reads=[pp], writes=[evb])
                        else:
                            k.op("dve", lambda e, m=m, pp=pp, evb=evb: e.tensor_copy(out=evb[0:m, :], in_=pp[0:m, :]),
                                 reads=[pp], writes=[evb])
                        k.dma("act", projT.t[n * 128:n * 128 + m, g * 512:(g + 1) * 512], evb[0:m, :],
                              reads=[evb], writes=[projT])
                    prog1["B"] = g + 1

            k.interleave([p1A, p1B], weights=[1, 4])
            k.barrier()
        k.es = es

        if layer + 1 < nlayers:
            cast_mlp_weights(layer + 1)

        if TEST_DENSE:
            with contextlib.ExitStack() as esx:
                k.es = esx
                tb = [k.sb("tb%d" % i, [128, 1024], BF16) for i in range(2)]
                i = 0
                for r in range(8):
                    for cc in range(NTOK // 1024):
                        b = tb[i % 2]
                        i += 1
                        k.dma("pool", b[:], projT.t[r * 128:(r + 1) * 128, cc * 1024:(cc + 1) * 1024],
                              reads=[projT], writes=[b])
                        k.dma("sp", mixT.t[r * 128:(r + 1) * 128, cc * 1024:(cc + 1) * 1024], b[:],
                              reads=[b], writes=[mixT])
                k.barrier()
            k.es = es
        else:
            TEST_GDN = flags.get("test_gdn", False)
            TEST_S5 = flags.get("test_s5", False)
            if TEST_GDN or TEST_S5:
                with contextlib.ExitStack() as esx:
                    k.es = esx
                    tb = [k.sb("tb%d" % i, [128, 1024], BF16) for i in range(2)]
                    i = 0
                    for r in (range(4) if TEST_GDN else range(4, 8)):
                        for cc in range(NTOK // 1024):
                            b = tb[i % 2]
                            i += 1
                            k.dma("pool", b[:], projT.t[r * 128:(r + 1) * 128, cc * 1024:(cc + 1) * 1024],
                                  reads=[projT], writes=[b])
                            k.dma("sp", mixT.t[r * 128:(r + 1) * 128, cc * 1024:(cc + 1) * 1024], b[:],
                                  reads=[b], writes=[mixT])
                    k.barrier()
                k.es = es
            if not TEST_S5:
                gdn_phase(layer)
            if not TEST_GDN:
                s5_phase(layer)

        with contextlib.ExitStack() as es3:
            k.es = es3
            wo = k.sb("wo", [128, 8, D], BF16)
            for c in range(8):
                k.dma("sp", wo[:, c, :], woutbs[layer % 2].t[c * 128:(c + 1) * 128, :], reads=[woutbs[layer % 2]], writes=[wo])
            G1 = [k.sb("G1_%d" % v, [128, D], F32) for v in range(2)]
            A2 = [k.sb("A2_%d" % v, [128, D], F32) for v in range(2)]
            S2 = [k.sb("S2_%d" % v, [128, D], F32) for v in range(2)]
            G2 = [k.sb("G2_%d" % v, [128, D], F32) for v in range(2)]
            gtmp = k.sb("gtmp3", [128, D], F32)
            for v in range(2):
                load_mod(G1[v], v, 2, layer, None, "gate")
                load_mod(S2[v], v, 3, layer, None, "shift")
                load_mod(A2[v], v, 4, layer, None, "scale", gsrc=norm_mlp, gt=gtmp)
                load_mod(G2[v], v, 5, layer, None, "gate")
            mx = [k.sb("mx%d" % i, [128, 8, 512], BF16) for i in range(1)]
            xt = [k.sb("x3_%d" % i, [128, D], F32) for i in range(8)]
            sqA = [k.sb("sq3A_%d" % i, [128, D], F32) for i in range(2)]
            ssA = [k.sb("ss3A_%d" % i, [128, 4], F32) for i in range(2)]
            sqB = [k.sb("sq3B_%d" % i, [128, D], F32) for i in range(2)]
            ssB = [k.sb("ss3B_%d" % i, [128, 4], F32) for i in range(2)]
            hb = [k.sb("hb3_%d" % i, [128, D], BF16) for i in range(2)]
            hT2 = [k.sb("hT3_%d" % i, [128, 8, 512], BF16) for i in range(2)]
            ffT = k.sb("ffT", [128, 32, 512], BF16)
            wq = [k.sb("wq%d" % i, [128, 8, 1024], BF16) for i in range(2)]
            wqi = [0]
            rl = k.sb("rl", [128, 512], F32)
            last = (layer == nlayers - 1)
            if last:
                NF = k.sb("NF", [128, D], F32)
                k.dma("sp", NF[:], norm_final.t.rearrange("(o d) -> o d", o=1).to_broadcast([128, D]),
                      reads=[norm_final], writes=[NF])
            NG = NTOK // 512
            prog = {"A": 0, "B": 0}

            def stageA():
                bk = [psf[4], psf[5]]
                bi = [0]
                for g in range(NG):
                    while prog["B"] < g - 1:
                        k._yield(force=True)
                    mxg = mx[0]
                    hT = hT2[g % 2]
                    k.dma("sp", mxg[:], mixT.t[:, g * 512:(g + 1) * 512].rearrange("(c p) t -> p c t", p=128),
                          reads=[mixT], writes=[mxg])
                    for j in range(4):
                        t = g * 4 + j
                        v = 0 if t * 128 < NPS * LP else 1
                        xb = xt[(g % 2) * 4 + j]
                        sq_, ss_ = sqA[t % 2], ssA[t % 2]
                        k.dma("sp", xb[:], xres.t[t * 128:(t + 1) * 128, :], reads=[xres], writes=[xb])
                        for half in range(2):
                            pp = bk[bi[0] % 2]
                            bi[0] += 1
                            for c in range(8):
                                k.op("pe", lambda e, c=c, pp=pp, half=half, j=j: e.matmul(
                                    out=pp[:, :], lhsT=mxg[:, c, j * 128:(j + 1) * 128],
                                    rhs=wo[:, c, half * 512:(half + 1) * 512], start=(c == 0), stop=(c == 7)),
                                    reads=[mxg, wo], writes=[pp], inc=(c == 7))
                            sl = slice(half * 512, (half + 1) * 512)
                            k.op("dve", lambda e, pp=pp, sl=sl, v=v: e.tensor_tensor(out=sq_[:, sl], in0=pp[:, :],
                                                                                   in1=G1[v][:, sl], op=ALU.mult),
                                 reads=[pp, G1[v]], writes=[sq_])
                        k.op("dve", lambda e: e.tensor_tensor(out=xb[:], in0=xb[:], in1=sq_[:], op=ALU.add),
                             reads=[xb, sq_], writes=[xb])
                        rms_tile(xb, hb[t % 2], A2[v], S2[v], sq_, ss_)
                        transpose_to(hb[t % 2], hT, j)
                    prog["A"] = g + 1

            def stageB():
                bk = psf[0:4] + [psf6]
                bi = [0]

                def nb():
                    p = bk[bi[0] % 5]
                    bi[0] += 1
                    return p

                for g in range(NG):
                    while prog["A"] < g + 1:
                        k._yield(force=True)
                    hT = hT2[g % 2]
                    for q4 in range(4):
                        wb = wq[wqi[0] % 2]
                        wqi[0] += 1
                        k.dma("sp", wb[:], w1bs[layer % 2].t[:, q4 * 1024:(q4 + 1) * 1024].rearrange("(c p) n -> p c n", p=128),
                              reads=[w1bs[layer % 2]], writes=[wb])
                        for fi in range(8):
                            f = q4 * 8 + fi
                            pp = nb()
                            for c in range(8):
                                k.op("pe", lambda e, c=c, pp=pp, fi=fi, wb=wb: e.matmul(
                                    out=pp[:, :], lhsT=wb[:, c, fi * 128:(fi + 1) * 128], rhs=hT[:, c, :],
                                    start=(c == 0), stop=(c == 7)), reads=[wb, hT], writes=[pp], inc=(c == 7))
                            k.op("act", lambda e, pp=pp: e.activation(out=rl[:], in_=pp[:, :], func=ACT.Relu),
                                 reads=[pp], writes=[rl])
                            k.op("dve", lambda e, f=f: e.tensor_tensor(out=ffT[:, f, :], in0=rl[:], in1=rl[:], op=ALU.mult),
                                 reads=[rl], writes=[ffT])
                    for jp in range(2):
                        pps = [[nb(), nb()], [nb(), nb()]]
                        for q4 in range(4):
                            wb = wq[wqi[0] % 2]
                            wqi[0] += 1
                            k.dma("sp", wb[:], w2bs[layer % 2].t[q4 * 1024:(q4 + 1) * 1024, :].rearrange("(c p) n -> p c n", p=128),
                                  reads=[w2bs[layer % 2]], writes=[wb])
                            for jj in range(2):
                                j = jp * 2 + jj
                                for half in range(2):
                                    pp = pps[jj][half]
                                    for fi in range(8):
                                        f = q4 * 8 + fi
                                        k.op("pe", lambda e, pp=pp, f=f, fi=fi, j=j, half=half, wb=wb: e.matmul(
                                            out=pp[:, :], lhsT=ffT[:, f, j * 128:(j + 1) * 128],
                                            rhs=wb[:, fi, half * 512:(half + 1) * 512],
                                            start=(f == 0), stop=(f == 31)), reads=[ffT, wb], writes=[pp], inc=(fi == 7))
                        for jj in range(2):
                            j = jp * 2 + jj
                            t = g * 4 + j
                            v = 0 if t * 128 < NPS * LP else 1
                            xb = xt[(g % 2) * 4 + j]
                            sq_, ss_ = sqB[t % 2], ssB[t % 2]
                            for half in range(2):
                                sl = slice(half * 512, (half + 1) * 512)
                                pp = pps[jj][half]
                                k.op("dve", lambda e, pp=pp, sl=sl, v=v: e.tensor_tensor(out=sq_[:, sl], in0=pp[:, :],
                                                                                       in1=G2[v][:, sl], op=ALU.mult),
                                     reads=[pp, G2[v]], writes=[sq_])
                            k.op("dve", lambda e, xb=xb: e.tensor_tensor(out=xb[:], in0=xb[:], in1=sq_[:], op=ALU.add),
                                 reads=[xb, sq_], writes=[xb])
                            if not last:
                                k.dma("sp", xres.t[t * 128:(t + 1) * 128, :], xb[:], reads=[xb], writes=[xres])
                            else:
                                rms_tile(xb, None, None, None, sq_, ss_)
                                k.op("dve", lambda e, xb=xb: e.scalar_tensor_tensor(
                                    out=sq_[:], in0=xb[:], scalar=ss_[:, 2:3], in1=NF[:],
                                    op0=ALU.mult, op1=ALU.mult), reads=[xb, ss_, NF], writes=[sq_])
                                k.dma("sp", y_out.t[t * 128:(t + 1) * 128, :], sq_[:], reads=[sq_], writes=[y_out])
                    prog["B"] = g + 1

            k.interleave([stageA, stageB], weights=[1, 4])
            k.barrier()
        k.es = es

    k.finish([y_out, nre_out, nim_out, ngdn_out])
    es.close()
    return nc, k


_WNAMES = ["norm_mix", "norm_mlp", "w_ada", "b_ada", "w_in", "conv_qkv", "s5_lambda_re", "s5_lambda_im",
           "s5_log_dt", "s5_b_re", "s5_b_im", "s5_c_re", "s5_c_im", "s5_d", "gdn_a_log", "gdn_dt_bias",
           "gdn_norm", "w_out", "w_mlp_in", "w_mlp_out", "norm_final"]


def _consts():
    j = np.arange(128)[:, None]
    i = np.arange(128)[None, :]
    cm = np.stack([np.where(j <= i, 0.0, -30000.0), np.where(j >= i, 0.0, -30000.0)]).astype(np.float32)
    sr = np.zeros((8, 8, 128), np.float32)
    for r in range(8):
        sr[r, r, :] = 1.0
    s8 = np.zeros((8, 2), np.float32)
    s8[0:4, 0] = 1.0
    s8[4:8, 1] = 1.0
    a = np.arange(128)[:, None]
    b = np.arange(128)[None, :]
    lv = np.stack([(a // 16 == b // 16), (a // 32 == b // 32) & (a // 16 != b // 16),
                   (a // 64 == b // 64) & (a // 32 != b // 32), (a // 64 != b // 64)]).astype(np.float32)
    return cm, sr.reshape(8, 1024), s8, lv


CMASK, SELROWS, SEL8, LVLMASK = _consts()
SWAPM = np.roll(np.eye(128, dtype=np.float32), 64, axis=1)


def make_in_maps(inp):
    f = lambda a: np.ascontiguousarray(np.asarray(a, dtype=np.float32))
    xp, xs = f(inp["x_prompt"]), f(inp["x_sample"])
    ident = np.eye(128, dtype=np.float32)
    maps = []
    for c in range(8):
        b = c % 2
        m = {
            "xin": np.concatenate([xp[NPS * c:NPS * (c + 1)].reshape(NPS * LP, D), xs[b]], axis=0),
            "cvec": np.stack([f(inp["c_ctx"]), f(inp["c"])[b]], axis=0),
            "h0re": f(inp["state_s5_re"])[b], "h0im": f(inp["state_s5_im"])[b], "s0": f(inp["state_gdn"])[b],
            "ident": ident, "cmask": CMASK, "selrows": SELROWS, "sel8": SEL8, "lvlmask": LVLMASK, "swapm": SWAPM,
        }
        for n in _WNAMES:
            m[n] = f(inp[n])
        maps.append(m)
    return maps


def kernel(**inputs):
    nc, _ = build()
    maps = make_in_maps(inputs)
    res = run_bass_kernel_spmd(nc, maps, core_ids=list(range(8)))
    r = res.results
    y_prompt = np.concatenate([r[c]["y"][:NPS * LP].reshape(NPS, LP, D) for c in range(8)], axis=0)
    y_sample = np.stack([r[b]["y"][NPS * LP:] for b in range(2)], axis=0)
    nre = np.concatenate([r[c]["nre"] for c in range(8)], axis=0)
    nim = np.concatenate([r[c]["nim"] for c in range(8)], axis=0)
    ngdn = np.concatenate([r[c]["ngdn"] for c in range(8)], axis=0)
    return (y_prompt.astype(np.float32), y_sample.astype(np.float32), nre.astype(np.float32),
            nim.astype(np.float32), ngdn.astype(np.float32))
```

```python
import contextlib
import math
import numpy as np
import concourse.bass as bass
import concourse.mybir as mybir
from concourse.bass_utils import run_bass_kernel_spmd

F32 = mybir.dt.float32
BF16 = mybir.dt.bfloat16
I32 = mybir.dt.int32
ALU = mybir.AluOpType
ACT = mybir.ActivationFunctionType
AX = mybir.AxisListType

D = 1024
DEPTH = 4
NPS = 4
LP = 256
LS = 4096
NTOK = NPS * LP + LS
IN_W = 3088
DFF = 4096
EPS = 1e-6
TWO_PI = 2.0 * math.pi

SEQS = [(i * LP, LP) for i in range(NPS)] + [(NPS * LP, LS)]


class Buf:
    def __init__(self, t, name):
        self.t = t
        self.name = name
        self.lastw = None
        self.readers = {}

    def __getitem__(self, k):
        return self.t[k]


class K:
    def __init__(self, nc, es):
        self.nc = nc
        self.es = es
        self.eng = {"pe": nc.tensor, "dve": nc.vector, "act": nc.scalar, "pool": nc.gpsimd, "sp": nc.sync}
        self.sem = {}
        self.cnt = {}
        for e in self.eng:
            self.sem[e] = es.enter_context(nc.semaphore("sem_" + e))
            self.cnt[e] = 0
        self.seen = {e: {} for e in self.eng}
        self.dsem = {"sp": [], "pool": [], "act": [], "bg": []}
        self.duse = {"sp": [], "pool": [], "act": [], "bg": []}
        self.dnext = {"sp": 0, "pool": 0, "act": 0, "bg": 0}
        self.eng["bg"] = nc.gpsimd
        self.seen["bg"] = self.seen["pool"]
        for q, n in (("sp", 16), ("pool", 12), ("act", 8), ("bg", 8)):
            for i in range(n):
                self.dsem[q].append(es.enter_context(nc.semaphore("dsem_%s%d" % (q, i))))
                self.duse[q].append(0)
        self.ninst = 0

    def sb(self, name, shape, dt):
        self.uid = getattr(self, "uid", 0) + 1
        name = "%s_u%d" % (name, self.uid)
        return Buf(self.es.enter_context(self.nc.sbuf_tensor(name, list(shape), dt)), name)

    def ps(self, name, shape, dt):
        return Buf(self.es.enter_context(self.nc.psum_tensor(name, list(shape), dt)), name)

    def dram(self, name, shape, dt, kind="Internal"):
        b = Buf(self.nc.dram_tensor(name, list(shape), dt, kind=kind).ap(), name)
        b.is_dram = True
        return b

    def _wait(self, e, dep):
        if dep is None:
            return
        sem, val = dep
        key = sem.name
        if self.seen[e].get(key, 0) >= val:
            return
        self.eng[e].wait_ge(sem, val)
        self.seen[e][key] = val

    def _deps(self, e, reads, writes):
        for b in reads:
            self._wait(e, b.lastw)
        for b in writes:
            self._wait(e, b.lastw)
            for r in b.readers.values():
                self._wait(e, r)

    def _commit(self, tok, reads, writes):
        for b in reads:
            b.readers[tok[0].name] = tok
        for b in writes:
            b.lastw = tok
            b.readers = {}

    def op(self, e, inst_fn, reads=(), writes=(), inc=True):
        self._deps(e, reads, writes)
        inst = inst_fn(self.eng[e])
        if inc or e != "pe":
            self.cnt[e] += 1
            inst.then_inc(self.sem[e], 1)
            tok = (self.sem[e], self.cnt[e])
        else:
            tok = (self.sem[e], self.cnt[e] + 1)
        if e == "pe":
            self.seen["pe"][self.sem["pe"].name] = tok[1]
        self._commit(tok, reads, writes)
        self.ninst += 1
        self._yield()
        return inst

    def dma(self, q, out, in_, reads=(), writes=(), slow=False):
        eq = "pool" if q == "bg" else q
        i = self.dnext[q]
        self.dnext[q] = (i + 1) % len(self.dsem[q])
        sem = self.dsem[q][i]
        if self.duse[q][i] > 0:
            self._wait(eq, (sem, 16 * self.duse[q][i]))
        self._deps(eq, reads, writes)
        self.duse[q][i] += 1
        if slow:
            inst = self.eng[eq].dma_start(out=out, in_=in_, allow_slow_non_contiguous=True)
        else:
            inst = self.eng[eq].dma_start(out=out, in_=in_)
        inst.then_inc(sem, 16)
        tok = (sem, 16 * self.duse[q][i])
        self._commit(tok, reads, writes)
        self.ninst += 1
        self._yield()
        return inst

    def _yield(self, force=False):
        il = getattr(self, "_il", None)
        if il is None:
            return
        import threading
        me = il["ids"].get(threading.get_ident())
        if me is None:
            return
        if not force:
            il["cnt"][me] += 1
            if il["cnt"][me] % il["w"][me] != 0:
                return
        with il["cv"]:
            alive = [i for i in range(il["n"]) if il["alive"][i]]
            nxt = [i for i in alive if i > me] + [i for i in alive if i < me]
            if nxt:
                il["turn"] = nxt[0]
                il["cv"].notify_all()
                while il["turn"] != me:
                    il["cv"].wait()

    def interleave(self, fns, weights=None):
        import threading
        n = len(fns)
        il = {"turn": 0, "alive": [True] * n, "cv": threading.Condition(), "n": n, "ids": {}, "err": [],
              "w": list(weights or [1] * n), "cnt": [0] * n}
        self._il = il

        def runner(i):
            il["ids"][threading.get_ident()] = i
            try:
                with il["cv"]:
                    while il["turn"] != i:
                        il["cv"].wait()
                fns[i]()
            except BaseException as ex:
                il["err"].append(ex)
            finally:
                with il["cv"]:
                    il["alive"][i] = False
                    alive = [j for j in range(n) if il["alive"][j]]
                    if alive:
                        nxt = [j for j in alive if j > i] + [j for j in alive if j < i]
                        il["turn"] = nxt[0]
                    il["cv"].notify_all()

        ths = [threading.Thread(target=runner, args=(i,)) for i in range(n)]
        for t in ths:
            t.start()
        for t in ths:
            t.join()
        self._il = None
        if il["err"]:
            raise il["err"][0]

    def barrier(self, bufs=()):
        toks = [(self.sem[e], self.cnt[e]) for e in self.sem if self.cnt[e] > 0]
        for q in self.dsem:
            if q == "bg" and not getattr(self, "_final", False):
                continue
            for i, s in enumerate(self.dsem[q]):
                if self.duse[q][i] > 0:
                    toks.append((s, 16 * self.duse[q][i]))
        for e in self.sem:
            for t in toks:
                self._wait(e, t)

    def finish(self, outs):
        self._final = True
        for b in outs:
            self._wait("sp", b.lastw)
        self.barrier()


def build(flags=None):
    flags = flags or {}
    TEST_DENSE = flags.get("test_dense", False)
    nlayers = flags.get("nlayers", DEPTH)

    nc = bass.Bass("TRN2", target_bir_lowering=False)
    es = contextlib.ExitStack()
    k = K(nc, es)

    def din(name, shape, dt=F32):
        return k.dram(name, shape, dt, kind="ExternalInput")

    def dout(name, shape, dt=F32):
        return k.dram(name, shape, dt, kind="ExternalOutput")

    xin = din("xin", [NTOK, D])
    cvec = din("cvec", [2, D])
    h0re = din("h0re", [DEPTH, 2, 32, 64])
    h0im = din("h0im", [DEPTH, 2, 32, 64])
    s0 = din("s0", [DEPTH, 2, 4, 128, 128])
    norm_mix = din("norm_mix", [DEPTH, D])
    norm_mlp = din("norm_mlp", [DEPTH, D])
    w_ada = din("w_ada", [DEPTH, D, 6 * D])
    b_ada = din("b_ada", [DEPTH, 6 * D])
    w_in = din("w_in", [DEPTH, D, IN_W])
    conv_qkv = din("conv_qkv", [DEPTH, 3, 1536])
    lam_re = din("s5_lambda_re", [DEPTH, 2, 32, 64])
    lam_im = din("s5_lambda_im", [DEPTH, 2, 32, 64])
    log_dt = din("s5_log_dt", [DEPTH, 2, 32])
    b_re = din("s5_b_re", [DEPTH, 2, 32, 64, 16])
    b_im = din("s5_b_im", [DEPTH, 2, 32, 64, 16])
    c_re = din("s5_c_re", [DEPTH, 2, 32, 16, 64])
    c_im = din("s5_c_im", [DEPTH, 2, 32, 16, 64])
    s5_d = din("s5_d", [DEPTH, 512])
    a_log = din("gdn_a_log", [DEPTH, 2, 4])
    dt_bias = din("gdn_dt_bias", [DEPTH, 2, 4])
    gdn_norm = din("gdn_norm", [DEPTH, 128])
    w_out = din("w_out", [DEPTH, D, D])
    w_mlp_in = din("w_mlp_in", [DEPTH, D, DFF])
    w_mlp_out = din("w_mlp_out", [DEPTH, DFF, D])
    norm_final = din("norm_final", [D])
    ident_in = din("ident", [128, 128])
    cmask_in = din("cmask", [2, 128, 128])
    selrows_in = din("selrows", [8, 8 * 128])
    sel8_in = din("sel8", [8, 2])
    swap_in = din("swapm", [128, 128])
    lvl_in = din("lvlmask", [4, 128, 128])

    y_out = dout("y", [NTOK, D])
    nre_out = dout("nre", [NPS, DEPTH, 2, 32, 64])
    nim_out = dout("nim", [NPS, DEPTH, 2, 32, 64])
    ngdn_out = dout("ngdn", [NPS, DEPTH, 2, 4, 128, 128])

    DBGK = "ExternalOutput" if flags.get("dbg") else "Internal"
    xres = k.dram("xres", [NTOK, D], F32)
    projT = k.dram("projT", [IN_W + 16, NTOK], F32, kind=DBGK)
    mixT = k.dram("mixT", [D, NTOK], BF16, kind=DBGK)
    mod_d = k.dram("mod_d", [2, 6 * D], F32)
    w1bs = [k.dram("w1b%d" % i, [D, DFF], BF16) for i in range(2)]
    w2bs = [k.dram("w2b%d" % i, [DFF, D], BF16) for i in range(2)]

    winbs = [k.dram("winb%d" % i, [D, IN_W], BF16) for i in range(2)]
    woutbs = [k.dram("woutb%d" % i, [D, D], BF16) for i in range(2)]

    def cast_mlp_weights(layer):
        for c in range(8):
            k.dma("bg", winbs[layer % 2].t[c * 128:(c + 1) * 128, :], w_in.t[layer, c * 128:(c + 1) * 128, :],
                  reads=[w_in], writes=[winbs[layer % 2]])
        for c in range(8):
            k.dma("bg", woutbs[layer % 2].t[c * 128:(c + 1) * 128, :], w_out.t[layer, c * 128:(c + 1) * 128, :],
                  reads=[w_out], writes=[woutbs[layer % 2]])
        for c in range(8):
            k.dma("bg", w1bs[layer % 2].t[c * 128:(c + 1) * 128, :], w_mlp_in.t[layer, c * 128:(c + 1) * 128, :],
                  reads=[w_mlp_in], writes=[w1bs[layer % 2]])
        for c in range(32):
            k.dma("bg", w2bs[layer % 2].t[c * 128:(c + 1) * 128, :], w_mlp_out.t[layer, c * 128:(c + 1) * 128, :],
                  reads=[w_mlp_out], writes=[w2bs[layer % 2]])
    pe_d = k.dram("pe_d", [64, 512], F32)
    gqkvT = k.dram("gqkvT", [1536, NTOK], BF16, kind=DBGK)
    gkv_tm = k.dram("gkv_tm", [NTOK, 8, 128], BF16, kind=DBGK)
    grow = k.dram("grow", [2, 8, NTOK], F32, kind=DBGK)
    gtm = k.dram("gtm", [NTOK, 16], F32, kind=DBGK)
    ktab = k.dram("ktab", [2, 4, 128, 32 * 128], BF16, kind=DBGK)
    wctab = k.dram("wctab", [64, 128, 32 * 32], BF16, kind=DBGK)
    ofT = k.dram("ofT", [512, NTOK], F32, kind=DBGK)
    obT = k.dram("obT", [512, NTOK], F32, kind=DBGK)
    odT = [ofT, obT]

    ident_f = k.sb("ident_f", [128, 128], F32)
    ident_b = k.sb("ident_b", [128, 128], BF16)
    k.dma("sp", ident_f[:], ident_in[:, :], writes=[ident_f])
    k.op("dve", lambda e: e.tensor_copy(out=ident_b[:], in_=ident_f[:]), reads=[ident_f], writes=[ident_b])
    sc_t = k.sb("sc_t", [128, 8, 2], F32)
    sc_b = k.sb("sc_b", [128, 8, 2], BF16)
    epsc = k.sb("epsc", [128, 1], F32)
    k.op("dve", lambda e: e.memset(epsc[:], EPS), writes=[epsc])

    psf = [k.ps("psf%d" % i, [128, 512], F32) for i in range(6)]
    psb = k.ps("psb", [128, 1024], BF16)
    psb2 = k.ps("psb2", [128, 1024], BF16)
    psf6 = Buf(psb2.t[:].bitcast(F32), "psb2_as_f32")
    psi = [0]

    def nps():
        p = psf[psi[0] % 6]
        psi[0] += 1
        return p

    cast_mlp_weights(0)

    with contextlib.ExitStack() as es0:
        k.es = es0
        craw = k.sb("craw", [128, 2, 8], F32)
        for v in range(2):
            k.dma("sp", craw[:, v, :], cvec.t[v, :].rearrange("(c p) -> p c", p=128), reads=[cvec], writes=[craw], slow=True)
        k.op("act", lambda e: e.activation(out=sc_t[:].rearrange("p c v -> p v c"), in_=craw[:], func=ACT.Silu),
             reads=[craw], writes=[sc_t])
        k.op("dve", lambda e: e.tensor_copy(out=sc_b[:], in_=sc_t[:]), reads=[sc_t], writes=[sc_b])

        jrow = k.sb("jrow", [64, 256], F32)
        rcol = k.sb("rcol", [64, 1], F32)
        k.op("pool", lambda e: e.iota(jrow[:], pattern=[[1, 256]], base=0, channel_multiplier=0,
                                      allow_small_or_imprecise_dtypes=True), writes=[jrow])
        k.op("pool", lambda e: e.iota(rcol[:], pattern=[[0, 1]], base=0, channel_multiplier=1,
                                      allow_small_or_imprecise_dtypes=True), writes=[rcol])
        om = k.sb("om", [64, 256], F32)
        k.op("act", lambda e: e.activation(out=om[:], in_=jrow[:], func=ACT.Exp, scale=-math.log(10000.0) / 256.0),
             reads=[jrow], writes=[om])
        ang = k.sb("ang", [64, 2, 256], F32)
        k.op("dve", lambda e: e.tensor_scalar(out=ang[:, 0, :], in0=om[:], scalar1=rcol[:, 0:1], scalar2=None,
                                              op0=ALU.mult), reads=[om, rcol], writes=[ang])
        k.op("dve", lambda e: e.tensor_scalar(out=ang[:, 1, :], in0=ang[:, 0, :], scalar1=math.pi / 2, scalar2=None,
                                              op0=ALU.add), reads=[ang], writes=[ang])
        pet = k.sb("pet", [64, 2, 256], F32)

        def sin_reduced(dst, src, shape, tagn):
            kf = k.sb("kf" + tagn, shape, F32)
            ki = k.sb("ki" + tagn, shape, I32)
            k.op("dve", lambda e: e.tensor_scalar(out=kf[:], in0=src, scalar1=1.0 / TWO_PI, scalar2=None,
                                                  op0=ALU.mult), reads=[], writes=[kf])
            k.op("dve", lambda e: e.tensor_copy(out=ki[:], in_=kf[:]), reads=[kf], writes=[ki])
            k.op("dve", lambda e: e.tensor_copy(out=kf[:], in_=ki[:]), reads=[ki], writes=[kf])
            k.op("dve", lambda e: e.scalar_tensor_tensor(out=kf[:], in0=kf[:], scalar=-TWO_PI, in1=src,
                                                         op0=ALU.mult, op1=ALU.add), reads=[kf], writes=[kf])
            k.op("dve", lambda e: e.tensor_scalar(out=kf[:], in0=kf[:], scalar1=math.pi, scalar2=-math.pi,
                                                  op0=ALU.min, op1=ALU.max), reads=[kf], writes=[kf])
            k.op("act", lambda e: e.activation(out=dst, in_=kf[:], func=ACT.Sin), reads=[kf], writes=[])

        k.barrier()
        sin_reduced(pet[:], ang[:], [64, 2, 256], "pe")
        k.barrier()
        k.dma("sp", pe_d.t[:, :], pet[:].rearrange("p a b -> p (a b)"), reads=[pet], writes=[pe_d])

        pcol = k.sb("pcol", [128, 512], F32)
        k.dma("sp", pcol[0:64, :], pe_d.t[:, :], reads=[pe_d], writes=[pcol])
        k.dma("sp", pcol[64:128, :], pe_d.t[:, :], reads=[pe_d], writes=[pcol])
        xt0 = [k.sb("xt0_%d" % i, [128, D], F32) for i in range(2)]
        prow = [k.sb("prow%d" % i, [128, 512], F32) for i in range(2)]
        for t in range(NTOK // 128):
            xb = xt0[t % 2]
            k.dma("sp", xb[:], xin.t[t * 128:(t + 1) * 128, :], reads=[xin], writes=[xb])
            if t * 128 >= NPS * LP:
                ti = t - NPS * LP // 128
                pr = prow[t % 2]
                k.dma("pool", pr[0:64, :], pe_d.t[2 * ti:2 * ti + 1, :].to_broadcast([64, 512]), reads=[pe_d], writes=[pr])
                k.dma("pool", pr[64:128, :], pe_d.t[2 * ti + 1:2 * ti + 2, :].to_broadcast([64, 512]), reads=[pe_d], writes=[pr])
                k.op("dve", lambda e: e.tensor_tensor(out=xb[:, 0:512], in0=xb[:, 0:512], in1=pr[:], op=ALU.add),
                     reads=[xb, pr], writes=[xb])
                k.op("dve", lambda e: e.tensor_tensor(out=xb[:, 512:1024], in0=xb[:, 512:1024], in1=pcol[:], op=ALU.add),
                     reads=[xb, pcol], writes=[xb])
            k.dma("sp", xres.t[t * 128:(t + 1) * 128, :], xb[:], reads=[xb], writes=[xres])
        k.barrier()
    k.es = es

    def rms_tile(xb, hb, A, SH, sq, ss):
        k.op("act", lambda e: e.activation(out=sq[:], in_=xb[:], func=ACT.Square, accum_out=ss[:, 0:1]),
             reads=[xb], writes=[sq, ss])
        k.op("act", lambda e: e.activation(out=ss[:, 1:2], in_=ss[:, 0:1], func=ACT.Ln, scale=1.0 / D, bias=epsc[:, 0:1]),
             reads=[ss, epsc], writes=[ss])
        k.op("act", lambda e: e.activation(out=ss[:, 2:3], in_=ss[:, 1:2], func=ACT.Exp, scale=-0.5),
             reads=[ss], writes=[ss])
        if A is None:
            return
        k.op("dve", lambda e: e.scalar_tensor_tensor(out=sq[:], in0=xb[:], scalar=ss[:, 2:3], in1=A[:],
                                                     op0=ALU.mult, op1=ALU.mult), reads=[xb, ss, A], writes=[sq])
        k.op("dve", lambda e: e.tensor_tensor(out=hb[:], in0=sq[:], in1=SH[:], op=ALU.add),
             reads=[sq, SH], writes=[hb])

    def transpose_to(hb, hT, j):
        for c in range(8):
            k.op("pe", lambda e, c=c: e.transpose(out=psb[:, c * 128:(c + 1) * 128], in_=hb[:, c * 128:(c + 1) * 128],
                                                  identity=ident_b[:]), reads=[hb, ident_b], writes=[psb])
        k.op("dve", lambda e: e.tensor_copy(out=hT[:, :, j * 128:(j + 1) * 128],
                                            in_=psb[:].rearrange("p (c t) -> p c t", c=8)), reads=[psb], writes=[hT])

    def load_mod(dst, v, idx, layer, tmpb, kind, gsrc=None, gt=None):
        k.dma("sp", dst[:], mod_d.t[v:v + 1, idx * D:(idx + 1) * D].to_broadcast([128, D]), reads=[mod_d], writes=[dst])
        if kind == "scale":
            k.dma("sp", gt[:], gsrc.t[layer:layer + 1, :].to_broadcast([128, D]), reads=[gsrc], writes=[gt])
            k.op("dve", lambda e: e.scalar_tensor_tensor(out=dst[:], in0=dst[:], scalar=1.0, in1=gt[:],
                                                         op0=ALU.add, op1=ALU.mult), reads=[dst, gt], writes=[dst])

    PIECES = []
    for (s0_, l_) in SEQS:
        for off in range(0, l_, 512):
            PIECES.append((s0_ + off, min(512, l_ - off), s0_, s0_ + l_))

    def V(fn, r, w):
        return k.op("dve", fn, reads=r, writes=w)

    def A(fn, r, w):
        return k.op("act", fn, reads=r, writes=w)

    def PE(fn, r, w):
        return k.op("pe", fn, reads=r, writes=w)

    def gdn_phase(layer):
        with contextlib.ExitStack() as esg:
            k.es = esg
            ones_b = k.sb("ones_b", [128, 128], BF16)
            V(lambda e: e.memset(ones_b[:], 1.0), [], [ones_b])
            cm = [k.sb("cm%d" % d, [128, 4, 128], F32) for d in range(2)]
            for d in range(2):
                for h in range(4):
                    k.dma("sp", cm[d][:, h, :], cmask_in.t[d, :, :], reads=[cmask_in], writes=[cm[d]])
            offd = k.sb("offd", [128, 4, 128], F32)
            idb4 = k.sb("idb4", [128, 4, 128], BF16)
            for h in range(4):
                V(lambda e, h=h: e.tensor_scalar(out=offd[:, h, :], in0=ident_f[:], scalar1=-1.0, scalar2=1.0,
                                                 op0=ALU.mult, op1=ALU.add), [ident_f], [offd])
                V(lambda e, h=h: e.tensor_copy(out=idb4[:, h, :], in_=ident_f[:]), [ident_f], [idb4])
            lvm = [k.sb("lvm%d" % i, [128, 4, 128], F32) for i in range(4)]
            for i in range(4):
                for h in range(4):
                    k.dma("sp", lvm[i][:, h, :], lvl_in.t[i, :, :], reads=[lvl_in], writes=[lvm[i]])
            idf4 = k.sb("idf4", [128, 4, 128], F32)
            for h in range(4):
                V(lambda e, h=h: e.tensor_copy(out=idf4[:, h, :], in_=ident_f[:]), [ident_f], [idf4])
            selr = k.sb("selr", [8, 8, 128], F32)
            k.dma("sp", selr[:].rearrange("p a b -> p (a b)"), selrows_in.t[:, :], reads=[selrows_in], writes=[selr])
            sel8 = k.sb("sel8", [8, 2], F32)
            k.dma("sp", sel8[:], sel8_in.t[:, :], reads=[sel8_in], writes=[sel8])
            convw = k.sb("convw", [128, 3, 12], F32)
            for tap in range(3):
                k.dma("sp", convw[:, tap, :], conv_qkv.t[layer, tap, :].rearrange("(b p) -> p b", p=128),
                      reads=[conv_qkv], writes=[convw], slow=True)
            dg = k.sb("dg", [128, 36, 128], BF16)
            for tap in range(3):
                for blk in range(12):
                    V(lambda e, tap=tap, blk=blk: e.tensor_scalar(
                        out=dg[:, tap * 12 + blk, :], in0=ident_f[:], scalar1=convw[:, tap, blk:blk + 1], scalar2=None,
                        op0=ALU.mult), [ident_f, convw], [dg])
            dtb = k.sb("dtb", [8, 1], F32)
            nal = k.sb("nal", [8, 1], F32)
            k.dma("sp", dtb[:], dt_bias.t[layer].rearrange("d (h o) -> (d h) o", o=1), reads=[dt_bias], writes=[dtb])
            k.dma("sp", nal[:], a_log.t[layer].rearrange("d (h o) -> (d h) o", o=1), reads=[a_log], writes=[nal])
            A(lambda e: e.activation(out=nal[:], in_=nal[:], func=ACT.Exp), [nal], [nal])
            V(lambda e: e.tensor_scalar(out=nal[:], in0=nal[:], scalar1=-1.0, scalar2=None, op0=ALU.mult), [nal], [nal])
            gnw = k.sb("gnw", [128, 1], F32)
            k.dma("sp", gnw[:], gdn_norm.t[layer, :].rearrange("(p o) -> p o", o=1), reads=[gdn_norm], writes=[gnw])
            mask01 = k.sb("mask01", [8, 512], F32)
            V(lambda e: e.memset(mask01[:], 1.0), [], [mask01])
            V(lambda e: e.memset(mask01[:].rearrange("p (c t) -> p c t", t=128)[:, :, 0:1], 0.0), [], [mask01])

            esp = contextlib.ExitStack()
            k.es = esp
            psbs = [psb, psb2]
            class TSP:
                pass

            def mk_prep(tag):
                P_ = TSP()
                P_.raw = [k.sb(tag + "raw%d" % i, [128, 12, 514], BF16) for i in range(2)]
                P_.cv = [k.sb(tag + "cv%d" % i, [128, 12, 512], BF16) for i in range(2)]
                P_.sqb = k.sb(tag + "sqb", [128, 512], BF16)
                P_.lnt = k.sb(tag + "lnt", [128, 512], F32)
                P_.rnt = k.sb(tag + "rnt", [128, 512], F32)
                P_.ktm = [k.sb(tag + "ktm%d" % i, [128, 8, 128], BF16) for i in range(2)]
                P_.br = k.sb(tag + "br", [8, 512], F32)
                P_.ar = k.sb(tag + "ar", [8, 512], F32)
                P_.bt = k.sb(tag + "bt", [8, 512], F32)
                P_.gg = k.sb(tag + "gg", [8, 512], F32)
                P_.Gf = k.sb(tag + "Gf", [8, 512], F32)
                P_.Gb = k.sb(tag + "Gb", [8, 512], F32)
                P_.Gm = k.sb(tag + "Gm", [8, 512], F32)
                P_.gtt = [k.sb(tag + "gtt%d" % i, [128, 16], F32) for i in range(2)]
                return P_

            PP_ = [mk_prep("pa_"), mk_prep("pb_")]

            def prep_chain(par):
                P_ = PP_[par]
                raw = P_.raw
                cv = P_.cv
                sqb = P_.sqb
                lnt = P_.lnt
                rnt = P_.rnt
                ktm = P_.ktm
                br = P_.br
                ar = P_.ar
                bt = P_.bt
                gg = P_.gg
                Gf = P_.Gf
                Gb = P_.Gb
                Gm = P_.Gm
                gtt = P_.gtt
                psb_ = psbs[par]
                bkp = psf[3 * par:3 * par + 3]
                bip = [0]

                def nps():
                    p = bkp[bip[0] % 3]
                    bip[0] += 1
                    return p

                for pi, (st, ln, sq0, sq1) in enumerate(PIECES):
                    if pi % 2 != par:
                        continue
                    rw = raw[pi % 2]
                    cvp = cv[pi % 2]
                    src = projT.t[1024:2560, :].rearrange("(b p) t -> p b t", p=128)
                    k.dma("pool", rw[:, :, 1:ln + 1], src[:, :, st:st + ln], reads=[projT], writes=[rw])
                    if st == sq0:
                        V(lambda e, rw=rw: e.memset(rw[:, :, 0:1], 0.0), [], [rw])
                    else:
                        k.dma("pool", rw[:, :, 0:1], src[:, :, st - 1:st], reads=[projT], writes=[rw], slow=True)
                    if st + ln == sq1:
                        V(lambda e, rw=rw, ln=ln: e.memset(rw[:, :, ln + 1:ln + 2], 0.0), [], [rw])
                    else:
                        k.dma("pool", rw[:, :, ln + 1:ln + 2], src[:, :, st + ln:st + ln + 1], reads=[projT], writes=[rw], slow=True)
                    for blk in range(12):
                        pp = nps()
                        for tap in range(3):
                            PE(lambda e, pp=pp, tap=tap, blk=blk, rw=rw, ln=ln: e.matmul(
                                out=pp[:, 0:ln], lhsT=dg[:, tap * 12 + blk, :], rhs=rw[:, blk, tap:tap + ln],
                                start=(tap == 0), stop=(tap == 2)), [dg, rw], [pp])
                        A(lambda e, pp=pp, blk=blk, cvp=cvp, ln=ln: e.activation(out=cvp[:, blk, 0:ln], in_=pp[:, 0:ln], func=ACT.Silu),
                          [pp], [cvp])
                    for blk in range(8):
                        V(lambda e, blk=blk, cvp=cvp, ln=ln: e.tensor_tensor(out=sqb[:, 0:ln], in0=cvp[:, blk, 0:ln],
                                                                           in1=cvp[:, blk, 0:ln], op=ALU.mult), [cvp], [sqb])
                        pp = nps()
                        PE(lambda e, pp=pp, ln=ln: e.matmul(out=pp[:, 0:ln], lhsT=ones_b[:], rhs=sqb[:, 0:ln], start=True, stop=True),
                           [ones_b, sqb], [pp])
                        A(lambda e, pp=pp, ln=ln: e.activation(out=lnt[:, 0:ln], in_=pp[:, 0:ln], func=ACT.Ln, bias=epsc[:, 0:1]),
                          [pp, epsc], [lnt])
                        A(lambda e, ln=ln: e.activation(out=rnt[:, 0:ln], in_=lnt[:, 0:ln], func=ACT.Exp, scale=-0.5), [lnt], [rnt])
                        scl = (128.0 ** -0.5) if blk < 4 else 1.0
                        V(lambda e, blk=blk, cvp=cvp, ln=ln, scl=scl: e.scalar_tensor_tensor(
                            out=cvp[:, blk, 0:ln], in0=cvp[:, blk, 0:ln], scalar=scl, in1=rnt[:, 0:ln],
                            op0=ALU.mult, op1=ALU.mult), [cvp, rnt], [cvp])
                    k.dma("sp", gqkvT.t[:, st:st + ln].rearrange("(b p) t -> p b t", p=128), cvp[:, :, 0:ln],
                          reads=[cvp], writes=[gqkvT])
                    for tt in range(ln // 128):
                        kt = ktm[tt % 2]
                        for b8 in range(8):
                            PE(lambda e, b8=b8, tt=tt, cvp=cvp: e.transpose(
                                out=psb_[:, b8 * 128:(b8 + 1) * 128], in_=cvp[:, 4 + b8, tt * 128:(tt + 1) * 128],
                                identity=ident_b[:]), [cvp, ident_b], [psb_])
                        V(lambda e, kt=kt: e.tensor_copy(out=kt[:], in_=psb_[:].rearrange("p (b t) -> p b t", b=8)), [psb_], [kt])
                        k.dma("sp", gkv_tm.t[st + tt * 128:st + (tt + 1) * 128, :, :], kt[:], reads=[kt], writes=[gkv_tm])
                    k.dma("sp", br[:, 0:ln], projT.t[3072:3080, st:st + ln], reads=[projT], writes=[br])
                    k.dma("sp", ar[:, 0:ln], projT.t[3080:3088, st:st + ln], reads=[projT], writes=[ar])
                    A(lambda e, ln=ln: e.activation(out=bt[:, 0:ln], in_=br[:, 0:ln], func=ACT.Sigmoid), [br], [bt])
                    A(lambda e, ln=ln: e.activation(out=ar[:, 0:ln], in_=ar[:, 0:ln], func=ACT.Exp, bias=dtb[:, 0:1]), [ar, dtb], [ar])
                    A(lambda e, ln=ln: e.activation(out=ar[:, 0:ln], in_=ar[:, 0:ln], func=ACT.Ln, bias=1.0), [ar], [ar])
                    V(lambda e, ln=ln: e.tensor_scalar(out=gg[:, 0:ln], in0=ar[:, 0:ln], scalar1=nal[:, 0:1], scalar2=None,
                                                       op0=ALU.mult), [ar, nal], [gg])
                    V(lambda e, ln=ln: e.tensor_tensor_scan(out=Gf[:, 0:ln], data0=mask01[:, 0:ln], data1=gg[:, 0:ln],
                                                            initial=0.0, op0=ALU.mult, op1=ALU.add), [mask01, gg], [Gf])
                    nch = ln // 128
                    V(lambda e, ln=ln: e.tensor_tensor(out=Gb[:, 0:ln], in0=gg[:, 0:ln], in1=Gf[:, 0:ln], op=ALU.subtract),
                      [gg, Gf], [Gb])
                    V(lambda e, ln=ln, nch=nch: e.tensor_tensor(
                        out=Gb[:, 0:ln].rearrange("p (c t) -> p c t", t=128), in0=Gb[:, 0:ln].rearrange("p (c t) -> p c t", t=128),
                        in1=Gf[:, 0:ln].rearrange("p (c t) -> p c t", t=128)[:, :, 127:128].to_broadcast([8, nch, 128]),
                        op=ALU.add), [Gb, Gf], [Gb])
                    V(lambda e, ln=ln: e.tensor_scalar(out=Gm[:, 0:ln], in0=Gf[:, 0:ln], scalar1=sel8[:, 0:1], scalar2=None,
                                                       op0=ALU.mult), [Gf, sel8], [Gm])
                    V(lambda e, ln=ln: e.scalar_tensor_tensor(out=Gm[:, 0:ln], in0=Gb[:, 0:ln], scalar=sel8[:, 1:2], in1=Gm[:, 0:ln],
                                                              op0=ALU.mult, op1=ALU.add), [Gb, sel8, Gm], [Gm])
                    k.dma("sp", grow.t[0, :, st:st + ln], bt[:, 0:ln], reads=[bt], writes=[grow])
                    k.dma("sp", grow.t[1, :, st:st + ln], Gm[:, 0:ln], reads=[Gm], writes=[grow])
                    for tt in range(nch):
                        pp = nps()
                        PE(lambda e, pp=pp, tt=tt: e.transpose(out=pp[:, 0:8], in_=bt[:, tt * 128:(tt + 1) * 128],
                                                              identity=ident_f[0:8, 0:8]), [bt, ident_f], [pp])
                        PE(lambda e, pp=pp, tt=tt: e.transpose(out=pp[:, 8:16], in_=Gm[:, tt * 128:(tt + 1) * 128],
                                                              identity=ident_f[0:8, 0:8]), [Gm, ident_f], [pp])
                        g2 = gtt[tt % 2]
                        V(lambda e, pp=pp, g2=g2: e.tensor_copy(out=g2[:], in_=pp[:, 0:16]), [pp], [g2])
                        k.dma("sp", gtm.t[st + tt * 128:st + (tt + 1) * 128, :], g2[:], reads=[g2], writes=[gtm])

            k.interleave([lambda: prep_chain(0), lambda: prep_chain(1)])
            k.barrier()
            esp.close()
            k.es = esg

            def v4(t):
                return t[:].rearrange("p a b -> p (a b)")

            def pv4(pp):
                return pp[:].rearrange("p (a b) -> p a b", a=4)

            class TS:
                pass

            def mk_tiles(tag):
                T = TS()
                for nm, shp, dt_ in (("qk", [128, 8, 128], BF16),
                                     ("kv", [128, 8, 128], BF16), ("rows", [8, 2, 128], F32), ("gt", [128, 16], F32),
                                     ("cols", [128, 24], F32), ("tmpm", [128, 4, 128], F32), ("DTi", [128, 4, 128], F32),
                                     ("EG", [128, 4, 128], F32), ("W1", [128, 4, 128], F32), ("Nf", [128, 4, 128], F32),
                                     ("NT", [128, 4, 128], F32), ("Pm0", [128, 4, 128], F32), ("Pm1", [128, 4, 128], F32),
                                     ("PT0", [128, 4, 128], F32), ("PT1", [128, 4, 128], F32), ("Xf", [128, 4, 128], F32),

                                     ("Xb", [128, 4, 128], BF16), ("NlTb", [128, 4, 128], BF16), ("XTb", [128, 4, 128], BF16),
                                     ("Zb", [128, 4, 128], BF16), ("ktil", [128, 4, 128], BF16), ("khat", [128, 4, 128], BF16),
                                     ("bv", [128, 4, 128], BF16), ("nwT", [128, 4, 128], BF16), ("vn", [128, 4, 128], BF16),
                                     ("AT", [128, 4, 128], BF16), ("qs", [128, 4, 128], BF16), ("oT", [128, 4, 128], F32)):
                    setattr(T, nm, k.sb(nm + tag, shp, dt_))
                return T

            esm2 = contextlib.ExitStack()
            k.es = esm2
            TT = [[mk_tiles("_c%d%d" % (d_, p_)) for p_ in range(2)] for d_ in range(2)]
            SS = [(k.sb("S_d%d" % d_, [128, 4, 128], F32), k.sb("Sb_d%d" % d_, [128, 4, 128], BF16)) for d_ in range(2)]
            psf7 = Buf(psb.t[:].bitcast(F32), "psb_as_f32")
            allbanks = psf[0:6] + [psf6, psf7]
            progB = [0, 0]
            POS = {}
            for d_ in range(2):
                lst = []
                for si, (sq0, sl) in enumerate(SEQS):
                    nchunk = sl // 128
                    order = list(range(nchunk)) if d_ == 0 else list(range(nchunk - 1, -1, -1))
                    for oi, c in enumerate(order):
                        lst.append((si, sq0 + c * 128, oi == 0, oi == nchunk - 1))
                POS[d_] = lst

            def chain(d, par):
                T = TT[d][par]
                S, Sb = SS[d]
                qk, kv, rows, gt, cols, tmpm, DTi, EG, W1, Nf, NT = (T.qk, T.kv, T.rows, T.gt, T.cols, T.tmpm,
                                                                    T.DTi, T.EG, T.W1, T.Nf, T.NT)
                Pm = [T.Pm0, T.Pm1]
                PT = [T.PT0, T.PT1]
                Xf, Xb, ktil, khat, bv, nwT, vn, AT, qs, oT = (T.Xf, T.Xb, T.ktil, T.khat, T.bv, T.nwT, T.vn, T.AT, T.qs, T.oT)
                banks = allbanks[2 * (2 * d + par):2 * (2 * d + par) + 2]
                bi = [0]

                def nps():
                    p = banks[bi[0] % 2]
                    bi[0] += 1
                    return p

                last = 127 if d == 0 else 0
                for pos, (si, t0, first_, last_) in enumerate(POS[d]):
                    if pos % 2 != par:
                        continue
                    if True:
                        k.dma("sp", qk[:], gqkvT.t[0:1024, t0:t0 + 128].rearrange("(b p) t -> p b t", p=128),
                              reads=[gqkvT], writes=[qk])
                        k.dma("sp", kv[:], gkv_tm.t[t0:t0 + 128, :, :], reads=[gkv_tm], writes=[kv])
                        k.dma("sp", rows[:], grow.t[:, :, t0:t0 + 128].rearrange("a r t -> r a t"), reads=[grow], writes=[rows])
                        k.dma("sp", gt[:], gtm.t[t0:t0 + 128, :], reads=[gtm], writes=[gt])
                        pGB = nps()
                        pBB = nps()
                        for h in range(4):
                            PE(lambda e, h=h: e.matmul(out=pGB[:, h * 128:(h + 1) * 128], lhsT=selr[:, 4 * d + h, :],
                                                       rhs=rows[:, 1, :], start=True, stop=True), [selr, rows], [pGB])
                        for h in range(4):
                            PE(lambda e, h=h: e.matmul(out=pBB[:, h * 128:(h + 1) * 128], lhsT=selr[:, 4 * d + h, :],
                                                       rhs=rows[:, 0, :], start=True, stop=True), [selr, rows], [pBB])
                        V(lambda e: e.tensor_scalar(out=cols[:, 0:4], in0=gt[:, 8 + 4 * d:12 + 4 * d], scalar1=-1.0,
                                                    scalar2=None, op0=ALU.mult), [gt], [cols])
                        V(lambda e: e.tensor_copy(out=cols[:, 4:8], in_=pv4(pGB)[:, :, last]), [pGB], [cols])
                        A(lambda e: e.activation(out=cols[:, 8:12], in_=cols[:, 4:8], func=ACT.Exp), [cols], [cols])
                        A(lambda e: e.activation(out=cols[:, 12:16], in_=gt[:, 8 + 4 * d:12 + 4 * d], func=ACT.Exp),
                          [gt], [cols])
                        V(lambda e: e.tensor_tensor(out=cols[:, 12:16], in0=cols[:, 12:16], in1=gt[:, 4 * d:4 * d + 4],
                                                    op=ALU.mult), [cols, gt], [cols])
                        V(lambda e: e.tensor_tensor(out=cols[:, 16:20], in0=cols[:, 4:8], in1=cols[:, 0:4], op=ALU.add),
                          [cols], [cols])
                        A(lambda e: e.activation(out=cols[:, 16:20], in_=cols[:, 16:20], func=ACT.Exp), [cols], [cols])
                        V(lambda e: e.tensor_tensor(out=v4(tmpm), in0=pGB[:, :], in1=v4(cm[d]), op=ALU.add), [pGB, cm[d]], [tmpm])
                        for h in range(4):
                            A(lambda e, h=h: e.activation(out=DTi[:, h, :], in_=tmpm[:, h, :], func=ACT.Exp,
                                                          bias=cols[:, h:h + 1]), [tmpm, cols], [DTi])
                        A(lambda e: e.activation(out=v4(EG), in_=pGB[:, :], func=ACT.Exp), [pGB], [EG])
                        V(lambda e: e.tensor_tensor(out=v4(W1), in0=v4(DTi), in1=v4(offd), op=ALU.mult), [DTi, offd], [W1])
                        V(lambda e: e.tensor_tensor(out=v4(W1), in0=pBB[:, :], in1=v4(W1), op=ALU.mult), [pBB, W1], [W1])
                        pKK = nps()
                        for h in range(4):
                            PE(lambda e, h=h: e.matmul(out=pKK[:, h * 128:(h + 1) * 128], lhsT=qk[:, 4 + h, :],
                                                       rhs=qk[:, 4 + h, :], start=True, stop=True), [qk], [pKK])
                        V(lambda e: e.scalar_tensor_tensor(out=v4(Nf), in0=pKK[:, :], scalar=-1.0, in1=v4(W1),
                                                           op0=ALU.mult, op1=ALU.mult), [pKK, W1], [Nf])
                        pp = nps()
                        for h in range(4):
                            PE(lambda e, h=h, pp=pp: e.transpose(out=pp[:, h * 128:(h + 1) * 128], in_=Nf[:, h, :],
                                                                 identity=ident_f[:]), [Nf, ident_f], [pp])
                        A(lambda e, pp=pp: e.copy(out=v4(NT), in_=pp[:, :]), [pp], [NT])
                        Pc, Pn = Pm
                        Tc, Tn = PT
                        V(lambda e, Pc=Pc: e.tensor_tensor(out=v4(Pc), in0=v4(Nf), in1=v4(lvm[0]), op=ALU.mult), [Nf, lvm[0]], [Pc])
                        V(lambda e, Tc=Tc: e.tensor_tensor(out=v4(Tc), in0=v4(NT), in1=v4(lvm[0]), op=ALU.mult), [NT, lvm[0]], [Tc])
                        V(lambda e, Pc=Pc: e.tensor_tensor(out=v4(Xf), in0=v4(Pc), in1=v4(idf4), op=ALU.add), [Pc, idf4], [Xf])
                        for s_ in range(1, 4):
                            if s_ < 3:
                                pP = nps()
                                for h in range(4):
                                    PE(lambda e, h=h, pP=pP, Tc=Tc, Pc=Pc: e.matmul(
                                        out=pP[:, h * 128:(h + 1) * 128], lhsT=Tc[:, h, :], rhs=Pc[:, h, :],
                                        start=True, stop=True), [Tc, Pc], [pP])
                            pT = nps()
                            for h in range(4):
                                PE(lambda e, h=h, pT=pT, Tc=Tc, Pc=Pc: e.matmul(
                                    out=pT[:, h * 128:(h + 1) * 128], lhsT=Pc[:, h, :], rhs=Tc[:, h, :],
                                    start=True, stop=True), [Tc, Pc], [pT])
                            if s_ < 3:
                                A(lambda e, pP=pP, Pn=Pn: e.copy(out=v4(Pn), in_=pP[:, :]), [pP], [Pn])
                            V(lambda e, pT=pT, Tn=Tn: e.tensor_copy(out=v4(Tn), in_=pT[:, :]), [pT], [Tn])
                            pX = nps()
                            for h in range(4):
                                PE(lambda e, h=h, pX=pX, Tn=Tn: e.matmul(
                                    out=pX[:, h * 128:(h + 1) * 128], lhsT=Tn[:, h, :], rhs=Xf[:, h, :],
                                    start=True, stop=True), [Tn, Xf], [pX])
                            V(lambda e, pX=pX: e.tensor_tensor(out=v4(Xf), in0=pX[:, :], in1=v4(Xf), op=ALU.add), [pX, Xf], [Xf])
                            Pc, Pn, Tc, Tn = Pn, Pc, Tn, Tc
                        for lv in range(1, 4):
                            V(lambda e, lv=lv: e.tensor_tensor(out=v4(T.NlTb), in0=v4(NT), in1=v4(lvm[lv]), op=ALU.mult),
                              [NT, lvm[lv]], [T.NlTb])
                            A(lambda e: e.copy(out=Xb[:], in_=Xf[:]), [Xf], [Xb])
                            pp = nps()
                            ppb = pp.t[:].bitcast(BF16)
                            for h in range(4):
                                PE(lambda e, h=h, ppb=ppb: e.transpose(out=ppb[:, h * 128:(h + 1) * 128], in_=Xb[:, h, :],
                                                                       identity=ident_b[:]), [Xb, ident_b], [pp])
                            A(lambda e, ppb=ppb: e.copy(out=v4(T.XTb), in_=ppb[:, 0:512]), [pp], [T.XTb])
                            pZ = nps()
                            for h in range(4):
                                PE(lambda e, h=h, pZ=pZ: e.matmul(out=pZ[:, h * 128:(h + 1) * 128], lhsT=T.NlTb[:, h, :],
                                                                  rhs=Xb[:, h, :], start=True, stop=True), [T.NlTb, Xb], [pZ])
                            A(lambda e, pZ=pZ: e.copy(out=v4(T.Zb), in_=pZ[:, :]), [pZ], [T.Zb])
                            pY = nps()
                            for h in range(4):
                                PE(lambda e, h=h, pY=pY: e.matmul(out=pY[:, h * 128:(h + 1) * 128], lhsT=T.XTb[:, h, :],
                                                                  rhs=T.Zb[:, h, :], start=True, stop=True), [T.XTb, T.Zb], [pY])
                            V(lambda e, pY=pY: e.tensor_tensor(out=v4(Xf), in0=v4(Xf), in1=pY[:, :], op=ALU.add),
                              [pY, Xf], [Xf])
                        A(lambda e: e.copy(out=Xb[:], in_=Xf[:]), [Xf], [Xb])
                        X = Xb
                        V(lambda e: e.tensor_tensor(out=ktil[:], in0=kv[:, 0:4, :],
                                                    in1=cols[:, 12:16].unsqueeze(2).to_broadcast([128, 4, 128]), op=ALU.mult),
                          [kv, cols], [ktil])
                        V(lambda e: e.tensor_tensor(out=khat[:], in0=kv[:, 0:4, :],
                                                    in1=cols[:, 16:20].unsqueeze(2).to_broadcast([128, 4, 128]), op=ALU.mult),
                          [kv, cols], [khat])
                        V(lambda e: e.tensor_tensor(out=bv[:], in0=kv[:, 4:8, :],
                                                    in1=gt[:, 4 * d:4 * d + 4].unsqueeze(2).to_broadcast([128, 4, 128]), op=ALU.mult),
                          [kv, gt], [bv])
                        pW = nps()
                        for h in range(4):
                            PE(lambda e, h=h, X=X: e.matmul(out=pW[:, h * 128:(h + 1) * 128], lhsT=ktil[:, h, :], rhs=X[:, h, :],
                                                            start=True, stop=True), [ktil, X], [pW])
                        V(lambda e: e.tensor_scalar(out=v4(nwT), in0=pW[:, :], scalar1=-1.0, scalar2=None, op0=ALU.mult), [pW], [nwT])
                        pQK = nps()
                        for h in range(4):
                            PE(lambda e, h=h: e.matmul(out=pQK[:, h * 128:(h + 1) * 128], lhsT=qk[:, 4 + h, :], rhs=qk[:, h, :],
                                                       start=True, stop=True), [qk], [pQK])
                        V(lambda e: e.tensor_tensor(out=v4(AT), in0=pQK[:, :], in1=v4(DTi), op=ALU.mult), [pQK, DTi], [AT])
                        V(lambda e: e.tensor_tensor(out=qs[:], in0=qk[:, 0:4, :], in1=EG[:], op=ALU.mult), [qk, EG], [qs])
                        while progB[d] != pos:
                            k._yield(force=True)
                        if first_:
                            if si < NPS:
                                V(lambda e: e.memset(S[:], 0.0), [], [S])
                            else:
                                k.dma("sp", S[:], s0.t[layer, d].rearrange("h p e -> p h e"), reads=[s0], writes=[S])
                            A(lambda e: e.copy(out=Sb[:], in_=S[:]), [S], [Sb])
                        pV = nps()
                        for h in range(4):
                            PE(lambda e, h=h, X=X: e.matmul(out=pV[:, h * 128:(h + 1) * 128], lhsT=X[:, h, :], rhs=bv[:, h, :],
                                                            start=True, stop=False), [X, bv], [pV])
                            PE(lambda e, h=h: e.matmul(out=pV[:, h * 128:(h + 1) * 128], lhsT=nwT[:, h, :], rhs=Sb[:, h, :],
                                                       start=False, stop=True), [nwT, Sb], [pV])
                        A(lambda e: e.copy(out=v4(vn), in_=pV[:, :]), [pV], [vn])
                        pO = nps()
                        for h in range(4):
                            PE(lambda e, h=h: e.matmul(out=pO[:, h * 128:(h + 1) * 128], lhsT=Sb[:, h, :], rhs=qs[:, h, :],
                                                       start=True, stop=False), [Sb, qs], [pO])
                            PE(lambda e, h=h: e.matmul(out=pO[:, h * 128:(h + 1) * 128], lhsT=vn[:, h, :], rhs=AT[:, h, :],
                                                       start=False, stop=True), [vn, AT], [pO])
                        A(lambda e: e.copy(out=v4(oT), in_=pO[:, :]), [pO], [oT])
                        pS = nps()
                        for h in range(4):
                            PE(lambda e, h=h: e.matmul(out=pS[:, h * 128:(h + 1) * 128], lhsT=khat[:, h, :], rhs=vn[:, h, :],
                                                       start=True, stop=True), [khat, vn], [pS])
                        V(lambda e: e.tensor_tensor(out=S[:], in0=S[:], in1=cols[:, 8:12].unsqueeze(2).to_broadcast([128, 4, 128]),
                                                    op=ALU.mult), [S, cols], [S])
                        V(lambda e: e.tensor_tensor(out=v4(S), in0=pS[:, :], in1=v4(S), op=ALU.add), [pS, S], [S])
                        A(lambda e: e.copy(out=Sb[:], in_=S[:]), [S], [Sb])
                        k.dma("pool", odT[d].t[:, t0:t0 + 128].rearrange("(h p) t -> p h t", p=128), oT[:], reads=[oT], writes=[odT[d]])
                        if last_ and si < NPS:
                            k.dma("sp", ngdn_out.t[si, layer, d].rearrange("h p e -> p h e"), S[:], reads=[S], writes=[ngdn_out])
                        progB[d] = pos + 1

            k.interleave([lambda: chain(0, 0), lambda: chain(1, 0), lambda: chain(0, 1), lambda: chain(1, 1)])
            k.barrier()
            esm2.close()
            k.es = esg

            of = [k.sb("of%d" % i, [128, 4, 128], F32) for i in range(2)]
            o2 = [k.sb("o2_%d" % i, [128, 4, 128], F32) for i in range(2)]
            zt = [k.sb("zt%d" % i, [128, 4, 128], F32) for i in range(2)]
            osq = k.sb("osq", [128, 4, 128], BF16)
            ob = [k.sb("ob%d" % i, [128, 4, 128], BF16) for i in range(2)]
            for ci in range(NTOK // 128):
                t0 = ci * 128
                a_, b_, z_, r_ = of[ci % 2], o2[ci % 2], zt[ci % 2], ob[ci % 2]
                k.dma("sp", a_[:], odT[0].t[:, t0:t0 + 128].rearrange("(h p) t -> p h t", p=128), reads=[odT[0]], writes=[a_])
                k.dma("sp", b_[:], odT[1].t[:, t0:t0 + 128].rearrange("(h p) t -> p h t", p=128), reads=[odT[1]], writes=[b_])
                k.dma("sp", z_[:], projT.t[2560:3072, t0:t0 + 128].rearrange("(h p) t -> p h t", p=128), reads=[projT], writes=[z_])
                V(lambda e, a_=a_, b_=b_: e.tensor_tensor(out=b_[:], in0=b_[:], in1=a_[:], op=ALU.add), [a_, b_], [b_])
                V(lambda e, b_=b_: e.tensor_tensor(out=osq[:], in0=b_[:], in1=b_[:], op=ALU.mult), [b_], [osq])
                pN = nps()
                PE(lambda e, pN=pN: e.matmul(out=pN[:, :], lhsT=ones_b[:], rhs=v4(osq), start=True, stop=True), [ones_b, osq], [pN])
                A(lambda e, pN=pN, a_=a_: e.activation(out=v4(a_), in_=pN[:, :], func=ACT.Ln, scale=1.0 / 128.0, bias=epsc[:, 0:1]),
                  [pN, epsc], [a_])
                A(lambda e, a_=a_: e.activation(out=v4(a_), in_=v4(a_), func=ACT.Exp, scale=-0.5), [a_], [a_])
                A(lambda e, z_=z_: e.activation(out=z_[:], in_=z_[:], func=ACT.Silu), [z_], [z_])
                V(lambda e, a_=a_, b_=b_: e.tensor_tensor(out=b_[:], in0=b_[:], in1=a_[:], op=ALU.mult), [a_, b_], [b_])
                V(lambda e, b_=b_, z_=z_, r_=r_: e.scalar_tensor_tensor(out=r_[:], in0=b_[:], scalar=gnw[:, 0:1], in1=z_[:],
                                                                     op0=ALU.mult, op1=ALU.mult), [b_, gnw, z_], [r_])
                k.dma("pool", mixT.t[512:1024, t0:t0 + 128].rearrange("(h p) t -> p h t", p=128), r_[:], reads=[r_], writes=[mixT])
            k.barrier()
        k.es = es

    TB = 32
    NBLK = NTOK // TB

    def s5_phase(layer):
        with contextlib.ExitStack() as ess:
            k.es = ess
            uT = k.sb("uT", [128, 4, NTOK], BF16)
            for c in range(4):
                for hh in range(2):
                    k.dma("pool", uT[:, c, hh * 2560:(hh + 1) * 2560], projT.t[c * 128:(c + 1) * 128, hh * 2560:(hh + 1) * 2560],
                          reads=[projT], writes=[uT])
            Hsb = [k.sb("Hsb%d" % d, [128, 32, NBLK], BF16) for d in range(2)]
            swapm = k.sb("swapm", [128, 128], F32)
            k.dma("sp", swapm[:], swap_in.t[:, :], reads=[swap_in], writes=[swapm])

            with contextlib.ExitStack() as est:
                k.es = est
                def load_T(src, name):
                    raw_ = k.sb(name + "_raw", [64, 128], F32)
                    v = src.t[layer].rearrange("d g n -> (d g) n")
                    k.dma("sp", raw_[:, 0:64], v, reads=[src], writes=[raw_])
                    k.dma("sp", raw_[:, 64:128], v, reads=[src], writes=[raw_])
                    pp = nps()
                    PE(lambda e: e.transpose(out=pp[:, 0:64], in_=raw_[:], identity=ident_f[0:64, 0:64]), [raw_, ident_f], [pp])
                    t_ = k.sb(name, [128, 64], F32)
                    V(lambda e: e.tensor_copy(out=t_[:], in_=pp[:, 0:64]), [pp], [t_])
                    return t_
                LR = load_T(lam_re, "LR")
                LI = load_T(lam_im, "LI")
                DT = k.sb("DT", [128, 64], F32)
                k.dma("sp", DT[:].rearrange("p (d g) -> p d g", d=2),
                      log_dt.t[layer:layer + 1, :, :].to_broadcast([128, 2, 32]), reads=[log_dt], writes=[DT])
                A(lambda e: e.activation(out=DT[:].rearrange("p (d g) -> p d g", d=2), in_=DT[:].rearrange("p (d g) -> p d g", d=2), func=ACT.Exp), [DT], [DT])
                V(lambda e: e.tensor_scalar(out=LR[:], in0=LR[:], scalar1=-1e-4, scalar2=None, op0=ALU.min), [LR], [LR])
                mag = k.sb("mag", [128, 64], F32)
                th = k.sb("th", [128, 2, 64], F32)
                cs = k.sb("cs", [128, 2, 64], F32)
                V(lambda e: e.tensor_tensor(out=mag[:], in0=LR[:], in1=DT[:], op=ALU.mult), [LR, DT], [mag])
                A(lambda e: e.activation(out=mag[:], in_=mag[:], func=ACT.Exp), [mag], [mag])
                V(lambda e: e.tensor_tensor(out=th[:, 0, :], in0=LI[:], in1=DT[:], op=ALU.mult), [LI, DT], [th])
                V(lambda e: e.tensor_scalar(out=th[:, 1, :], in0=th[:, 0, :], scalar1=math.pi / 2, scalar2=None, op0=ALU.add), [th], [th])
                k.barrier()
                sin_reduced(cs[:], th[:], [128, 2, 64], "s5")
                k.barrier()
                QR = k.sb("QR", [128, 64, 33], F32)
                QI = k.sb("QI", [128, 64, 33], F32)
                V(lambda e: e.memset(QR[:, :, 0:1], 1.0), [], [QR])
                V(lambda e: e.memset(QI[:, :, 0:1], 0.0), [], [QI])
                V(lambda e: e.tensor_tensor(out=QR[:, :, 1], in0=mag[:], in1=cs[:, 1, :], op=ALU.mult), [mag, cs], [QR])
                V(lambda e: e.tensor_tensor(out=QI[:, :, 1], in0=mag[:], in1=cs[:, 0, :], op=ALU.mult), [mag, cs], [QI])
                SR = k.sb("SR", [128, 64], F32)
                SI = k.sb("SI", [128, 64], F32)
                S2 = k.sb("S2", [128, 64], F32)
                T1 = k.sb("T1", [128, 64, 16], F32)
                T2 = k.sb("T2", [128, 64, 16], F32)
                V(lambda e: e.tensor_copy(out=SR[:], in_=QR[:, :, 1]), [QR], [SR])
                V(lambda e: e.tensor_copy(out=SI[:], in_=QI[:, :, 1]), [QI], [SI])

                def cmul_bc(oR, oI, aR, aI, bR, bI, w):
                    bRb = bR.unsqueeze(2).to_broadcast([128, 64, w])
                    bIb = bI.unsqueeze(2).to_broadcast([128, 64, w])
                    V(lambda e: e.tensor_tensor(out=T1[:, :, 0:w], in0=aR, in1=bRb, op=ALU.mult), [QR, QI, SR, SI, GRt, GIt, FRt, FIt], [T1])
                    V(lambda e: e.tensor_tensor(out=T2[:, :, 0:w], in0=aI, in1=bIb, op=ALU.mult), [QR, QI, SR, SI, GRt, GIt, FRt, FIt], [T2])
                    V(lambda e: e.tensor_tensor(out=oR, in0=T1[:, :, 0:w], in1=T2[:, :, 0:w], op=ALU.subtract), [T1, T2], [QR, GRt])
                    V(lambda e: e.tensor_tensor(out=T1[:, :, 0:w], in0=aR, in1=bIb, op=ALU.mult), [QR, QI, SR, SI, GRt, GIt, FRt, FIt], [T1])
                    V(lambda e: e.tensor_tensor(out=T2[:, :, 0:w], in0=aI, in1=bRb, op=ALU.mult), [QR, QI, SR, SI, GRt, GIt, FRt, FIt], [T2])
                    V(lambda e: e.tensor_tensor(out=oI, in0=T1[:, :, 0:w], in1=T2[:, :, 0:w], op=ALU.add), [T1, T2], [QI, GIt])

                GRt = k.sb("GR", [128, 64, 32], F32)
                GIt = k.sb("GI", [128, 64, 32], F32)
                FRt = k.sb("FR", [128, 64], F32)
                FIt = k.sb("FI", [128, 64], F32)
                w = 2
                while w <= 32:
                    V(lambda e: e.tensor_tensor(out=S2[:], in0=SR[:], in1=SI[:], op=ALU.mult), [SR, SI], [S2])
                    V(lambda e: e.tensor_tensor(out=SR[:], in0=SR[:], in1=SR[:], op=ALU.mult), [SR], [SR])
                    V(lambda e: e.tensor_tensor(out=SI[:], in0=SI[:], in1=SI[:], op=ALU.mult), [SI], [SI])
                    V(lambda e: e.tensor_tensor(out=SR[:], in0=SR[:], in1=SI[:], op=ALU.subtract), [SR, SI], [SR])
                    V(lambda e: e.tensor_scalar(out=SI[:], in0=S2[:], scalar1=2.0, scalar2=None, op0=ALU.mult), [S2], [SI])
                    if w < 32:
                        cmul_bc(QR[:, :, w:2 * w], QI[:, :, w:2 * w], QR[:, :, 0:w], QI[:, :, 0:w], SR[:], SI[:], w)
                    else:
                        V(lambda e: e.tensor_copy(out=QR[:, :, 32], in_=SR[:]), [SR], [QR])
                        V(lambda e: e.tensor_copy(out=QI[:, :, 32], in_=SI[:]), [SI], [QI])
                    w *= 2
                nr = k.sb("nr", [128, 64], F32)
                den = k.sb("den", [128, 64], F32)
                V(lambda e: e.tensor_scalar(out=nr[:], in0=QR[:, :, 1], scalar1=-1.0, scalar2=None, op0=ALU.add), [QR], [nr])
                V(lambda e: e.tensor_tensor(out=den[:], in0=LR[:], in1=LR[:], op=ALU.mult), [LR], [den])
                V(lambda e: e.tensor_tensor(out=S2[:], in0=LI[:], in1=LI[:], op=ALU.mult), [LI], [S2])
                V(lambda e: e.tensor_tensor(out=den[:], in0=den[:], in1=S2[:], op=ALU.add), [den, S2], [den])
                V(lambda e: e.reciprocal(out=den[:], in_=den[:]), [den], [den])
                V(lambda e: e.tensor_tensor(out=FRt[:], in0=nr[:], in1=LR[:], op=ALU.mult), [nr, LR], [FRt])
                V(lambda e: e.tensor_tensor(out=S2[:], in0=QI[:, :, 1], in1=LI[:], op=ALU.mult), [QI, LI], [S2])
                V(lambda e: e.tensor_tensor(out=FRt[:], in0=FRt[:], in1=S2[:], op=ALU.add), [FRt, S2], [FRt])
                V(lambda e: e.tensor_tensor(out=FRt[:], in0=FRt[:], in1=den[:], op=ALU.mult), [FRt, den], [FRt])
                V(lambda e: e.tensor_tensor(out=FIt[:], in0=QI[:, :, 1], in1=LR[:], op=ALU.mult), [QI, LR], [FIt])
                V(lambda e: e.tensor_tensor(out=S2[:], in0=nr[:], in1=LI[:], op=ALU.mult), [nr, LI], [S2])
                V(lambda e: e.tensor_tensor(out=FIt[:], in0=FIt[:], in1=S2[:], op=ALU.subtract), [FIt, S2], [FIt])
                V(lambda e: e.tensor_tensor(out=FIt[:], in0=FIt[:], in1=den[:], op=ALU.mult), [FIt, den], [FIt])
                for h2 in range(2):
                    ks = slice(h2 * 16, (h2 + 1) * 16)
                    cmul_bc(GRt[:, :, ks], GIt[:, :, ks], QR[:, :, ks], QI[:, :, ks], FRt[:], FIt[:], 16)
                AR = k.sb("ARt", [128, 64], F32)
                AIs = k.sb("AIs", [128, 64], F32)
                V(lambda e: e.tensor_copy(out=AR[:], in_=QR[:, :, 32]), [QR], [AR])
                V(lambda e: e.tensor_scalar(out=AIs[0:64, :], in0=QI[0:64, :, 32], scalar1=-1.0, scalar2=None, op0=ALU.mult), [QI], [AIs])
                V(lambda e: e.tensor_copy(out=AIs[64:128, :], in_=QI[64:128, :, 32]), [QI], [AIs])
                Ba = k.sb("Ba", [128, 64, 16], F32)
                Bb = k.sb("Bb", [128, 64, 16], F32)
                bre_v = b_re.t[layer].rearrange("d g n q -> n (d g) q")
                bim_v = b_im.t[layer].rearrange("d g n q -> n (d g) q")
                k.dma("sp", Ba[0:64, :, :], bre_v, reads=[b_re], writes=[Ba])
                k.dma("sp", Ba[64:128, :, :], bim_v, reads=[b_im], writes=[Ba])
                k.dma("sp", Bb[0:64, :, :], bim_v, reads=[b_im], writes=[Bb])
                k.dma("sp", Bb[64:128, :, :], bre_v, reads=[b_re], writes=[Bb])
                V(lambda e: e.tensor_scalar(out=Bb[0:64, :, :], in0=Bb[0:64, :, :], scalar1=-1.0, scalar2=None, op0=ALU.mult), [Bb], [Bb])
                Ca = k.sb("Ca", [128, 64, 16], BF16)
                Cb = k.sb("Cb", [128, 64, 16], BF16)
                CaB = Ca
                esc_ = contextlib.ExitStack()
                k.es = esc_
                CaT = k.sb("CaT", [16, 16, 128], F32)
                CbT = k.sb("CbT", [16, 16, 128], F32)
                cre_v = c_re.t[layer].rearrange("d g q n -> q (d g) n")
                cim_v = c_im.t[layer].rearrange("d g q n -> q (d g) n")
                for q4 in range(4):
                    qs_ = slice(q4 * 16, (q4 + 1) * 16)
                    k.dma("sp", CaT[:, :, 0:64], cre_v[:, qs_, :], reads=[c_re], writes=[CaT])
                    k.dma("sp", CaT[:, :, 64:128], cim_v[:, qs_, :], reads=[c_im], writes=[CaT])
                    k.dma("sp", CbT[:, :, 0:64], cim_v[:, qs_, :], reads=[c_im], writes=[CbT])
                    k.dma("sp", CbT[:, :, 64:128], cre_v[:, qs_, :], reads=[c_re], writes=[CbT])
                    V(lambda e: e.tensor_scalar(out=CaT[:, :, 64:128], in0=CaT[:, :, 64:128], scalar1=-1.0, scalar2=None,
                                                op0=ALU.mult), [CaT], [CaT])
                    V(lambda e: e.tensor_scalar(out=CbT[:], in0=CbT[:], scalar1=-1.0, scalar2=None, op0=ALU.mult), [CbT], [CbT])
                    for (srcT, dstC) in ((CaT, Ca), (CbT, Cb)):
                        pp = nps()
                        for i in range(16):
                            PE(lambda e, pp=pp, i=i, srcT=srcT: e.transpose(
                                out=pp[:, i * 16:(i + 1) * 16], in_=srcT[:, i, :], identity=ident_f[0:16, 0:16]),
                                [srcT, ident_f], [pp])
                        V(lambda e, pp=pp, qs_=qs_, dstC=dstC: e.tensor_copy(
                            out=dstC[:, qs_, :].rearrange("p a b -> p (a b)"), in_=pp[:, 0:256]), [pp], [dstC])
                k.barrier()
                esc_.close()
                k.es = est

                class TS5:
                    pass

                def mk_s5(d_):
                    C_ = TS5()
                    tag = "_s%d" % d_
                    C_.Xp = k.sb("Xp" + tag, [128, 32, 128], BF16)
                    V(lambda e: e.memset(C_.Xp[:], 0.0), [], [C_.Xp])
                    C_.XT1 = Buf(T1.t[:, 32 * d_:32 * d_ + 32, :], "XT1" + tag)
                    C_.XT2 = Buf(T2.t[:, 32 * d_:32 * d_ + 32, :], "XT2" + tag)
                    C_.WB = [k.sb("WB" + tag, [128, 32, 128], BF16)] * 2
                    C_.WCg = [k.sb("WCg%d" % i + tag, [128, 32, 32], BF16) for i in range(2)]
                    for i in range(2):
                        V(lambda e, i=i: e.memset(C_.WCg[i][:], 0.0), [], [C_.WCg[i]])
                    C_.Kt = k.sb("Kt" + tag, [128, 32, 128], BF16)
                    C_.Et = k.sb("Et" + tag, [128, 32, NBLK], BF16)
                    C_.cur = k.sb("cur" + tag, [128, 32, 4], F32)
                    C_.curs = k.sb("curs" + tag, [128, 32, 1], F32)
                    C_.tA = k.sb("tA" + tag, [128, 32, 4], F32)
                    C_.tB = k.sb("tB" + tag, [128, 32, 4], F32)
                    C_.hin = k.sb("hin" + tag, [32, 128], F32)
                    C_.fin = k.sb("fin" + tag, [32, 128], F32)
                    return C_

                CS5 = [mk_s5(0), mk_s5(1)]
                psbs5 = [psb, psb2]
                def s5_chain(d):
                    C_ = CS5[d]
                    Xp, XT1, XT2, WB, WCg, Kt, Et, cur, curs, tA, tB, hin, fin = (C_.Xp, C_.XT1, C_.XT2, C_.WB, C_.WCg, C_.Kt, C_.Et,
                                                                                 C_.cur, C_.curs, C_.tA, C_.tB, C_.hin, C_.fin)
                    psb_ = psbs5[d]
                    bk5 = psf[3 * d:3 * d + 3]
                    bi5 = [0]

                    def nps():
                        p = bk5[bi5[0] % 3]
                        bi5[0] += 1
                        return p

                    prev_slot = [None]
                    for g in range(32):
                        dg = d * 32 + g
                        c = g // 8
                        slot = g % 8
                        pr = slot // 2
                        if g % 8 == 0:
                            V(lambda e: e.memset(Kt[:], 0.0), [], [Kt])
                        if prev_slot[0] is not None:
                            ps_ = prev_slot[0]
                            V(lambda e, ps_=ps_: e.memset(Xp[:, :, ps_ * 16:(ps_ + 1) * 16], 0.0), [], [Xp])
                        prev_slot[0] = slot
                        V(lambda e, dg=dg: e.tensor_tensor(
                            out=XT1[:], in0=GRt[:, dg, :].unsqueeze(2).to_broadcast([128, 32, 16]),
                            in1=Ba[:, dg, :].unsqueeze(1).to_broadcast([128, 32, 16]), op=ALU.mult), [GRt, Ba], [XT1])
                        V(lambda e, dg=dg: e.tensor_tensor(
                            out=XT2[:], in0=GIt[:, dg, :].unsqueeze(2).to_broadcast([128, 32, 16]),
                            in1=Bb[:, dg, :].unsqueeze(1).to_broadcast([128, 32, 16]), op=ALU.mult), [GIt, Bb], [XT2])
                        V(lambda e, slot=slot: e.tensor_tensor(out=Xp[:, :, slot * 16:(slot + 1) * 16], in0=XT1[:], in1=XT2[:],
                                                               op=ALU.add), [XT1, XT2], [Xp])
                        pk = nps()
                        for tau in range(32):
                            PE(lambda e, tau=tau, pk=pk, dg=dg: e.matmul(out=pk[:, tau * 16:(tau + 1) * 16], lhsT=Xp[:, tau, :],
                                                                         rhs=CaB[:, dg, :], start=True, stop=True), [Xp, CaB], [pk])
                        A(lambda e, pk=pk, pr=pr, slot=slot: e.copy(
                            out=Kt[32 * pr:32 * pr + 32, :, slot * 16:(slot + 1) * 16],
                            in_=pk[32 * pr:32 * pr + 32, :].rearrange("p (t q) -> p t q", q=16)), [pk], [Kt])
                        if g % 8 == 7:
                            k.dma("sp", ktab.t[d, c, :, :], Kt[:].rearrange("p a b -> p (a b)"), reads=[Kt], writes=[ktab])
                        wc = WCg[g % 2]
                        sl2 = g % 2
                        V(lambda e, dg=dg: e.tensor_tensor(
                            out=XT1[:], in0=QR[:, dg, 1:33].unsqueeze(2).to_broadcast([128, 32, 16]),
                            in1=Ca[:, dg, :].unsqueeze(1).to_broadcast([128, 32, 16]), op=ALU.mult), [QR, Ca], [XT1])
                        V(lambda e, dg=dg: e.tensor_tensor(
                            out=XT2[:], in0=QI[:, dg, 1:33].unsqueeze(2).to_broadcast([128, 32, 16]),
                            in1=Cb[:, dg, :].unsqueeze(1).to_broadcast([128, 32, 16]), op=ALU.mult), [QI, Cb], [XT2])
                        V(lambda e, wc=wc, sl2=sl2: e.tensor_tensor(out=wc[:, :, sl2 * 16:(sl2 + 1) * 16], in0=XT1[:], in1=XT2[:],
                                                                   op=ALU.add), [XT1, XT2], [wc])
                        k.dma("sp", wctab.t[dg, :, :], wc[:].rearrange("p a b -> p (a b)"), reads=[wc], writes=[wctab])
                        wb = WB[g % 2]
                        for q8 in range(4):
                            for kk in range(8):
                                kx = q8 * 8 + kk
                                kwt = dict(tile_position=(0, 96)) if pr == 3 else {}
                                PE(lambda e, kk=kk, kx=kx, pr=pr, kwt=kwt: e.transpose(
                                    out=psb_[32 * pr:32 * pr + 32, kk * 128:(kk + 1) * 128], in_=Xp[:, kx, 32 * pr:32 * pr + 32],
                                    identity=ident_b[:], **kwt), [Xp, ident_b], [psb_])
                            V(lambda e, q8=q8, pr=pr, wb=wb: e.tensor_copy(
                                out=wb[32 * pr:32 * pr + 32, q8 * 8:(q8 + 1) * 8, :].rearrange("p a b -> p (a b)"),
                                in_=psb_[32 * pr:32 * pr + 32, :]), [psb_], [wb])
                        pe_ = nps()
                        for rho in range(32):
                            kx = (31 - rho) if d == 0 else rho
                            kw = dict(tile_position=(96, 0)) if pr == 3 else {}
                            PE(lambda e, rho=rho, kx=kx, pr=pr, c=c, wb=wb, pe_=pe_, kw=kw: e.matmul(
                                out=pe_[:, 0:NBLK], lhsT=wb[32 * pr:32 * pr + 32, kx, :],
                                rhs=uT[32 * pr:32 * pr + 32, c, :].rearrange("p (b r) -> p b r", r=TB)[:, :, rho],
                                start=(rho == 0), stop=(rho == 31), **kw), [wb, uT], [pe_])
                        A(lambda e, g=g, pe_=pe_: e.copy(out=Et[:, g, :], in_=pe_[:, 0:NBLK]), [pe_], [Et])
                    ARd = AR[:, d * 32:(d + 1) * 32]
                    AId = AIs[:, d * 32:(d + 1) * 32]

                    def rec_step(cu, w, bsel_E, bsel_H):
                        A(lambda e: e.copy(out=bsel_H, in_=cu), [cur, curs], [Hsb[d]])
                        psw = nps()
                        PE(lambda e: e.matmul(out=psw[:, 0:32 * w], lhsT=swapm[:], rhs=cu.rearrange("p a b -> p (a b)"),
                                              start=True, stop=True), [swapm, cur, curs], [psw])
                        V(lambda e: e.tensor_tensor(out=tA[:, :, 0:w], in0=cu, in1=ARd.unsqueeze(2).to_broadcast([128, 32, w]),
                                                    op=ALU.mult), [cur, curs, AR], [tA])
                        V(lambda e: e.tensor_tensor(out=tB[:, :, 0:w], in0=psw[:, 0:32 * w].rearrange("p (a b) -> p a b", b=w),
                                                    in1=AId.unsqueeze(2).to_broadcast([128, 32, w]), op=ALU.mult), [psw, AIs], [tB])
                        V(lambda e: e.tensor_tensor(out=tA[:, :, 0:w], in0=tA[:, :, 0:w], in1=tB[:, :, 0:w], op=ALU.add), [tA, tB], [tA])
                        V(lambda e: e.tensor_tensor(out=cu, in0=tA[:, :, 0:w], in1=bsel_E, op=ALU.add), [tA, Et], [cur, curs])

                    V(lambda e: e.memset(cur[:], 0.0), [], [cur])
                    nbp = LP // TB
                    for s_ in range(nbp):
                        x = s_ if d == 0 else nbp - 1 - s_
                        selE = Et[:, :, 0:NPS * nbp].rearrange("p g (s x) -> p g s x", x=nbp)[:, :, :, x]
                        selH = Hsb[d][:, :, 0:NPS * nbp].rearrange("p g (s x) -> p g s x", x=nbp)[:, :, :, x]
                        rec_step(cur[:], NPS, selE, selH)
                    for si in range(NPS):
                        pf = nps()
                        PE(lambda e, si=si, pf=pf: e.transpose(out=pf[0:32, 0:128], in_=cur[:, :, si], identity=ident_f[:]),
                           [cur, ident_f], [pf])
                        V(lambda e, pf=pf: e.tensor_copy(out=fin[:], in_=pf[0:32, 0:128]), [pf], [fin])
                        k.dma("sp", nre_out.t[si, layer, d, :, :], fin[:, 0:64], reads=[fin], writes=[nre_out])
                        k.dma("sp", nim_out.t[si, layer, d, :, :], fin[:, 64:128], reads=[fin], writes=[nim_out])
                    k.dma("sp", hin[:, 0:64], h0re.t[layer, d, :, :], reads=[h0re], writes=[hin])
                    k.dma("sp", hin[:, 64:128], h0im.t[layer, d, :, :], reads=[h0im], writes=[hin])
                    ph = nps()
                    PE(lambda e: e.transpose(out=ph[:, 0:32], in_=hin[:], identity=ident_f[0:32, 0:32]), [hin, ident_f], [ph])
                    V(lambda e: e.tensor_copy(out=curs[:].rearrange("p a b -> p (a b)"), in_=ph[:, 0:32]), [ph], [curs])
                    b0 = NPS * nbp
                    nbs = LS // TB
                    for s_ in range(nbs):
                        b = b0 + (s_ if d == 0 else nbs - 1 - s_)
                        rec_step(curs[:], 1, Et[:, :, b:b + 1], Hsb[d][:, :, b:b + 1])
                k.interleave([lambda: s5_chain(0), lambda: s5_chain(1)])
                k.barrier()
            k.es = ess

            WCc = k.sb("WCc", [128, 16, 1024], BF16)
            Kc = k.sb("Kc", [128, 2, 32, 128], BF16)
            yfar = k.sb("yfar", [128, NBLK, TB], F32)
            dcol = k.sb("dcol", [128, 1], F32)
            dgm = k.sb("dgm", [128, 128], BF16)
            zt5 = k.sb("zt5", [128, 512], F32)
            ys = k.sb("ys", [128, 512], F32)
            yo = k.sb("yo", [128, 512], BF16)
            for c in range(4):
                for d in range(2):
                    k.dma("sp", WCc[:, d * 8:(d + 1) * 8, :], wctab.t[d * 32 + c * 8:d * 32 + c * 8 + 8, :, :].rearrange("g p x -> p g x"),
                          reads=[wctab], writes=[WCc])
                    k.dma("sp", Kc[:, d, :, :].rearrange("p a b -> p (a b)"), ktab.t[d, c, :, :], reads=[ktab], writes=[Kc])
                k.dma("sp", dcol[:], s5_d.t[layer, c * 128:(c + 1) * 128].rearrange("(p o) -> p o", o=1), reads=[s5_d], writes=[dcol])
                V(lambda e: e.tensor_scalar(out=dgm[:], in0=ident_f[:], scalar1=dcol[:, 0:1], scalar2=None, op0=ALU.mult),
                  [ident_f, dcol], [dgm])
                for rho in range(32):
                    pp = nps()
                    started = [False] * 4
                    for d in range(2):
                        kk = rho if d == 0 else 31 - rho
                        for gi in range(8):
                            j = gi // 2
                            kw = dict(tile_position=(0, 96)) if j == 3 else {}
                            last_ = (d == 1 and gi % 2 == 1)
                            PE(lambda e, d=d, gi=gi, j=j, kk=kk, pp=pp, st=not started[j], last_=last_, kw=kw: e.matmul(
                                out=pp[32 * j:32 * j + 32, 0:NBLK], lhsT=WCc[:, d * 8 + gi, kk * 32:(kk + 1) * 32],
                                rhs=Hsb[d][:, c * 8 + gi, :], start=st, stop=last_, **kw), [WCc, Hsb[d]], [pp])
                            started[j] = True
                    A(lambda e, pp=pp, rho=rho: e.copy(out=yfar[:, :, rho], in_=pp[:, 0:NBLK]), [pp], [yfar])
                for tg in range(NTOK // 512):
                    pp = nps()
                    usl = uT[:, c, tg * 512:(tg + 1) * 512]
                    PE(lambda e, pp=pp, usl=usl: e.matmul(out=pp[:, :], lhsT=dgm[:], rhs=usl, start=True, stop=False), [dgm, uT], [pp])
                    u3 = usl.rearrange("p (b r) -> p b r", r=TB)
                    p3 = pp[:, :].rearrange("p (b r) -> p b r", r=TB)
                    for d in range(2):
                        for tau in range(32):
                            if d == 0:
                                o_ap, r_ap = p3[:, :, tau:32], u3[:, :, 0:32 - tau]
                            else:
                                o_ap, r_ap = p3[:, :, 0:32 - tau], u3[:, :, tau:32]
                            PE(lambda e, d=d, tau=tau, o_ap=o_ap, r_ap=r_ap, pp=pp: e.matmul(
                                out=o_ap, lhsT=Kc[:, d, tau, :], rhs=r_ap, start=False, stop=(d == 1 and tau == 31)), [Kc, uT], [pp])
                    k.dma("sp", zt5[:], projT.t[512 + c * 128:512 + (c + 1) * 128, tg * 512:(tg + 1) * 512], reads=[projT], writes=[zt5])
                    V(lambda e, pp=pp, tg=tg: e.tensor_tensor(
                        out=ys[:], in0=pp[:, :], in1=yfar[:, tg * 16:(tg + 1) * 16, :].rearrange("p a b -> p (a b)"), op=ALU.add),
                      [pp, yfar], [ys])
                    A(lambda e: e.activation(out=ys[:], in_=ys[:], func=ACT.Gelu), [ys], [ys])
                    A(lambda e: e.activation(out=zt5[:], in_=zt5[:], func=ACT.Sigmoid), [zt5], [zt5])
                    V(lambda e: e.tensor_tensor(out=yo[:], in0=ys[:], in1=zt5[:], op=ALU.mult), [ys, zt5], [yo])
                    k.dma("pool", mixT.t[c * 128:(c + 1) * 128, tg * 512:(tg + 1) * 512], yo[:], reads=[yo], writes=[mixT])
            k.barrier()
        k.es = es

    for layer in range(nlayers):
        with contextlib.ExitStack() as esm:
            k.es = esm
            wa = [k.sb("wa%d" % i, [128, 8, 512], F32) for i in range(2)]
            mrow = [k.sb("mrow%d" % i, [2, 512], F32) for i in range(2)]
            brow = [k.sb("brow%d" % i, [2, 512], F32) for i in range(2)]
            for nt in range(12):
                wt = wa[nt % 2]
                k.dma("sp", wt[:], w_ada.t[layer, :, nt * 512:(nt + 1) * 512].rearrange("(c p) n -> p c n", p=128),
                      reads=[w_ada], writes=[wt])
                br = brow[nt % 2]
                k.dma("sp", br[:], b_ada.t[layer:layer + 1, nt * 512:(nt + 1) * 512].to_broadcast([2, 512]),
                      reads=[b_ada], writes=[br])
                pp = nps()
                for c in range(8):
                    k.op("pe", lambda e, c=c: e.matmul(out=pp[0:2, :], lhsT=sc_t[:, c, :], rhs=wt[:, c, :],
                                                       start=(c == 0), stop=(c == 7)), reads=[sc_t, wt], writes=[pp])
                mr = mrow[nt % 2]
                k.op("dve", lambda e: e.tensor_tensor(out=mr[:], in0=pp[0:2, :], in1=br[:], op=ALU.add),
                     reads=[pp, br], writes=[mr])
                k.dma("sp", mod_d.t[:, nt * 512:(nt + 1) * 512], mr[:], reads=[mr], writes=[mod_d])
            k.barrier()
        k.es = es

        with contextlib.ExitStack() as es1:
            k.es = es1
            win = k.sb("win", [128, 8, IN_W], BF16)
            for c in range(8):
                k.dma("sp", win[:, c, :], winbs[layer % 2].t[c * 128:(c + 1) * 128, :], reads=[winbs[layer % 2]], writes=[win])
            A1 = [k.sb("A1_%d" % v, [128, D], F32) for v in range(2)]
            S1 = [k.sb("S1_%d" % v, [128, D], F32) for v in range(2)]
            gtmp = k.sb("gtmp", [128, D], F32)
            for v in range(2):
                load_mod(S1[v], v, 0, layer, None, "shift")
                load_mod(A1[v], v, 1, layer, None, "scale", gsrc=norm_mix, gt=gtmp)
            xt = [k.sb("xt%d" % i, [128, D], F32) for i in range(2)]
            sq = [k.sb("sq%d" % i, [128, D], F32) for i in range(2)]
            ss = [k.sb("ss%d" % i, [128, 4], F32) for i in range(2)]
            hb = [k.sb("hb%d" % i, [128, D], BF16) for i in range(2)]
            hT = [k.sb("hT%d" % i, [128, 8, 512], BF16) for i in range(2)]
            ev = [k.sb("ev%d" % i, [128, 512], F32) for i in range(4)]
            NG1 = NTOK // 512
            prog1 = {"A": 0, "B": 0}

            def p1A():
                for g in range(NG1):
                    while prog1["B"] < g - 1:
                        k._yield(force=True)
                    hTg = hT[g % 2]
                    for j in range(4):
                        t = g * 4 + j
                        v = 0 if t * 128 < NPS * LP else 1
                        xb = xt[t % 2]
                        k.dma("sp", xb[:], xres.t[t * 128:(t + 1) * 128, :], reads=[xres], writes=[xb])
                        rms_tile(xb, hb[t % 2], A1[v], S1[v], sq[t % 2], ss[t % 2])
                        transpose_to(hb[t % 2], hTg, j)
                    prog1["A"] = g + 1

            def p1B():
                nchunks = (IN_W + 127) // 128
                for g in range(NG1):
                    while prog1["A"] < g + 1:
                        k._yield(force=True)
                    hTg = hT[g % 2]
                    for n in range(nchunks):
                        m = min(128, IN_W - n * 128)
                        pp = nps()
                        for c in range(8):
                            k.op("pe", lambda e, c=c, n=n, m=m, pp=pp: e.matmul(
                                out=pp[0:m, :], lhsT=win[:, c, n * 128:n * 128 + m], rhs=hTg[:, c, :],
                                start=(c == 0), stop=(c == 7)), reads=[win, hTg], writes=[pp], inc=(c == 7))
                        evb = ev[n % 4]
                        if n % 2 == 0:
                            k.op("act", lambda e, m=m, pp=pp, evb=evb: e.copy(out=evb[0:m, :], in_=pp[0:m, :]),
                                 reads=[pp], writes=[evb])
                        else:
                            k.op("dve", lambda e, m=m, pp=pp, evb=evb: e.tensor_copy(out=evb[0:m, :], in_=pp[0:m, :]),
                                 reads=[pp], writes=[evb])
                        k.dma("act", projT.t[n * 128:n * 128 + m, g * 512:(g + 1) * 512], evb[0:m, :],
                              reads=[evb], writes=[projT])
                    prog1["B"] = g + 1

            k.interleave([p1A, p1B], weights=[1, 4])
            k.barrier()
        k.es = es

        if layer + 1 < nlayers:
            cast_mlp_weights(layer + 1)

        if TEST_DENSE:
            with contextlib.ExitStack() as esx:
                k.es = esx
                tb = [k.sb("tb%d" % i, [128, 1024], BF16) for i in range(2)]
                i = 0
                for r in range(8):
                    for cc in range(NTOK // 1024):
                        b = tb[i % 2]
                        i += 1
                        k.dma("pool", b[:], projT.t[r * 128:(r + 1) * 128, cc * 1024:(cc + 1) * 1024],
                              reads=[projT], writes=[b])
                        k.dma("sp", mixT.t[r * 128:(r + 1) * 128, cc * 1024:(cc + 1) * 1024], b[:],
                              reads=[b], writes=[mixT])
                k.barrier()
            k.es = es
        else:
            TEST_GDN = flags.get("test_gdn", False)
            TEST_S5 = flags.get("test_s5", False)
            if TEST_GDN or TEST_S5:
                with contextlib.ExitStack() as esx:
                    k.es = esx
                    tb = [k.sb("tb%d" % i, [128, 1024], BF16) for i in range(2)]
                    i = 0
                    for r in (range(4) if TEST_GDN else range(4, 8)):
                        for cc in range(NTOK // 1024):
                            b = tb[i % 2]
                            i += 1
                            k.dma("pool", b[:], projT.t[r * 128:(r + 1) * 128, cc * 1024:(cc + 1) * 1024],
                                  reads=[projT], writes=[b])
                            k.dma("sp", mixT.t[r * 128:(r + 1) * 128, cc * 1024:(cc + 1) * 1024], b[:],
                                  reads=[b], writes=[mixT])
                    k.barrier()
                k.es = es
            if not TEST_S5:
                gdn_phase(layer)
            if not TEST_GDN:
                s5_phase(layer)

        with contextlib.ExitStack() as es3:
            k.es = es3
            wo = k.sb("wo", [128, 8, D], BF16)
            for c in range(8):
                k.dma("sp", wo[:, c, :], woutbs[layer % 2].t[c * 128:(c + 1) * 128, :], reads=[woutbs[layer % 2]], writes=[wo])
            G1 = [k.sb("G1_%d" % v, [128, D], F32) for v in range(2)]
            A2 = [k.sb("A2_%d" % v, [128, D], F32) for v in range(2)]
            S2 = [k.sb("S2_%d" % v, [128, D], F32) for v in range(2)]
            G2 = [k.sb("G2_%d" % v, [128, D], F32) for v in range(2)]
            gtmp = k.sb("gtmp3", [128, D], F32)
            for v in range(2):
                load_mod(G1[v], v, 2, layer, None, "gate")
                load_mod(S2[v], v, 3, layer, None, "shift")
                load_mod(A2[v], v, 4, layer, None, "scale", gsrc=norm_mlp, gt=gtmp)
                load_mod(G2[v], v, 5, layer, None, "gate")
            mx = [k.sb("mx%d" % i, [128, 8, 512], BF16) for i in range(1)]
            xt = [k.sb("x3_%d" % i, [128, D], F32) for i in range(8)]
            sqA = [k.sb("sq3A_%d" % i, [128, D], F32) for i in range(2)]
            ssA = [k.sb("ss3A_%d" % i, [128, 4], F32) for i in range(2)]
            sqB = [k.sb("sq3B_%d" % i, [128, D], F32) for i in range(2)]
            ssB = [k.sb("ss3B_%d" % i, [128, 4], F32) for i in range(2)]
            hb = [k.sb("hb3_%d" % i, [128, D], BF16) for i in range(2)]
            hT2 = [k.sb("hT3_%d" % i, [128, 8, 512], BF16) for i in range(2)]
            ffT = k.sb("ffT", [128, 32, 512], BF16)
            wq = [k.sb("wq%d" % i, [128, 8, 1024], BF16) for i in range(2)]
            wqi = [0]
            rl = k.sb("rl", [128, 512], F32)
            last = (layer == nlayers - 1)
            if last:
                NF = k.sb("NF", [128, D], F32)
                k.dma("sp", NF[:], norm_final.t.rearrange("(o d) -> o d", o=1).to_broadcast([128, D]),
                      reads=[norm_final], writes=[NF])
            NG = NTOK // 512
            prog = {"A": 0, "B": 0}

            def stageA():
                bk = [psf[4], psf[5]]
                bi = [0]
                for g in range(NG):
                    while prog["B"] < g - 1:
                        k._yield(force=True)
                    mxg = mx[0]
                    hT = hT2[g % 2]
                    k.dma("sp", mxg[:], mixT.t[:, g * 512:(g + 1) * 512].rearrange("(c p) t -> p c t", p=128),
                          reads=[mixT], writes=[mxg])
                    for j in range(4):
                        t = g * 4 + j
                        v = 0 if t * 128 < NPS * LP else 1
                        xb = xt[(g % 2) * 4 + j]
                        sq_, ss_ = sqA[t % 2], ssA[t % 2]
                        k.dma("sp", xb[:], xres.t[t * 128:(t + 1) * 128, :], reads=[xres], writes=[xb])
                        for half in range(2):
                            pp = bk[bi[0] % 2]
                            bi[0] += 1
                            for c in range(8):
                                k.op("pe", lambda e, c=c, pp=pp, half=half, j=j: e.matmul(
                                    out=pp[:, :], lhsT=mxg[:, c, j * 128:(j + 1) * 128],
                                    rhs=wo[:, c, half * 512:(half + 1) * 512], start=(c == 0), stop=(c == 7)),
                                    reads=[mxg, wo], writes=[pp], inc=(c == 7))
                            sl = slice(half * 512, (half + 1) * 512)
                            k.op("dve", lambda e, pp=pp, sl=sl, v=v: e.tensor_tensor(out=sq_[:, sl], in0=pp[:, :],
                                                                                   in1=G1[v][:, sl], op=ALU.mult),
                                 reads=[pp, G1[v]], writes=[sq_])
                        k.op("dve", lambda e: e.tensor_tensor(out=xb[:], in0=xb[:], in1=sq_[:], op=ALU.add),
                             reads=[xb, sq_], writes=[xb])
                        rms_tile(xb, hb[t % 2], A2[v], S2[v], sq_, ss_)
                        transpose_to(hb[t % 2], hT, j)
                    prog["A"] = g + 1

            def stageB():
                bk = psf[0:4] + [psf6]
                bi = [0]

                def nb():
                    p = bk[bi[0] % 5]
                    bi[0] += 1
                    return p

                for g in range(NG):
                    while prog["A"] < g + 1:
                        k._yield(force=True)
                    hT = hT2[g % 2]
                    for q4 in range(4):
                        wb = wq[wqi[0] % 2]
                        wqi[0] += 1
                        k.dma("sp", wb[:], w1bs[layer % 2].t[:, q4 * 1024:(q4 + 1) * 1024].rearrange("(c p) n -> p c n", p=128),
                              reads=[w1bs[layer % 2]], writes=[wb])
                        for fi in range(8):
                            f = q4 * 8 + fi
                            pp = nb()
                            for c in range(8):
                                k.op("pe", lambda e, c=c, pp=pp, fi=fi, wb=wb: e.matmul(
                                    out=pp[:, :], lhsT=wb[:, c, fi * 128:(fi + 1) * 128], rhs=hT[:, c, :],
                                    start=(c == 0), stop=(c == 7)), reads=[wb, hT], writes=[pp], inc=(c == 7))
                            k.op("act", lambda e, pp=pp: e.activation(out=rl[:], in_=pp[:, :], func=ACT.Relu),
                                 reads=[pp], writes=[rl])
                            k.op("dve", lambda e, f=f: e.tensor_tensor(out=ffT[:, f, :], in0=rl[:], in1=rl[:], op=ALU.mult),
                                 reads=[rl], writes=[ffT])
                    for jp in range(2):
                        pps = [[nb(), nb()], [nb(), nb()]]
                        for q4 in range(4):
                            wb = wq[wqi[0] % 2]
                            wqi[0] += 1
                            k.dma("sp", wb[:], w2bs[layer % 2].t[q4 * 1024:(q4 + 1) * 1024, :].rearrange("(c p) n -> p c n", p=128),
                                  reads=[w2bs[layer % 2]], writes=[wb])
                            for jj in range(2):
                                j = jp * 2 + jj
                                for half in range(2):
                                    pp = pps[jj][half]
                                    for fi in range(8):
                                        f = q4 * 8 + fi
                                        k.op("pe", lambda e, pp=pp, f=f, fi=fi, j=j, half=half, wb=wb: e.matmul(
                                            out=pp[:, :], lhsT=ffT[:, f, j * 128:(j + 1) * 128],
                                            rhs=wb[:, fi, half * 512:(half + 1) * 512],
                                            start=(f == 0), stop=(f == 31)), reads=[ffT, wb], writes=[pp], inc=(fi == 7))
                        for jj in range(2):
                            j = jp * 2 + jj
                            t = g * 4 + j
                            v = 0 if t * 128 < NPS * LP else 1
                            xb = xt[(g % 2) * 4 + j]
                            sq_, ss_ = sqB[t % 2], ssB[t % 2]
                            for half in range(2):
                                sl = slice(half * 512, (half + 1) * 512)
                                pp = pps[jj][half]
                                k.op("dve", lambda e, pp=pp, sl=sl, v=v: e.tensor_tensor(out=sq_[:, sl], in0=pp[:, :],
                                                                                       in1=G2[v][:, sl], op=ALU.mult),
                                     reads=[pp, G2[v]], writes=[sq_])
                            k.op("dve", lambda e, xb=xb: e.tensor_tensor(out=xb[:], in0=xb[:], in1=sq_[:], op=ALU.add),
                                 reads=[xb, sq_], writes=[xb])
                            if not last:
                                k.dma("sp", xres.t[t * 128:(t + 1) * 128, :], xb[:], reads=[xb], writes=[xres])
                            else:
                                rms_tile(xb, None, None, None, sq_, ss_)
                                k.op("dve", lambda e, xb=xb: e.scalar_tensor_tensor(
                                    out=sq_[:], in0=xb[:], scalar=ss_[:, 2:3], in1=NF[:],
                                    op0=ALU.mult, op1=ALU.mult), reads=[xb, ss_, NF], writes=[sq_])
                                k.dma("sp", y_out.t[t * 128:(t + 1) * 128, :], sq_[:], reads=[sq_], writes=[y_out])
                    prog["B"] = g + 1

            k.interleave([stageA, stageB], weights=[1, 4])
            k.barrier()
        k.es = es

    k.finish([y_out, nre_out, nim_out, ngdn_out])
    es.close()
    return nc, k


_WNAMES = ["norm_mix", "norm_mlp", "w_ada", "b_ada", "w_in", "conv_qkv", "s5_lambda_re", "s5_lambda_im",
           "s5_log_dt", "s5_b_re", "s5_b_im", "s5_c_re", "s5_c_im", "s5_d", "gdn_a_log", "gdn_dt_bias",
           "gdn_norm", "w_out", "w_mlp_in", "w_mlp_out", "norm_final"]


def _consts():
    j = np.arange(128)[:, None]
    i = np.arange(128)[None, :]
    cm = np.stack([np.where(j <= i, 0.0, -30000.0), np.where(j >= i, 0.0, -30000.0)]).astype(np.float32)
    sr = np.zeros((8, 8, 128), np.float32)
    for r in range(8):
        sr[r, r, :] = 1.0
    s8 = np.zeros((8, 2), np.float32)
    s8[0:4, 0] = 1.0
    s8[4:8, 1] = 1.0
    a = np.arange(128)[:, None]
    b = np.arange(128)[None, :]
    lv = np.stack([(a // 16 == b // 16), (a // 32 == b // 32) & (a // 16 != b // 16),
                   (a // 64 == b // 64) & (a // 32 != b // 32), (a // 64 != b // 64)]).astype(np.float32)
    return cm, sr.reshape(8, 1024), s8, lv


CMASK, SELROWS, SEL8, LVLMASK = _consts()
SWAPM = np.roll(np.eye(128, dtype=np.float32), 64, axis=1)


def make_in_maps(inp):
    f = lambda a: np.ascontiguousarray(np.asarray(a, dtype=np.float32))
    xp, xs = f(inp["x_prompt"]), f(inp["x_sample"])
    ident = np.eye(128, dtype=np.float32)
    maps = []
    for c in range(8):
        b = c % 2
        m = {
            "xin": np.concatenate([xp[NPS * c:NPS * (c + 1)].reshape(NPS * LP, D), xs[b]], axis=0),
            "cvec": np.stack([f(inp["c_ctx"]), f(inp["c"])[b]], axis=0),
            "h0re": f(inp["state_s5_re"])[b], "h0im": f(inp["state_s5_im"])[b], "s0": f(inp["state_gdn"])[b],
            "ident": ident, "cmask": CMASK, "selrows": SELROWS, "sel8": SEL8, "lvlmask": LVLMASK, "swapm": SWAPM,
        }
        for n in _WNAMES:
            m[n] = f(inp[n])
        maps.append(m)
    return maps


def kernel(**inputs):
    nc, _ = build()
    maps = make_in_maps(inputs)
    res = run_bass_kernel_spmd(nc, maps, core_ids=list(range(8)))
    r = res.results
    y_prompt = np.concatenate([r[c]["y"][:NPS * LP].reshape(NPS, LP, D) for c in range(8)], axis=0)
    y_sample = np.stack([r[b]["y"][NPS * LP:] for b in range(2)], axis=0)
    nre = np.concatenate([r[c]["nre"] for c in range(8)], axis=0)
    nim = np.concatenate([r[c]["nim"] for c in range(8)], axis=0)
    ngdn = np.concatenate([r[c]["ngdn"] for c in range(8)], axis=0)
    return (y_prompt.astype(np.float32), y_sample.astype(np.float32), nre.astype(np.float32),
            nim.astype(np.float32), ngdn.astype(np.float32))
```
